# Optimizing a Trainium2 kernel written in Bass

```python
import math
import jax, jax.numpy as jnp
from jax import lax
import numpy as np

D_MODEL = 2048
BATCH = 4
SEQ = 2048
DEPTH = 4
DEC_BATCH = 128
DEC_SEQ = 4
PAST_LEN = 16384
PAGE_SIZE = 128

N_META = 16
CONV_K = 4
CHUNK = 64
MIX_W = D_MODEL
GROUP_W = MIX_W // 4
N_HEADS_GRP = 4
HEAD_V = GROUP_W // N_HEADS_GRP
DN_DK = HEAD_V
GLA_DK = HEAD_V // 2
RET_DK = HEAD_V // 2
GLA_RANK = 16
GLA_TAU = 16
LRU_BLOCKS = N_HEADS_GRP
LRU_C = 8
ROPE_BASE = 10000.0
EPS = 1e-6
DN_QKV_W = 3 * N_HEADS_GRP * DN_DK
IN_SIZES = (DN_QKV_W, N_HEADS_GRP, N_HEADS_GRP, GROUP_W,
            N_HEADS_GRP * GLA_DK, N_HEADS_GRP * GLA_DK, GROUP_W, GLA_RANK,
            N_HEADS_GRP * RET_DK, N_HEADS_GRP * RET_DK, GROUP_W, MIX_W)
IN_W = sum(IN_SIZES)

kernel_name = 'hymba_delta_lru_gla_retention_step'


def rms_norm(x, w=None):
    xf = x.astype(jnp.float32)
    y = xf * lax.rsqrt(jnp.mean(xf * xf, axis=-1, keepdims=True) + EPS)
    if w is not None:
        y = y * w.astype(jnp.float32)
    return y.astype(x.dtype)


def l2_normalize(x):
    return x * lax.rsqrt(jnp.sum(x * x, axis=-1, keepdims=True) + EPS)


def split_last(x, sizes):
    outs, off = [], 0
    for s in sizes:
        outs.append(x[..., off:off + s])
        off += s
    return outs


def causal_conv(x, buf, w):
    xp = jnp.concatenate([buf, x], axis=1)
    y = lax.conv_general_dilated(xp, w[:, None, :], window_strides=(1,), padding='VALID',
                                 dimension_numbers=('NWC', 'WIO', 'NWC'),
                                 feature_group_count=x.shape[-1])
    return y, xp[:, -(CONV_K - 1):]


def rotary(x, pos):
    half = x.shape[-1] // 2
    freqs = ROPE_BASE ** (-jnp.arange(half, dtype=jnp.float32) / half)
    ang = pos.astype(jnp.float32)[:, None] * freqs
    cos, sin = jnp.cos(ang)[None, :, None, :], jnp.sin(ang)[None, :, None, :]
    x1, x2 = x[..., :half], x[..., half:]
    return jnp.concatenate([x1 * cos - x2 * sin, x2 * cos + x1 * sin], axis=-1)


def to_chunks(x, c):
    b, l = x.shape[:2]
    return jnp.moveaxis(x.reshape(b, l // c, c, *x.shape[2:]), 1, 0)


def from_chunks(y):
    y = jnp.moveaxis(y, 0, 1)
    return y.reshape(y.shape[0], -1, *y.shape[3:])


def run_pieces(scan_fn, arrays, s0, pieces):
    outs, s = [], s0
    for start, stop, c in pieces:
        o, s = scan_fn(*[a[:, start:stop] for a in arrays], s, c)
        outs.append(o)
    return jnp.concatenate(outs, axis=1), s


def delta_chunk_scan(q, k, v, beta, g, s0, c):
    dv = v.shape[-1]
    incl = jnp.tril(jnp.ones((c, c), dtype=bool))
    strict = jnp.tril(jnp.ones((c, c), dtype=bool), -1)
    eye = jnp.eye(c, dtype=jnp.float32)

    def step(s, inp):
        qc, kc, vc, bc, gc = inp
        gcum = jnp.moveaxis(jnp.cumsum(gc, axis=1), 1, 2)
        decay = jnp.exp(jnp.where(incl, gcum[..., :, None] - gcum[..., None, :], -jnp.inf))
        kb = kc * bc[..., None]
        a = jnp.where(strict, jnp.einsum('bihk,bjhk->bhij', kb, kc) * decay, 0.0)
        rhs = jnp.concatenate([jnp.moveaxis(vc * bc[..., None], 1, 2),
                               jnp.moveaxis(kb, 1, 2) * jnp.exp(gcum)[..., None]], axis=-1)
        sol = lax.linalg.triangular_solve(a + eye, rhs, left_side=True, lower=True)
        u, w = sol[..., :dv], sol[..., dv:]
        v_new = u - jnp.einsum('bhik,bhkv->bhiv', w, s)
        attn = jnp.einsum('bihk,bjhk->bhij', qc, kc) * decay
        o = (jnp.einsum('bihk,bhi,bhkv->bihv', qc, jnp.exp(gcum), s)
             + jnp.einsum('bhij,bhjv->bihv', attn, v_new))
        g_last = gcum[..., -1]
        s_new = (s * jnp.exp(g_last)[..., None, None]
                 + jnp.einsum('bjhk,bhj,bhjv->bhkv', kc, jnp.exp(g_last[..., None] - gcum), v_new))
        return s_new, o

    s, o = lax.scan(step, s0, tuple(to_chunks(t, c) for t in (q, k, v, beta, g)))
    return from_chunks(o), s


def gla_chunk_scan(q, k, v, lg, s0, c):
    incl = jnp.tril(jnp.ones((c, c), dtype=bool))

    def step(s, inp):
        qc, kc, vc, lc = inp
        b = jnp.cumsum(lc, axis=1)
        rel = jnp.exp(jnp.where(incl[None, :, :, None, None], b[:, :, None] - b[:, None, :], -jnp.inf))
        attn = jnp.einsum('bihk,bjhk,bijhk->bhij', qc, kc, rel)
        o = (jnp.einsum('bihk,bhkv->bihv', qc * jnp.exp(b), s)
             + jnp.einsum('bhij,bjhv->bihv', attn, vc))
        b_last = b[:, -1]
        s_new = (s * jnp.exp(b_last)[..., None]
                 + jnp.einsum('bjhk,bjhv->bhkv', kc * jnp.exp(b_last[:, None] - b), vc))
        return s_new, o

    s, o = lax.scan(step, s0, tuple(to_chunks(t, c) for t in (q, k, v, lg)))
    return from_chunks(o), s


def retention_chunk_scan(q, k, v, s0, c, log_gamma):
    pos = jnp.arange(c, dtype=jnp.float32)
    rel = pos[:, None] - pos[None, :]
    intra = jnp.where(rel >= 0, jnp.exp(log_gamma[:, None, None] * jnp.maximum(rel, 0.0)), 0.0)
    from_state = jnp.exp(log_gamma[:, None] * (pos + 1.0))
    to_state = jnp.exp(log_gamma[:, None] * (c - 1.0 - pos))
    chunk_decay = jnp.exp(log_gamma * c)

    def step(s, inp):
        qc, kc, vc = inp
        attn = jnp.einsum('bihk,bjhk->bhij', qc, kc) * intra
        o = (jnp.einsum('bihk,hi,bhkv->bihv', qc, from_state, s)
             + jnp.einsum('bhij,bjhv->bihv', attn, vc))
        s_new = s * chunk_decay[:, None, None] + jnp.einsum('bjhk,hj,bjhv->bhkv', kc, to_state, vc)
        return s_new, o

    s, o = lax.scan(step, s0, tuple(to_chunks(t, c) for t in (q, k, v)))
    return from_chunks(o), s


def rg_lru(x, h0, wa, ba, wx, bx, lam):
    bsz, l, w = x.shape
    xb = x.reshape(bsz, l, LRU_BLOCKS, w // LRU_BLOCKS)
    r = jax.nn.sigmoid(jnp.einsum('blnc,ncd->blnd', xb, wa).reshape(bsz, l, w) + ba)
    i = jax.nn.sigmoid(jnp.einsum('blnc,ncd->blnd', xb, wx).reshape(bsz, l, w) + bx)
    log_a = -LRU_C * r * jax.nn.softplus(-lam)
    a = jnp.exp(log_a)
    b = jnp.sqrt(-jnp.expm1(2.0 * log_a)) * (i * x)
    b = b.at[:, 0].add(a[:, 0] * h0)

    def combine(left, right):
        return (left[0] * right[0], right[0] * left[1] + right[1])

    _, h = lax.associative_scan(combine, (a, b), axis=1)
    return h, h[:, -1]


def hybrid_layer(x, pos, pieces, s_delta, buf_delta, h_lru, buf_lru, s_gla, s_ret, lp):
    f32 = jnp.float32
    bsz, l, _ = x.shape
    nh = N_HEADS_GRP
    proj = jnp.einsum('bld,de->ble', rms_norm(x, lp['norm']), lp['w_in']).astype(f32)
    (qkv_a, alpha_a, beta_a, x_b, q_c, k_c, v_c, r_c, q_d, k_d, v_d, gate) = split_last(proj, IN_SIZES)

    qkv_a, new_buf_delta = causal_conv(qkv_a, buf_delta.astype(f32), lp['conv_a'].astype(f32))
    qkv_a = jax.nn.silu(qkv_a).reshape(bsz, l, 3, nh, DN_DK)
    q_a = l2_normalize(qkv_a[:, :, 0]) * (DN_DK ** -0.5)
    k_a = l2_normalize(qkv_a[:, :, 1])
    v_a = qkv_a[:, :, 2]
    beta = jax.nn.sigmoid(beta_a)
    g = -jnp.exp(lp['a_log'].astype(f32)) * jax.nn.softplus(alpha_a + lp['dt_bias'].astype(f32))
    o_a, new_s_delta = run_pieces(delta_chunk_scan, (q_a, k_a, v_a, beta, g), s_delta.astype(f32), pieces)
    o_a = rms_norm(o_a, lp['norm_a']).reshape(bsz, l, GROUP_W)

    x_b, new_buf_lru = causal_conv(x_b, buf_lru.astype(f32), lp['conv_b'].astype(f32))
    x_b = x_b + lp['conv_b_bias'].astype(f32)
    o_b, new_h_lru = rg_lru(x_b, h_lru.astype(f32), lp['lru_wa'].astype(f32), lp['lru_ba'].astype(f32),
                           lp['lru_wx'].astype(f32), lp['lru_bx'].astype(f32), lp['lru_lambda'].astype(f32))

    q_c = q_c.reshape(bsz, l, nh, GLA_DK) * (GLA_DK ** -0.5)
    k_c = k_c.reshape(bsz, l, nh, GLA_DK)
    v_c = v_c.reshape(bsz, l, nh, HEAD_V)
    lg_c = jax.nn.log_sigmoid(r_c @ lp['gla_w2'].astype(f32) + lp['gla_b2'].astype(f32))
    lg_c = lg_c.reshape(bsz, l, nh, GLA_DK) / GLA_TAU
    o_c, new_s_gla = run_pieces(gla_chunk_scan, (q_c, k_c, v_c, lg_c), s_gla.astype(f32), pieces)
    o_c = rms_norm(o_c, lp['norm_c']).reshape(bsz, l, GROUP_W)

    q_d = rotary(q_d.reshape(bsz, l, nh, RET_DK), pos)
    k_d = rotary(k_d.reshape(bsz, l, nh, RET_DK), pos) * (RET_DK ** -0.5)
    v_d = v_d.reshape(bsz, l, nh, HEAD_V)
    log_gamma = jnp.log(1.0 - 2.0 ** (-5.0 - jnp.arange(nh, dtype=f32)))
    ret_fn = lambda q, k, v, s, c: retention_chunk_scan(q, k, v, s, c, log_gamma)
    o_d, new_s_ret = run_pieces(ret_fn, (q_d, k_d, v_d), s_ret.astype(f32), pieces)
    o_d = rms_norm(o_d).reshape(bsz, l, GROUP_W)

    mixed = jnp.concatenate([o_a, o_b, o_c, o_d], axis=-1) * jax.nn.silu(gate)
    y = x + jnp.einsum('ble,ed->bld', mixed.astype(x.dtype), lp['w_out'])
    return y, (new_s_delta, new_buf_delta, new_h_lru, new_buf_lru, new_s_gla, new_s_ret)


def setup_inputs(seed: int = 0) -> dict:
    key = jax.random.key(seed)
    ks = jax.random.split(key, 32)
    f32 = jnp.float32
    nh = N_HEADS_GRP
    bw = GROUP_W // LRU_BLOCKS

    def nrm(k, shape, s):
        return s * jax.random.normal(k, shape, f32)

    a_log = jnp.log(jax.random.uniform(ks[10], (DEPTH, nh), f32, 1.0, 16.0))
    dt = jnp.exp(jax.random.uniform(ks[11], (DEPTH, nh), f32, math.log(1e-3), math.log(0.1)))
    dt_bias = dt + jnp.log(-jnp.expm1(-dt))
    a0 = jax.random.uniform(ks[12], (DEPTH, GROUP_W), f32, 0.9, 0.999) ** (1.0 / LRU_C)
    lru_lambda = jnp.log(a0) - jnp.log1p(-a0)
    return {
        'x_prompt': nrm(ks[0], (BATCH, SEQ, D_MODEL), 1.0),
        'x_sample': nrm(ks[1], (DEC_BATCH, DEC_SEQ, D_MODEL), 1.0),
        'state_delta': nrm(ks[2], (DEPTH, DEC_BATCH, nh, DN_DK, HEAD_V), 0.1),
        'state_delta_conv': nrm(ks[3], (DEPTH, DEC_BATCH, CONV_K - 1, DN_QKV_W), 1.0),
        'state_lru': nrm(ks[4], (DEPTH, DEC_BATCH, GROUP_W), 0.5),
        'state_lru_conv': nrm(ks[5], (DEPTH, DEC_BATCH, CONV_K - 1, GROUP_W), 1.0),
        'state_gla': nrm(ks[6], (DEPTH, DEC_BATCH, nh, GLA_DK, HEAD_V), 0.5),
        'state_ret': nrm(ks[7], (DEPTH, DEC_BATCH, nh, RET_DK, HEAD_V), 0.5),
        'meta_tokens': nrm(ks[8], (N_META, D_MODEL), 1.0),
        'norm_w': 1.0 + nrm(ks[9], (DEPTH, D_MODEL), 0.01),
        'w_in': nrm(ks[13], (DEPTH, D_MODEL, IN_W), D_MODEL ** -0.5),
        'conv_a': nrm(ks[14], (DEPTH, CONV_K, DN_QKV_W), CONV_K ** -0.5),
        'a_log': a_log,
        'dt_bias': dt_bias,
        'norm_a': 1.0 + nrm(ks[15], (DEPTH, HEAD_V), 0.01),
        'conv_b': nrm(ks[16], (DEPTH, CONV_K, GROUP_W), CONV_K ** -0.5),
        'conv_b_bias': nrm(ks[17], (DEPTH, GROUP_W), 0.01),
        'lru_wa': nrm(ks[18], (DEPTH, LRU_BLOCKS, bw, bw), bw ** -0.5),
        'lru_ba': nrm(ks[19], (DEPTH, GROUP_W), 0.01),
        'lru_wx': nrm(ks[20], (DEPTH, LRU_BLOCKS, bw, bw), bw ** -0.5),
        'lru_bx': nrm(ks[21], (DEPTH, GROUP_W), 0.01),
        'lru_lambda': lru_lambda,
        'gla_w2': nrm(ks[22], (DEPTH, GLA_RANK, nh * GLA_DK), GLA_RANK ** -0.5),
        'gla_b2': nrm(ks[23], (DEPTH, nh * GLA_DK), 0.01),
        'norm_c': 1.0 + nrm(ks[24], (DEPTH, HEAD_V), 0.01),
        'w_out': nrm(ks[25], (DEPTH, MIX_W, D_MODEL), 0.5 * MIX_W ** -0.5),
        'final_norm': 1.0 + nrm(ks[26], (D_MODEL,), 0.01),
    }


def reference(x_prompt, x_sample, state_delta, state_delta_conv, state_lru, state_lru_conv, state_gla,
              state_ret, meta_tokens, norm_w, w_in, conv_a, a_log, dt_bias, norm_a, conv_b, conv_b_bias,
              lru_wa, lru_ba, lru_wx, lru_bx, lru_lambda, gla_w2, gla_b2, norm_c, w_out, final_norm):
    f32 = jnp.float32
    nh = N_HEADS_GRP
    bp, lp_len = x_prompt.shape[0], x_prompt.shape[1]
    ls = x_sample.shape[1]
    meta = jnp.broadcast_to(meta_tokens[None].astype(x_prompt.dtype), (bp, N_META, D_MODEL))
    hp = jnp.concatenate([meta, x_prompt], axis=1)
    hs = x_sample
    pos_p = jnp.arange(N_META + lp_len)
    pos_s = PAST_LEN + jnp.arange(ls)
    pieces_p = ((0, N_META, N_META), (N_META, N_META + lp_len, math.gcd(lp_len, CHUNK)))
    pieces_s = ((0, ls, math.gcd(ls, CHUNK)),)

    p_states, s_states = [], []
    for layer in range(DEPTH):
        lp = {'norm': norm_w[layer], 'w_in': w_in[layer], 'conv_a': conv_a[layer], 'a_log': a_log[layer],
              'dt_bias': dt_bias[layer], 'norm_a': norm_a[layer], 'conv_b': conv_b[layer],
              'conv_b_bias': conv_b_bias[layer], 'lru_wa': lru_wa[layer], 'lru_ba': lru_ba[layer],
              'lru_wx': lru_wx[layer], 'lru_bx': lru_bx[layer], 'lru_lambda': lru_lambda[layer],
              'gla_w2': gla_w2[layer], 'gla_b2': gla_b2[layer], 'norm_c': norm_c[layer], 'w_out': w_out[layer]}
        hp, st_p = hybrid_layer(
            hp, pos_p, pieces_p,
            jnp.zeros((bp, nh, DN_DK, HEAD_V), f32), jnp.zeros((bp, CONV_K - 1, DN_QKV_W), f32),
            jnp.zeros((bp, GROUP_W), f32), jnp.zeros((bp, CONV_K - 1, GROUP_W), f32),
            jnp.zeros((bp, nh, GLA_DK, HEAD_V), f32), jnp.zeros((bp, nh, RET_DK, HEAD_V), f32), lp)
        hs, st_s = hybrid_layer(
            hs, pos_s, pieces_s, state_delta[layer], state_delta_conv[layer], state_lru[layer],
            state_lru_conv[layer], state_gla[layer], state_ret[layer], lp)
        p_states.append(st_p)
        s_states.append(st_s)

    p_delta, p_delta_conv, p_lru, p_lru_conv, p_gla, p_ret = [
        jnp.stack([st[i] for st in p_states]) for i in range(6)]
    s_delta, s_delta_conv, s_lru, s_lru_conv, s_gla, s_ret = [
        jnp.stack([st[i] for st in s_states]) for i in range(6)]
    y_prompt = rms_norm(hp, final_norm)[:, N_META:]
    y_sample = rms_norm(hs, final_norm)
    return (y_prompt, y_sample, p_delta, p_delta_conv, p_lru, p_lru_conv, p_gla, p_ret,
            s_delta, s_delta_conv, s_lru, s_lru_conv, s_gla, s_ret)
```

```python
import math
import numpy as np
import ml_dtypes
import concourse.bass as bass
import concourse.mybir as mybir
from concourse.bass_utils import run_bass_kernel_spmd

F32 = mybir.dt.float32
BF16 = mybir.dt.bfloat16
ALU = mybir.AluOpType
AF = mybir.ActivationFunctionType
ESZ = {F32: 4, BF16: 2}

D = 2048
KT = 16
INW = 6168
EPS = 1e-6
NEG = -30000.0
import os as _os
LOOKD = int(_os.environ.get("LOOKD", "2"))
LOOKC = int(_os.environ.get("LOOKC", "1"))
C_QKV, C_AL, C_XB, C_QC, C_KC, C_VC, C_RC, C_QD, C_KD, C_VD, C_GATE = 0, 1536, 1544, 2056, 2312, 2568, 3080, 3096, 3352, 3608, 4120


def _rng(ap):
    t = ap.tensor
    dims = list(ap.ap)
    off = int(ap.offset)
    es = ESZ.get(ap.dtype, 4)
    if type(t).__name__.startswith("DRam"):
        ext = 0
        for s, n in dims:
            ext += (int(n) - 1) * abs(int(s))
        return (t.name, off * es, (off + ext + 1) * es)
    if type(t).__name__.startswith("PSum"):
        return (t.name, 0, 1 << 30)
    pstep = int(dims[0][0])
    if pstep <= 0:
        pstep = 1 << 40
    lo = off % pstep
    ext = 0
    for s, n in dims[1:]:
        ext += (int(n) - 1) * abs(int(s))
    return (t.name, lo * es, (lo + ext + 1) * es)


class Sch:
    NDMA = 24

    def __init__(self, nc):
        self.nc = nc
        self.eng = {"pe": nc.tensor, "dve": nc.vector, "act": nc.scalar, "pool": nc.gpsimd, "sp": nc.sync}
        self.sem = {k: nc.alloc_semaphore("sem_" + k) for k in self.eng}
        self.cnt = {k: 0 for k in self.eng}
        self.seen = {k: {} for k in self.eng}
        self.dsem = [nc.alloc_semaphore("dsem%d" % i) for i in range(self.NDMA)]
        self.dcnt = [0] * self.NDMA
        self.drr = 0
        self.tr = {}
        self.nins = 0
        self.out_dmas = {}

    def _deps(self, reads, writes, e=None):
        deps = {}
        for ap in reads:
            name, lo, hi = _rng(ap)
            t = self.tr.get(name)
            if t is None:
                continue
            for (l, h, src, val) in t["w"]:
                if l < hi and lo < h and deps.get(src, 0) < val:
                    deps[src] = val
            if type(ap.tensor).__name__.startswith("PSum"):
                for (l, h, src, val) in t["r"]:
                    if src != e and deps.get(src, 0) < val:
                        deps[src] = val
        for ap in writes:
            name, lo, hi = _rng(ap)
            t = self.tr.get(name)
            if t is None:
                continue
            for (l, h, src, val) in t["w"]:
                if l < hi and lo < h and deps.get(src, 0) < val:
                    deps[src] = val
            for (l, h, src, val) in t["r"]:
                if l < hi and lo < h and deps.get(src, 0) < val:
                    deps[src] = val
        return deps

    def _record(self, reads, writes, src, val):
        for ap in writes:
            name, lo, hi = _rng(ap)
            t = self.tr.setdefault(name, {"w": [], "r": []})
            t["w"] = [e for e in t["w"] if not (lo <= e[0] and e[1] <= hi)]
            t["r"] = [e for e in t["r"] if not (lo <= e[0] and e[1] <= hi)]
            t["w"].append((lo, hi, src, val))
        for ap in reads:
            name, lo, hi = _rng(ap)
            t = self.tr.setdefault(name, {"w": [], "r": []})
            t["r"] = [e for e in t["r"] if not (e[2] == src and lo <= e[0] and e[1] <= hi)]
            t["r"].append((lo, hi, src, val))

    def _semof(self, src):
        return self.dsem[src[1]] if isinstance(src, tuple) else self.sem[src]

    def _wait(self, e, deps):
        for src, val in deps.items():
            if src == e and e == "pe":
                continue
            if self.seen[e].get(src, 0) >= val:
                continue
            self.eng[e].wait_ge(self._semof(src), val)
            self.seen[e][src] = val
            self.nins += 1

    def op(self, e, fn, reads, writes, inc=True):
        self._wait(e, self._deps(reads, writes, e))
        ins = fn()
        self.nins += 1
        if inc:
            self.cnt[e] += 1
            ins.then_inc(self.sem[e], 1)
            val = self.cnt[e]
        else:
            val = self.cnt[e] + 1
        self._record(reads, writes, e, val)
        return ins

    def dma(self, out, in_, q="sp", is_out=False):
        deps = self._deps([in_], [out])
        i = self.drr
        self.drr = (self.drr + 1) % self.NDMA
        if self.dcnt[i] > 0:
            deps[("dma", i)] = max(deps.get(("dma", i), 0), 16 * self.dcnt[i])
        self._wait(q, deps)
        ins = self.eng[q].dma_start(out=out, in_=in_, allow_slow_non_contiguous=True)
        self.nins += 1
        self.dcnt[i] += 1
        ins.then_inc(self.dsem[i], 16)
        val = 16 * self.dcnt[i]
        self._record([in_], [out], ("dma", i), val)
        if is_out:
            self.out_dmas[("dma", i)] = val

    def finish(self, q="sp"):
        deps = dict(self.out_dmas)
        for e in self.eng:
            if e != q and self.cnt[e] > 0:
                deps[e] = self.cnt[e]
        self._wait(q, deps)

    def mm(self, out, lhsT, rhs, start=True, stop=True):
        return self.op("pe", lambda: self.nc.tensor.matmul(out, lhsT, rhs, start=start, stop=stop),
                       [lhsT, rhs], [out], inc=stop)

    def transpose(self, out, in_, ident, inc=True):
        return self.op("pe", lambda: self.nc.tensor.transpose(out, in_, ident), [in_, ident], [out], inc=inc)

    def tt(self, out, in0, in1, op, e="dve"):
        return self.op(e, lambda: self.eng[e].tensor_tensor(out=out, in0=in0, in1=in1, op=op), [in0, in1], [out])

    def ts(self, out, in0, s1, s2=None, op0=ALU.mult, op1=None, e="dve"):
        rd = [in0] + [s for s in (s1, s2) if not isinstance(s, (int, float, type(None)))]
        if op1 is None:
            return self.op(e, lambda: self.eng[e].tensor_scalar(out=out, in0=in0, scalar1=s1, scalar2=None, op0=op0),
                           rd, [out])
        return self.op(e, lambda: self.eng[e].tensor_scalar(out=out, in0=in0, scalar1=s1, scalar2=s2, op0=op0,
                                                            op1=op1), rd, [out])

    def stt(self, out, in0, scalar, in1, op0, op1):
        rd = [in0, in1] + ([] if isinstance(scalar, (int, float)) else [scalar])
        return self.op("dve", lambda: self.nc.vector.scalar_tensor_tensor(out=out, in0=in0, scalar=scalar, in1=in1,
                                                                           op0=op0, op1=op1), rd, [out])

    def act(self, out, in_, func, bias=None, scale=None, accum_out=None):
        rd = [in_]
        wr = [out]
        kw = {}
        if bias is not None:
            kw["bias"] = bias
            if not isinstance(bias, (int, float)):
                rd.append(bias)
        if scale is not None:
            kw["scale"] = scale
            if not isinstance(scale, (int, float)):
                rd.append(scale)
        if accum_out is not None:
            kw["accum_out"] = accum_out
            wr.append(accum_out)
        return self.op("act", lambda: self.nc.scalar.activation(out=out, in_=in_, func=func, **kw), rd, wr)

    def copy(self, out, in_, e="dve"):
        if e == "act":
            return self.act(out, in_, AF.Copy)
        return self.op(e, lambda: self.eng[e].tensor_copy(out=out, in_=in_), [in_], [out])

    def scan(self, out, d0, d1, initial):
        rd = [d0, d1] + ([] if isinstance(initial, (int, float)) else [initial])
        return self.op("dve", lambda: self.nc.vector.tensor_tensor_scan(out=out, data0=d0, data1=d1, initial=initial,
                                                                         op0=ALU.mult, op1=ALU.add), rd, [out])

    def memset(self, ap, val, e="pool"):
        return self.op(e, lambda: self.eng[e].memset(ap, val), [], [ap])

    def recip(self, out, in_, fast=False):
        return self.op("dve", lambda: self.nc.vector.reciprocal(out=out, in_=in_), [in_], [out])


class Arena:
    def __init__(self, nc, name, nbytes):
        self.words = nbytes // 4
        self.t = nc.alloc_sbuf_tensor(name, [128, self.words], F32)
        self.off = 0
        self.peak = 0

    def alloc(self, shape, dtype, parts=128):
        n = 1
        for s in shape:
            n *= s
        nb = (n * ESZ[dtype] + 31) // 32 * 32
        w0 = self.off // 4
        self.off += nb
        self.peak = max(self.peak, self.off)
        assert self.off // 4 <= self.words, "arena overflow %s %d > %d" % (self.t.name, self.off, self.words * 4)
        v = self.t[0:parts, w0:w0 + nb // 4]
        if dtype != F32:
            v = v.bitcast(dtype)
        v = v[:, 0:n]
        if len(shape) == 2:
            v = v.rearrange("p (a b) -> p a b", a=shape[0])
        elif len(shape) == 3:
            v = v.rearrange("p (a b c) -> p a b c", a=shape[0], b=shape[1])
        elif len(shape) == 4:
            v = v.rearrange("p (a b c d) -> p a b c d", a=shape[0], b=shape[1], c=shape[2])
        return v

    def mark(self):
        return self.off

    def reset(self, m):
        self.off = m


class _Stop(Exception):
    pass


class Cfg:
    def __init__(self, nch=32, depth=4, nsb=4, nseq=16):
        self.nch, self.depth, self.nsb, self.nseq = nch, depth, nsb, nseq
        self.stop = 0


def host_consts(cfg):
    nch, nsb = cfg.nch, cfg.nsb
    f = np.float32
    c = {}
    c["ident"] = np.eye(128, dtype=f)
    j = np.arange(64)[:, None]
    i = np.arange(64)[None, :]
    same = (j // 4) == (i // 4)
    c["negmask_p"] = np.where(j <= i, 0.0, NEG).astype(f)
    c["negmask_s"] = np.where((j <= i) & same, 0.0, NEG).astype(f)
    c["strict_p"] = np.where(j < i, -1.0, 0.0).astype(f)
    c["strict_s"] = np.where((j < i) & same, -1.0, 0.0).astype(f)
    c["incl_p"] = np.where(j <= i, 1.0, 0.0).astype(f)
    c["incl_s"] = np.where((j <= i) & same, 1.0, 0.0).astype(f)
    lg = np.log(1.0 - 2.0 ** (-5.0 - np.arange(4, dtype=np.float64)))
    rp = np.zeros((64, 4, 64), np.float64)
    rs = np.zeros((64, 4, 64), np.float64)
    for h in range(4):
        rp[:, h, :] = np.where(j <= i, np.exp(lg[h] * np.maximum(i - j, 0)), 0.0) * 0.125
        rs[:, h, :] = np.where((j <= i) & same, np.exp(lg[h] * np.maximum(i - j, 0)), 0.0) * 0.125
    c["retm_p"] = rp.astype(f)
    c["retm_s"] = rs.astype(f)
    c["seqmask"] = (np.arange(64)[:, None] // 4 == np.arange(16)[None, :]).astype(f)
    sel = np.zeros((8, 8, 128), f)
    for k in range(8):
        sel[k, k, :] = 1.0
    c["sel"] = sel
    perm = np.zeros((128, 128), f)
    for m in range(128):
        k = m + 32 if (m % 64) < 32 else m - 32
        perm[k, m] = 1.0
    c["perm"] = perm
    hrow = np.arange(128) // 64
    fs64 = np.zeros((128, 2, 64), np.float64)
    ts64 = np.zeros((128, 2, 64), np.float64)
    ts16 = np.zeros((128, 2, 16), np.float64)
    fss = np.zeros((128, 2, 64), np.float64)
    tss = np.zeros((128, 2, 64), np.float64)
    dec = np.zeros((128, 2, 3), np.float64)
    pos = np.arange(64)
    for t in range(2):
        g = lg[2 * t + hrow][:, None]
        fs64[:, t, :] = np.exp(g * (pos[None, :] + 1.0))
        ts64[:, t, :] = np.exp(g * (63.0 - pos[None, :])) * 0.125
        ts16[:, t, :] = np.exp(g * (15.0 - pos[None, :16])) * 0.125
        fss[:, t, :] = np.exp(g * ((pos[None, :] % 4) + 1.0))
        tss[:, t, :] = np.exp(g * (3.0 - (pos[None, :] % 4))) * 0.125
        dec[:, t, 0] = np.exp(g[:, 0] * 64.0)
        dec[:, t, 1] = np.exp(g[:, 0] * 16.0)
        dec[:, t, 2] = np.exp(g[:, 0] * 4.0)
    c["fs64"], c["ts64"], c["ts16"], c["fss"], c["tss"], c["retdec"] = [a.astype(f) for a in (fs64, ts64, ts16, fss, tss, dec)]
    lay = layout(cfg)
    tbm = lay["tbmax"]
    reset = np.ones((nsb, 128, tbm), f)
    cos = np.zeros((nsb, 128, tbm), f)
    sin = np.zeros((nsb, 128, tbm), f)
    half = 32
    freqs = (10000.0 ** (-np.arange(half, dtype=np.float32) / half)).astype(np.float32)
    fr = freqs[np.arange(128) % 32]
    sgn = np.where((np.arange(128) % 64) < 32, -1.0, 1.0).astype(f)
    for sb in range(nsb):
        L = lay["sb"][sb]
        posv = np.zeros(tbm, np.float32)
        for (c0, cc, kind, g0) in L["chunks"]:
            if kind == "p":
                reset[sb, :, c0] = 0.0
                posv[c0:c0 + cc] = np.arange(g0, g0 + cc)
            else:
                for s in range(16):
                    reset[sb, :, c0 + 4 * s] = 0.0
                posv[c0:c0 + cc] = 16384 + (np.arange(64) % 4)
        ang = (posv[None, :].astype(np.float32) * fr[:, None]).astype(np.float32)
        cos[sb] = np.cos(ang)
        sin[sb] = np.sin(ang) * sgn[:, None]
    c["reset"], c["cos"], c["sin"] = reset, cos, sin
    return c


def layout(cfg):
    nch, nsb = cfg.nch, cfg.nsb
    tp = 16 + 64 * nch
    per = nch // nsb
    sbs = []
    for sb in range(nsb):
        chunks = []
        col = 0
        p0 = 0 if sb == 0 else 16 + 64 * per * sb
        if sb == 0:
            chunks.append((0, 16, "p", 0))
            col = 16
        for i in range(per * sb, per * (sb + 1)):
            chunks.append((col, 64, "p", 16 + 64 * i))
            col += 64
        npc = col
        if sb == nsb - 1:
            chunks.append((col, 64, "s", tp))
            col += 64
        sbs.append({"chunks": chunks, "np": npc, "tbl": col, "p0": p0, "last": sb == nsb - 1})
    return {"sb": sbs, "tbmax": max(s["tbl"] for s in sbs), "tp": tp, "tt": tp + 64}


def build(cfg, dbg=False):
    nc = bass.Bass("TRN2", target_bir_lowering=False)
    nc.allow_non_contiguous_dma(reason="small strided parameter / state loads").__enter__()
    S = Sch(nc)
    DEPTH, NCH, NSB, NSEQ = cfg.depth, cfg.nch, cfg.nsb, cfg.nseq
    lay = layout(cfg)
    TP, TT, TBM = lay["tp"], lay["tt"], lay["tbmax"]
    SEQ = 64 * NCH

    def din(name, shape):
        return nc.dram_tensor(name, list(shape), F32, kind="ExternalInput").ap()

    def dout(name, shape):
        return nc.dram_tensor(name, list(shape), F32, kind="ExternalOutput").ap()

    xp_d = din("xp", [SEQ, D])
    xs_d = din("xs", [64, D])
    meta_d = din("meta_tokens", [16, D])
    sdelta_d = din("sdelta", [DEPTH, NSEQ, 4, 128, 128])
    sdconv_d = din("sdconv", [DEPTH, NSEQ * 3, 1536])
    slru_d = din("slru", [DEPTH, NSEQ, 512])
    slconv_d = din("slconv", [DEPTH, NSEQ * 3, 512])
    sgla_d = din("sgla", [DEPTH, NSEQ, 4, 64, 128])
    sret_d = din("sret", [DEPTH, NSEQ, 4, 64, 128])
    normw_d = din("norm_w", [DEPTH, D])
    win_d = din("w_in", [DEPTH, D, INW])
    conva_d = din("conv_a", [DEPTH, 4, 1536])
    alog_d = din("a_log", [DEPTH, 4])
    dtb_d = din("dt_bias", [DEPTH, 4])
    norma_d = din("norm_a", [DEPTH, 128])
    convb_d = din("conv_b", [DEPTH, 4, 512])
    convbb_d = din("conv_b_bias", [DEPTH, 512])
    lruwa_d = din("lru_wa", [DEPTH, 4, 128, 128])
    lruba_d = din("lru_ba", [DEPTH, 512])
    lruwx_d = din("lru_wx", [DEPTH, 4, 128, 128])
    lrubx_d = din("lru_bx", [DEPTH, 512])
    lrulam_d = din("lru_lambda", [DEPTH, 512])
    glaw2_d = din("gla_w2", [DEPTH, 16, 256])
    glab2_d = din("gla_b2", [DEPTH, 256])
    normc_d = din("norm_c", [DEPTH, 128])
    wout_d = din("w_out", [DEPTH, D, D])
    fnorm_d = din("final_norm", [D])
    hc = host_consts(cfg)
    cd = {k: din("c_" + k, v.shape) for k, v in hc.items()}

    yp_d = dout("y_p", [SEQ, D])
    ys_d = dout("y_s", [64, D])
    pdelta_d = dout("p_delta", [DEPTH, 4, 128, 128])
    pdconv_d = dout("p_dconv", [DEPTH, 3, 1536])
    plru_d = dout("p_lru", [DEPTH, 4, 128])
    plconv_d = dout("p_lconv", [DEPTH, 3, 512])
    pgla_d = dout("p_gla", [DEPTH, 4, 64, 128])
    pret_d = dout("p_ret", [DEPTH, 4, 64, 128])
    sdelta_o = dout("s_delta", [DEPTH, NSEQ, 4, 128, 128])
    sdconv_o = dout("s_dconv", [DEPTH, NSEQ * 3, 1536])
    slru_o = dout("s_lru", [DEPTH, NSEQ, 512])
    slconv_o = dout("s_lconv", [DEPTH, NSEQ * 3, 512])
    sgla_o = dout("s_gla", [DEPTH, NSEQ, 4, 64, 128])
    sret_o = dout("s_ret", [DEPTH, NSEQ, 4, 64, 128])
    xres = nc.dram_tensor("xres", [TT, D], F32, kind="Internal").ap()

    def sb(name, shape, dt=F32):
        return nc.alloc_sbuf_tensor(name, list(shape), dt)

    ident_f = sb("ident_f", [128, 128])
    ident_b = sb("ident_b", [128, 128], BF16)
    ones_b = sb("ones_b", [128, 128], BF16)
    perm_b = sb("perm_b", [128, 128], BF16)
    perm_f = sb("perm_f", [128, 128])
    cm = {}
    for k in ("negmask_p", "negmask_s", "strict_p", "strict_s", "incl_p", "incl_s"):
        cm[k] = sb("m_" + k, [128, 64])
    for k in ("retm_p", "retm_s"):
        cm[k] = sb("m_" + k, [128, 4, 64])
    cm["seqmask"] = sb("m_seqmask", [128, 16])
    seqmask_b = sb("seqmask_b", [128, 16], BF16)
    sel = sb("sel", [8, 8, 128])
    for k in ("fs64", "ts64", "fss", "tss"):
        cm[k] = sb("m_" + k, [128, 2, 64])
    cm["ts16"] = sb("m_ts16", [128, 2, 16])
    cm["retdec"] = sb("m_retdec", [128, 2, 3])
    reset_t = sb("reset_t", [128, TBM])
    cos_t = sb("cos_t", [128, TBM])
    sin_t = sb("sin_t", [128, TBM])
    NPC = 100 * DEPTH + 8 * DEPTH + 8
    pcols = sb("pcols", [128, NPC])
    prm8 = sb("prm8", [8, DEPTH, 2])
    alg8 = sb("alg8", [8, DEPTH])
    xnT = sb("xnT", [128, KT, TBM], BF16)
    mixT = sb("mixT", [128, KT, TBM], BF16)
    NST = 2
    wstg = [sb("wstg%d" % i, [128, KT, 128]) for i in range(NST)]
    NWB = 3
    wbf = [sb("wbf%d" % i, [128, KT, 128], BF16) for i in range(NWB)]
    wo2 = [sb("wo%d" % i, [128, KT, 256], BF16) for i in range(2)]
    sstg = [sb("sstg%d" % i, [128, 6, 128]) for i in range(2)]
    Sd = sb("Sd", [128, 4, 128])
    Sdb = sb("Sdb", [128, 4, 128], BF16)
    Sg = sb("Sg", [128, 2, 128])
    Sgb = sb("Sgb", [128, 2, 128], BF16)
    Sr = sb("Sr", [128, 2, 128])
    Srb = sb("Srb", [128, 2, 128], BF16)
    hl = sb("hl", [128, 4])
    carryA = sb("carryA", [128, 12, 3])
    carryB = sb("carryB", [128, 4, 3])
    lwa = sb("lwa", [128, 4, 128], BF16)
    lwx = sb("lwx", [128, 4, 128], BF16)
    w2b = sb("w2b", [16, 256], BF16)
    ga = Arena(nc, "garena", cfg.ga_bytes if hasattr(cfg, "ga_bytes") else 86 * 1024)

    pbanks = [nc.alloc_psum_tensor("pb%d" % i, [128, 512], F32) for i in range(8)]
    pstate = {"i": 0}

    def ps():
        b = pbanks[pstate["i"] % 8]
        pstate["i"] += 1
        return b

    evs = {"i": 0}

    def evac(out, in_):
        evs["i"] += 1
        S.copy(out, in_, e=("act" if evs["i"] % 2 else "dve"))

    def CP(k):
        if cfg.stop == k:
            raise _Stop()

    try:
        S.dma(ident_f[:], cd["ident"])
        S.copy(ident_b[:], ident_f[:], e="dve")
        S.memset(ones_b[:], 1.0)
        S.dma(perm_f[:], cd["perm"])
        S.copy(perm_b[:], perm_f[:], e="dve")
        for k in cm:
            if cm[k].shape[0] == 128 and cd[k].shape[0] == 64:
                S.dma(cm[k][0:64], cd[k])
                S.dma(cm[k][64:128], cd[k])
            else:
                S.dma(cm[k][:], cd[k])
        S.copy(seqmask_b[:], cm["seqmask"][:], e="dve")
        S.dma(sel[:], cd["sel"])

        CP(1)
        pc = {"n": 0}

        def load_cols(dram2d, rows):
            base = pc["n"]
            r0 = 0
            while r0 < rows:
                r = min(128, rows - r0)
                stg = sstg[0][0:r, 0, :]
                S.dma(stg, dram2d[r0:r0 + r, :])
                pb = ps()
                S.transpose(pb[:, 0:r], stg, ident_f[0:r, 0:r])
                evac(pcols[:, base + r0: base + r0 + r], pb[:, 0:r])
                r0 += r
            pc["n"] += rows
            return base

        P_NORMW = load_cols(normw_d.rearrange("l (a p) -> (l a) p", p=128), DEPTH * 16)
        P_CONVA = load_cols(conva_d.rearrange("l k (a p) -> (l k a) p", p=128), DEPTH * 48)
        P_CONVB = load_cols(convb_d.rearrange("l k (a p) -> (l k a) p", p=128), DEPTH * 16)
        P_CONVBB = load_cols(convbb_d.rearrange("l (a p) -> (l a) p", p=128), DEPTH * 4)
        P_LBA = load_cols(lruba_d.rearrange("l (a p) -> (l a) p", p=128), DEPTH * 4)
        P_LBX = load_cols(lrubx_d.rearrange("l (a p) -> (l a) p", p=128), DEPTH * 4)
        P_LAM = load_cols(lrulam_d.rearrange("l (a p) -> (l a) p", p=128), DEPTH * 4)
        P_B2 = load_cols(glab2_d.rearrange("l (a p) -> (l a) p", p=128), DEPTH * 2)
        P_NA = load_cols(norma_d, DEPTH)
        P_NC = load_cols(normc_d, DEPTH)
        P_NSP8 = pc["n"]
        pc["n"] += DEPTH * 4
        P_NB2 = pc["n"]
        pc["n"] += DEPTH * 2
        assert pc["n"] <= NPC
        lamc = pcols[:, P_LAM:P_LAM + DEPTH * 4]
        nsp = pcols[:, P_NSP8:P_NSP8 + DEPTH * 4]
        S.act(nsp, lamc, AF.Exp, scale=-1.0)
        S.act(nsp, nsp, AF.Ln, bias=1.0)
        S.ts(nsp, nsp, -8.0, op0=ALU.mult)
        S.ts(pcols[:, P_NB2:P_NB2 + DEPTH * 2], pcols[:, P_B2:P_B2 + DEPTH * 2], -1.0, op0=ALU.mult)
        S.memset(prm8[:], 0.0)
        S.memset(alg8[:], 0.0)
        S.dma(prm8[0:4, :, 0], dtb_d.rearrange("l h -> h l"))
        S.dma(alg8[0:4, :], alog_d.rearrange("l h -> h l"))
        S.act(alg8[:], alg8[:], AF.Exp)
        S.ts(prm8[:, :, 1], alg8[:], -1.0, op0=ALU.mult)

        CP(2)
        S.dma(xres[0:16, :], meta_d)
        R = 0
        while R < SEQ:
            r = min(512, SEQ - R)
            S.dma(xres[16 + R:16 + R + r, :], xp_d[R:R + r, :])
            R += r
        S.dma(xres[TP:TP + 64, :], xs_d)

        CP(3)
        WSEQ = []
        nwb_ = 0
        ncb_ = 0
        for l_ in range(DEPTH):
            for sb_ in range(NSB):
                blocks = [(C_QKV + j * 128, 128) for j in range(12)] + [(C_AL, 8)] + [(C_GATE + m * 128, 128) for m in range(4)]
                for n_ in range(4):
                    blocks += [(C_XB + n_ * 128, 128), (C_GATE + (4 + n_) * 128, 128)]
                blocks += [(C_RC, 16)]
                for t in range(2):
                    blocks += [(C_QC + t * 128, 128), (C_KC + t * 128, 128)]
                blocks += [(C_VC + h * 128, 128) for h in range(4)] + [(C_GATE + (8 + h) * 128, 128) for h in range(4)]
                blocks += [(C_QD + t * 128, 128) for t in range(2)] + [(C_KD + t * 128, 128) for t in range(2)]
                blocks += [(C_VD + h * 128, 128) for h in range(4)] + [(C_GATE + (12 + h) * 128, 128) for h in range(4)]
                for (c0_, nco_) in blocks:
                    WSEQ.append(("in", l_, c0_, nco_, nwb_ % NWB, 0))
                    nwb_ += 1
                for cb in range(8):
                    for q in range(2):
                        WSEQ.append(("out", l_, cb * 256 + q * 128, 128, ncb_ % 2, q))
                    ncb_ += 1
        wst = {"dma": 0, "cast": 0, "used": 0}

        def w_dest(k):
            kind, l_, c0_, nco_, slot, q = WSEQ[k]
            if kind == "in":
                return wbf[slot][:, :, 0:nco_]
            return wo2[slot][:, :, q * 128:(q + 1) * 128]

        def w_dma(k):
            kind, l_, c0_, nco_, slot, q = WSEQ[k]
            src = (win_d if kind == "in" else wout_d)[l_]
            S.dma(wstg[k % NST][:, :, 0:nco_], src[:, c0_:c0_ + nco_].rearrange("(kt p) c -> p kt c", p=128))

        def w_cast(k):
            nco_ = WSEQ[k][3]
            S.copy(w_dest(k), wstg[k % NST][:, :, 0:nco_], e=("act" if k % 2 else "dve"))

        def w_get(kind, l_, c0_, nco_):
            k = wst["used"]
            assert WSEQ[k][0:4] == (kind, l_, c0_, nco_), (k, WSEQ[k], kind, l_, c0_, nco_)
            while True:
                prog = False
                j = wst["dma"]
                if j < min(k + 1 + LOOKD, len(WSEQ)) and wst["cast"] > j - NST:
                    w_dma(j)
                    wst["dma"] += 1
                    prog = True
                j = wst["cast"]
                if j < min(k + 1 + LOOKC, len(WSEQ)) and j < wst["dma"]:
                    w_cast(j)
                    wst["cast"] += 1
                    prog = True
                if not prog:
                    break
            assert wst["cast"] > k
            wst["used"] += 1
            return WSEQ[k]

        def load_w(w2d, c0, ncols, scale_base=None, rows=KT):
            e_ = w_get("in", l, c0, ncols)
            return wbf[e_[4]]

        def ttiles(L):
            res = []
            t0 = 0
            while t0 < L["np"]:
                n = min(512, L["np"] - t0)
                res.append((t0, n, "p"))
                t0 += n
            if L["last"]:
                res.append((L["np"], 64, "s"))
            return res

        pend = {"f": None, "f2": None}

        def _drain_one():
            f, f2 = pend["f"], pend["f2"]
            pend["f"] = None
            pend["f2"] = None
            if f is not None:
                pend["f2"] = f()
            if f2 is not None:
                f2()

        def flush():
            _drain_one()
            _drain_one()

        def project(wb, M, L, consumer, post=None, post2=None):
            banks = []
            for (t0, n, kind) in ttiles(L):
                pb = ps()
                for kt in range(KT):
                    S.mm(pb[0:M, 0:n], wb[:, kt, 0:M], xnT[:, kt, t0:t0 + n], start=(kt == 0), stop=(kt == KT - 1))
                banks.append((pb, t0, n, kind))
            _drain_one()

            def epi():
                for (pb, t0, n, kind) in banks:
                    consumer(pb, t0, n, kind)
                if post is not None:
                    post()
                return post2

            pend["f"] = epi

        def bview(ap2, a, b):
            return ap2.rearrange("p (a b) -> p a b", a=a)

        def chunk_last(dst, src, L):
            npc = L["np"]
            c0 = 0
            if L["chunks"][0][1] == 16:
                S.copy(dst[:, 0:16], src[:, 15:16].to_broadcast([dst.shape[0], 16]), e="dve")
                c0 = 16
            n64 = (npc - c0) // 64
            if n64 > 0:
                sv = bview(src[:, c0:npc], n64, 64)
                S.copy(bview(dst[:, c0:npc], n64, 64), sv[:, :, 63:64].to_broadcast([dst.shape[0], n64, 64]), e="dve")
            if L["last"]:
                sv = bview(src[:, npc:npc + 64], 16, 4)
                S.copy(bview(dst[:, npc:npc + 64], 16, 4), sv[:, :, 3:4].to_broadcast([dst.shape[0], 16, 4]), e="dve")

        def chunk_last_cols(L):
            cols = []
            for (c0, cc, kind, g0) in L["chunks"]:
                if kind == "p":
                    cols.append(c0 + cc - 1)
                else:
                    cols += [c0 + 4 * s + 3 for s in range(16)]
            return cols

        def compact_last(dst, src, L):
            k = 0
            npc = L["np"]
            c0 = 0
            if L["chunks"][0][1] == 16:
                S.copy(dst[:, 0:1], src[:, 15:16], e="dve")
                k = 1
                c0 = 16
            n64 = (npc - c0) // 64
            if n64 > 0:
                S.copy(dst[:, k:k + n64], bview(src[:, c0:npc], n64, 64)[:, :, 63], e="dve")
                k += n64
            if L["last"]:
                S.copy(dst[:, k:k + 16], bview(src[:, npc:npc + 64], 16, 4)[:, :, 3], e="dve")
                k += 16
            return k

        def conv_block(prew, presw, acc, wbase, L, bias_col=None):
            npc = L["np"]
            w = [pcols[:, wbase + k:wbase + k + 1] for k in range(4)]
            if bias_col is None:
                S.act(acc[:, 0:npc], prew[:, 3:3 + npc], AF.Identity, scale=w[3])
            else:
                S.act(acc[:, 0:npc], prew[:, 3:3 + npc], AF.Identity, scale=w[3], bias=bias_col)
            for k in (2, 1, 0):
                S.stt(acc[:, 0:npc], prew[:, k:k + npc], w[k], acc[:, 0:npc], ALU.mult, ALU.add)
            if L["last"]:
                a3 = bview(acc[:, npc:npc + 64], 16, 4)
                if bias_col is None:
                    S.act(a3, presw[:, :, 3:7], AF.Identity, scale=w[3])
                else:
                    S.act(a3, presw[:, :, 3:7], AF.Identity, scale=w[3], bias=bias_col)
                for k in (2, 1, 0):
                    S.stt(a3, presw[:, :, k:k + 4], w[k], a3, ALU.mult, ALU.add)

        def out_rows(dst2d, src, ncols):
            pb = ps()
            S.transpose(pb[0:ncols, 0:128], src, ident_f[:, :])
            tmp = ga.alloc([128], F32)
            evac(tmp[0:ncols, :], pb[0:ncols, 0:128])
            S.dma(dst2d, tmp[0:ncols, :], is_out=True)

        for l in range(DEPTH):
            wl = win_d[l]
            S.dma(sstg[0][:, 0:4, :], lruwa_d[l].rearrange("n c d -> c n d"))
            S.copy(lwa[:], sstg[0][:, 0:4, :], e="pool")
            S.dma(sstg[1][:, 0:4, :], lruwx_d[l].rearrange("n c d -> c n d"))
            S.copy(lwx[:], sstg[1][:, 0:4, :], e="pool")
            S.dma(sstg[0][0:16, 4, :], glaw2_d[l][:, 0:128])
            S.dma(sstg[0][0:16, 5, :], glaw2_d[l][:, 128:256])
            S.copy(w2b[:].rearrange("p (a b) -> p a b", a=2), sstg[0][0:16, 4:6, :], e="pool")
            S.memset(Sd[:], 0.0)
            S.memset(Sdb[:], 0.0)
            S.memset(Sg[:], 0.0)
            S.memset(Sgb[:], 0.0)
            S.memset(Sr[:], 0.0)
            S.memset(Srb[:], 0.0)
            S.memset(hl[:], 0.0)
            S.memset(carryA[:], 0.0)
            S.memset(carryB[:], 0.0)

            for sbi in range(NSB):
                L = lay["sb"][sbi]
                npc, TBL, last = L["np"], L["tbl"], L["last"]
                if NSB > 1 or l == 0:
                    S.dma(reset_t[:, 0:TBM], cd["reset"][sbi])
                    S.dma(cos_t[:, 0:TBM], cd["cos"][sbi])
                    S.dma(sin_t[:, 0:TBM], cd["sin"][sbi])
                gm0 = ga.mark()

                rows = []
                r = 0
                while r < npc:
                    n = min(128, npc - r)
                    rows.append((L["p0"] + r, n, r))
                    r += n
                if last:
                    rows.append((TP, 64, npc))

                m0 = ga.mark()
                xt2 = [ga.alloc([D], F32) for _ in range(2)]
                xs2 = [ga.alloc([D], BF16) for _ in range(2)]
                junk = ga.alloc([D], BF16)
                st = ga.alloc([8], F32)
                for ti, (r0, n, c0) in enumerate(rows):
                    xt = xt2[ti % 2]
                    xs = xs2[ti % 2]
                    S.dma(xt[0:n, :], xres[r0:r0 + n, :])
                    S.act(junk[0:n, :], xt[0:n, :], AF.Square, accum_out=st[0:n, 0:1])
                    S.act(st[0:n, 1:2], st[0:n, 0:1], AF.Sqrt, scale=1.0 / D, bias=EPS)
                    S.recip(st[0:n, 2:3], st[0:n, 1:2])
                    S.act(xs[0:n, :], xt[0:n, :], AF.Identity, scale=st[0:n, 2:3])
                    for half in range(2):
                        pb = ps()
                        pbb = pb[:, :].bitcast(BF16)
                        for j in range(8):
                            kt = half * 8 + j
                            S.transpose(pbb[:, j * 128:j * 128 + n], xs[0:n, kt * 128:(kt + 1) * 128], ident_b[0:n, 0:n],
                                        inc=(j == 7))
                        S.tt(xnT[:, half * 8:half * 8 + 8, c0:c0 + n], pbb.rearrange("p (a b) -> p a b", a=8)[:, :, 0:n],
                             pcols[:, P_NORMW + l * 16 + half * 8:P_NORMW + l * 16 + half * 8 + 8].unsqueeze(2)
                             .to_broadcast([128, 8, n]), ALU.mult)
                ga.reset(m0)

                CP(4)
                nwcol = P_NORMW + l * 16

                gcount = {"i": 0}

                def gate_mix(mt, o_ap, normcol, do_norm):
                    wb = load_w(wl, C_GATE + mt * 128, 128, nwcol)
                    mk = ga.mark()
                    SGs = [ga.alloc([TBL], F32) for _ in range(2)]
                    RSs = [ga.alloc([TBL], F32) for _ in range(2)]
                    SQg = ga.alloc([TBL], BF16)
                    TMg = ga.alloc([TBL], F32)
                    par = gcount["i"] % 2
                    gcount["i"] += 1
                    SG, RS = SGs[par], RSs[par]

                    def cons(pb, t0, n, kind):
                        S.act(SG[:, t0:t0 + n], pb[:, 0:n], AF.Silu)
                        o = o_ap[:, t0:t0 + n]
                        if do_norm:
                            S.act(SQg[:, t0:t0 + n], o, AF.Square)
                            pq = ps()
                            S.mm(pq[:, 0:n], ones_b[:, :], SQg[:, t0:t0 + n])
                            S.act(RS[:, t0:t0 + n], pq[:, 0:n], AF.Ln, scale=1.0 / 128, bias=EPS)
                        else:
                            S.tt(mixT[:, mt, t0:t0 + n], o, SG[:, t0:t0 + n], ALU.mult)

                    def post2():
                        if not do_norm:
                            return
                        S.act(RS[:, 0:TBL], RS[:, 0:TBL], AF.Exp, scale=-0.5)
                        S.tt(TMg[:, 0:TBL], o_ap[:, 0:TBL], RS[:, 0:TBL], ALU.mult)
                        if normcol is not None:
                            S.stt(mixT[:, mt, 0:TBL], TMg[:, 0:TBL], normcol, SG[:, 0:TBL], ALU.mult, ALU.mult)
                        else:
                            S.tt(mixT[:, mt, 0:TBL], TMg[:, 0:TBL], SG[:, 0:TBL], ALU.mult)

                    project(wb, 128, L, cons, None, post2)
                    ga.reset(mk)

                mA = ga.mark()
                QN = ga.alloc([4, TBL], BF16)
                KN = ga.alloc([4, TBL], BF16)
                BKN = ga.alloc([4, TBL], BF16)
                KQG = ga.alloc([4, 2, TBL], BF16)
                BKT = ga.alloc([4, TBL], BF16)
                VT = ga.alloc([4, TBL], BF16)
                OT = ga.alloc([4, TBL], BF16)
                mA2 = ga.mark()
                pre2 = [ga.alloc([3 + max(npc, 1)], F32) for _ in range(2)]
                pres2 = [ga.alloc([16, 7], F32) for _ in range(2)]
                acc2 = [ga.alloc([TBL], F32) for _ in range(2)]
                sqb = ga.alloc([TBL], BF16)
                rinA = [ga.alloc([TBL], F32) for _ in range(2)]
                cst = ga.alloc([12, 48], F32)
                if last:
                    for half in range(2):
                        stg = sstg[half][0:48, 0:6, :]
                        S.dma(stg, sdconv_d[l][:, half * 768:(half + 1) * 768].rearrange("r (a p) -> r a p", p=128))
                        pb = ps()
                        for a in range(6):
                            S.transpose(pb[:, a * 48:(a + 1) * 48], sstg[half][0:48, a, :], ident_f[0:48, 0:48], inc=(a == 5))
                        evac(cst[:, half * 6:half * 6 + 6, :], pb[:, 0:288].rearrange("p (a b) -> p a b", a=6))
                for j in range(12):
                    which, h = j // 4, j % 4
                    wb = load_w(wl, C_QKV + j * 128, 128, nwcol)
                    prew, presw, acc = pre2[j % 2], pres2[j % 2], acc2[j % 2]
                    S.copy(prew[:, 0:3], carryA[:, j, :], e="pool")
                    if last:
                        S.copy(presw[:, :, 0:3], cst[:, j, :].rearrange("p (s r) -> p s r", r=3), e="pool")

                    def cons(pb, t0, n, kind, prew=prew, presw=presw):
                        if kind == "p":
                            evac(prew[:, 3 + t0:3 + t0 + n], pb[:, 0:n])
                        else:
                            evac(presw[:, :, 3:7], pb[:, 0:64].rearrange("p (s t) -> p s t", t=4))

                    def post(j=j, which=which, h=h, prew=prew, presw=presw, acc=acc):
                        wcolk = [pcols[:, P_CONVA + (l * 4 + k) * 12 + j: P_CONVA + (l * 4 + k) * 12 + j + 1] for k in range(4)]
                        if npc > 0:
                            S.act(acc[:, 0:npc], prew[:, 3:3 + npc], AF.Identity, scale=wcolk[3])
                            for k in (2, 1, 0):
                                S.stt(acc[:, 0:npc], prew[:, k:k + npc], wcolk[k], acc[:, 0:npc], ALU.mult, ALU.add)
                            S.copy(carryA[:, j, :], prew[:, npc:npc + 3], e="pool")
                        if last:
                            a3 = bview(acc[:, npc:npc + 64], 16, 4)
                            S.act(a3, presw[:, :, 3:7], AF.Identity, scale=wcolk[3])
                            for k in (2, 1, 0):
                                S.stt(a3, presw[:, :, k:k + 4], wcolk[k], a3, ALU.mult, ALU.add)
                            S.copy(cst[:, j, :].rearrange("p (s r) -> p s r", r=3), presw[:, :, 4:7], e="pool")
                        if which == 2:
                            S.act(VT[:, h, :], acc[:, 0:TBL], AF.Silu)
                        else:
                            S.act(acc[:, 0:TBL], acc[:, 0:TBL], AF.Silu)
                            S.act(sqb[:, 0:TBL], acc[:, 0:TBL], AF.Square)
                            dst = QN if which == 0 else KN
                            sc_ = 128.0 if which == 0 else 1.0
                            for (t0, n, kind) in ttiles(L):
                                pq = ps()
                                S.mm(pq[:, 0:n], ones_b[:, :], sqb[:, t0:t0 + n])
                                S.act(rinA[j % 2][:, t0:t0 + n], pq[:, 0:n], AF.Ln, scale=sc_, bias=sc_ * EPS)

                    def post2(j=j, which=which, h=h, acc=acc):
                        if which == 2:
                            return
                        dst = QN if which == 0 else KN
                        rn = rinA[j % 2]
                        S.act(rn[:, 0:TBL], rn[:, 0:TBL], AF.Exp, scale=-0.5)
                        S.tt(dst[:, h, 0:TBL], acc[:, 0:TBL], rn[:, 0:TBL], ALU.mult)

                    project(wb, 128, L, cons, post, post2)
                flush()
                if last:
                    for half in range(2):
                        tmp = ga.alloc([768], F32)
                        pb = ps()
                        for a in range(4):
                            S.transpose(pb[0:48, a * 128:(a + 1) * 128], cst[:, half * 6 + a, :], ident_f[:, :], inc=(a == 3))
                        evac(tmp[0:48, 0:512], pb[0:48, 0:512])
                        pb2 = ps()
                        for a in range(2):
                            S.transpose(pb2[0:48, a * 128:(a + 1) * 128], cst[:, half * 6 + 4 + a, :], ident_f[:, :], inc=(a == 1))
                        evac(tmp[0:48, 512:768], pb2[0:48, 0:256])
                        S.dma(sdconv_o[l][:, half * 768:(half + 1) * 768], tmp[0:48, :], is_out=True)
                    pb = ps()
                    S.transpose(pb[0:36, 0:128], carryA[:, :, :].rearrange("p a r -> p (a r)"), ident_f[:, :])
                    tmp = ga.alloc([128], F32)
                    evac(tmp[0:36, :], pb[0:36, 0:128])
                    for a in range(12):
                        S.dma(pdconv_d[l][:, a * 128:(a + 1) * 128], tmp[a * 3:a * 3 + 3, :], is_out=True)
                ga.reset(mA2)

                CP(5)
                GC = ga.alloc([TBL], F32)
                nlast = len(chunk_last_cols(L))
                GLc = ga.alloc([nlast], F32)
                decS = ga.alloc([4, nlast], F32)
                colG = ga.alloc([len(L["chunks"]), 8], F32)
                mAs = ga.mark()
                AB = ga.alloc([TBL], F32)
                Bt = ga.alloc([TBL], F32)
                G = ga.alloc([TBL], F32)
                GL = ga.alloc([TBL], F32)
                EKT = ga.alloc([TBL], F32)
                EG = ga.alloc([512], F32)
                wb = load_w(wl, C_AL, 8, nwcol)

                def cons(pb, t0, n, kind):
                    evac(AB[0:8, t0:t0 + n], pb[0:8, 0:n])

                project(wb, 8, L, cons)
                flush()
                A8, B8, G8, GC8, GL8, EK8 = AB[0:8, :], Bt[0:8, :], G[0:8, :], GC[0:8, :], GL[0:8, :], EKT[0:8, :]
                S.act(B8, A8, AF.Sigmoid)
                S.act(G8, A8, AF.Exp, bias=prm8[:, l, 0:1])
                S.act(G8, G8, AF.Ln, bias=1.0)
                S.ts(G8, G8, prm8[:, l, 1:2], op0=ALU.mult)
                S.scan(GC8, reset_t[0:8, 0:TBL], G8, 0.0)
                chunk_last(GL8, GC8, L)
                S.tt(EK8, GL8, GC8, ALU.subtract)
                S.act(EK8, EK8, AF.Exp)
                compact_last(GLc[0:8, :], GC8, L)
                for h in range(4):
                    pb = ps()
                    S.mm(pb[:, 0:nlast], sel[:, h, :], GLc[0:8, :])
                    S.act(decS[:, h, :], pb[:, 0:nlast], AF.Exp)
                for (t0, n, kind) in ttiles(L):
                    for h in range(4):
                        pg = ps()
                        S.mm(pg[:, 0:n], sel[:, h, :], GC8[:, t0:t0 + n])
                        S.act(EG[:, 0:n], pg[:, 0:n], AF.Exp)
                        S.tt(KQG[:, h, 0, t0:t0 + n], KN[:, h, t0:t0 + n], EG[:, 0:n], ALU.mult)
                        S.tt(KQG[:, h, 1, t0:t0 + n], QN[:, h, t0:t0 + n], EG[:, 0:n], ALU.mult)
                        pbb = ps()
                        S.mm(pbb[:, 0:n], sel[:, 4 + h, :], B8[:, t0:t0 + n])
                        S.tt(BKN[:, h, t0:t0 + n], KN[:, h, t0:t0 + n], pbb[:, 0:n], ALU.mult)
                        pe_ = ps()
                        S.mm(pe_[:, 0:n], sel[:, h, :], EK8[:, t0:t0 + n])
                        S.tt(BKT[:, h, t0:t0 + n], BKN[:, h, t0:t0 + n], pe_[:, 0:n], ALU.mult)
                for ci, (c0, cc, kind, g0) in enumerate(L["chunks"]):
                    pb = ps()
                    S.transpose(pb[0:cc, 0:8], GC8[:, c0:c0 + cc], ident_f[0:8, 0:8])
                    evac(colG[0:cc, ci, :], pb[0:cc, 0:8])

                CP(6)
                ga.reset(mAs)
                mA3 = ga.mark()
                nchk = len(L["chunks"])
                RT = ga.alloc([4, 64], BF16)
                OIN = ga.alloc([4, 64], F32)
                Rtok = ga.alloc([4, 128], BF16)
                Wb = ga.alloc([4, 128], BF16)

                def mkset():
                    d = {}
                    d["ARG"] = ga.alloc([4, 64], F32)
                    d["EI"] = ga.alloc([4, 64], F32)
                    d["ES"] = ga.alloc([4, 64], F32)
                    d["X"] = [ga.alloc([4, 64], F32) for _ in range(2)]
                    d["XT"] = [ga.alloc([4, 64], F32) for _ in range(2)]
                    d["PP"] = [ga.alloc([4, 64], F32) for _ in range(2)]
                    d["Xb"] = [ga.alloc([4, 64], BF16) for _ in range(2)]
                    d["XTb"] = [ga.alloc([4, 64], BF16) for _ in range(2)]
                    d["PPb"] = ga.alloc([4, 64], BF16)
                    d["AT"] = ga.alloc([4, 64], BF16)
                    d["TTb"] = ga.alloc([4, 64], BF16)
                    d["BKtok"] = ga.alloc([4, 128], BF16)
                    return d

                psets = [None, None]
                psets[(nchk + 1) % 2] = mkset()
                msamp = ga.mark()
                psets[nchk % 2] = mkset()

                def prep_gen(ci):
                    c0, c, kind, g0 = L["chunks"][ci]
                    isS = (kind == "s")
                    negm = cm["negmask_s"] if isS else cm["negmask_p"]
                    strm = cm["strict_s"] if isS else cm["strict_p"]
                    levels = 2 if isS else (6 if c == 64 else 4)
                    d = psets[ci % 2]
                    ARG, EI, ES, X, XT, PP, AT, TTb, BKtok = (d["ARG"], d["EI"], d["ES"], d["X"], d["XT"], d["PP"],
                                                              d["AT"], d["TTb"], d["BKtok"])
                    pg = ps()
                    for h in range(4):
                        S.mm(pg[0:c, h * 64:h * 64 + c], sel[:, h, 0:c], GC8[:, c0:c0 + c])
                    pk = ps()
                    pq = ps()
                    for h in range(4):
                        S.mm(pk[0:c, h * 64:h * 64 + c], BKN[:, h, c0:c0 + c], KN[:, h, c0:c0 + c])
                    for h in range(4):
                        S.mm(pq[0:c, h * 64:h * 64 + c], BKN[:, h, c0:c0 + c], QN[:, h, c0:c0 + c])
                    pbt = ps()
                    pbtb = pbt[:, :].bitcast(BF16)
                    for h in range(4):
                        S.transpose(pbtb[0:c, h * 128:(h + 1) * 128], BKT[:, h, c0:c0 + c], ident_b[:, :], inc=(h == 3))
                    for h in range(4):
                        S.stt(ARG[0:c, h, 0:c], pg[0:c, h * 64:h * 64 + c], colG[0:c, ci, h:h + 1], negm[0:c, 0:c],
                              ALU.subtract, ALU.add)
                    S.act(EI[0:c, :, 0:c], ARG[0:c, :, 0:c], AF.Exp)
                    S.tt(ES[0:c, :, 0:c], EI[0:c, :, 0:c], strm[0:c, 0:c].unsqueeze(1).to_broadcast([c, 4, c]), ALU.mult,
                         e="pool")
                    pk3 = pk[0:c, 0:256].rearrange("p (h i) -> p h i", h=4)[:, :, 0:c]
                    pq3 = pq[0:c, 0:256].rearrange("p (h i) -> p h i", h=4)[:, :, 0:c]
                    S.tt(X[0][0:c, :, 0:c], pk3, ES[0:c, :, 0:c], ALU.mult)
                    S.tt(AT[0:c, :, 0:c], pq3, EI[0:c, :, 0:c], ALU.mult)
                    evac(BKtok[0:c, :, :], pbtb[0:c, 0:512].rearrange("p (h d) -> p h d", h=4))
                    yield
                    pt = ps()
                    for h in range(4):
                        S.transpose(pt[0:c, h * 64:h * 64 + c], X[0][0:c, h, 0:c], ident_f[0:c, 0:c], inc=(h == 3))
                    pt3 = pt[0:c, 0:256].rearrange("p (h i) -> p h i", h=4)[:, :, 0:c]
                    evac(XT[0][0:c, :, 0:c], pt3)
                    S.tt(PP[0][0:c, :, 0:c], X[0][0:c, :, 0:c], ident_f[0:c, 0:c].unsqueeze(1).to_broadcast([c, 4, c]),
                         ALU.add, e="pool")
                    Xb, XTb, PPb = d["Xb"], d["XTb"], d["PPb"]
                    cur = 0
                    nlev = levels - 1
                    NF32 = 1
                    v3 = lambda p_: p_[0:c, 0:256].rearrange("p (h i) -> p h i", h=4)[:, :, 0:c]
                    for lv in range(nlev):
                        nxt = 1 - cur
                        lastlv = (lv == nlev - 1)
                        lowp = (lv >= NF32)
                        nlow = (lv + 1 >= NF32) and not lastlv
                        Xc, XTc = (Xb[cur], XTb[cur]) if lowp else (X[cur], XT[cur])
                        yield
                        pxt = ps()
                        for h in range(4):
                            S.mm(pxt[0:c, h * 64:h * 64 + c], Xc[0:c, h, 0:c], XTc[0:c, h, 0:c])
                        if not lastlv:
                            px = ps()
                            for h in range(4):
                                S.mm(px[0:c, h * 64:h * 64 + c], XTc[0:c, h, 0:c], Xc[0:c, h, 0:c])
                        XTn = XTb[nxt] if lowp else XT[nxt]
                        evac(XTn[0:c, :, 0:c], v3(pxt))
                        if nlow and not lowp:
                            evac(XTb[nxt][0:c, :, 0:c], v3(pxt))
                        if not lastlv:
                            if lowp:
                                evac(Xb[nxt][0:c, :, 0:c], v3(px))
                            else:
                                if nlow:
                                    evac(Xb[nxt][0:c, :, 0:c], v3(px))
                                else:
                                    evac(X[nxt][0:c, :, 0:c], v3(px))
                        yield
                        pp = ps()
                        for h in range(4):
                            if lowp:
                                S.mm(pp[0:c, h * 64:h * 64 + c], XTb[nxt][0:c, h, 0:c], PPb[0:c, h, 0:c])
                            else:
                                S.mm(pp[0:c, h * 64:h * 64 + c], XT[nxt][0:c, h, 0:c], PP[cur][0:c, h, 0:c])
                        if lastlv:
                            S.tt(TTb[0:c, :, 0:c], v3(pp), PP[cur][0:c, :, 0:c], ALU.add)
                        else:
                            S.tt(PP[nxt][0:c, :, 0:c], v3(pp), PP[cur][0:c, :, 0:c], ALU.add)
                            if nlow:
                                S.copy(PPb[0:c, :, 0:c], PP[nxt][0:c, :, 0:c], e="act")
                        cur = nxt

                def chain_gen(ci):
                    c0, c, kind, g0 = L["chunks"][ci]
                    isS = (kind == "s")
                    d = psets[ci % 2]
                    AT, TTb, BKtok = d["AT"], d["TTb"], d["BKtok"]
                    if not isS:
                        ppq = ps()
                        for h in range(4):
                            S.mm(ppq[:, h * 128:h * 128 + 2 * c].rearrange("p (a b) -> p a b", a=2), Sdb[:, h, :],
                                 KQG[:, h, :, c0:c0 + c])
                        for h in range(4):
                            v = ppq[:, h * 128:h * 128 + 2 * c].rearrange("p (a b) -> p a b", a=2)
                            S.tt(RT[:, h, 0:c], VT[:, h, c0:c0 + c], v[:, 0, :], ALU.subtract)
                        for h in range(4):
                            v = ppq[:, h * 128:h * 128 + 2 * c].rearrange("p (a b) -> p a b", a=2)
                            S.copy(OIN[:, h, 0:c], v[:, 1, :], e="act")
                        yield
                        prt = ps()
                        prtb = prt[:, :].bitcast(BF16)
                        for h in range(4):
                            S.transpose(prtb[0:c, h * 128:(h + 1) * 128], RT[:, h, 0:c], ident_b[:, :], inc=(h == 3))
                        evac(Rtok[0:c, :, :], prtb[0:c, 0:512].rearrange("p (h d) -> p h d", h=4))
                        yield
                        pw = ps()
                        for h in range(4):
                            S.mm(pw[0:c, h * 128:(h + 1) * 128], TTb[0:c, h, 0:c], Rtok[0:c, h, :])
                        evac(Wb[0:c, :, :], pw[0:c, :].rearrange("p (h d) -> p h d", h=4))
                        yield
                        pS = ps()
                        for h in range(4):
                            S.mm(pS[:, h * 128:(h + 1) * 128], BKtok[0:c, h, :], Wb[0:c, h, :])
                        po = ps()
                        for h in range(4):
                            S.mm(po[:, h * 64:h * 64 + c], Wb[0:c, h, :], AT[0:c, h, 0:c])
                        S.tt(Sd[:, :, :], Sd[:, :, :], decS[:, :, ci:ci + 1].to_broadcast([128, 4, 128]), ALU.mult)
                        S.tt(Sd[:, :, :], Sd[:, :, :], pS[:, :].rearrange("p (h d) -> p h d", h=4), ALU.add)
                        S.copy(Sdb[:, :, :], Sd[:, :, :], e="act")
                        S.tt(OT[:, :, c0:c0 + c], po[:, 0:256].rearrange("p (h i) -> p h i", h=4)[:, :, 0:c],
                             OIN[:, :, 0:c], ALU.add)
                    else:
                        mk_ = ga.mark()
                        ga.reset(msamp)
                        kbase = len(L["chunks"]) - 1
                        Ss = ga.alloc([16, 128], F32)
                        Ssb = ga.alloc([16, 128], BF16)
                        Sn = ga.alloc([16, 128], F32)
                        BKm = ga.alloc([16, 128], BF16)
                        for h in range(4):
                            S.dma(Ss[:, :, :], sdelta_d[l][:, h].rearrange("s k v -> k s v"))
                            S.copy(Ssb[:, :, :], Ss[:, :, :], e="pool")
                            ppq = ps()
                            for s in range(16):
                                for a_ in range(2):
                                    S.mm(ppq[:, a_ * 64 + 4 * s:a_ * 64 + 4 * s + 4], Ssb[:, s, :],
                                         KQG[:, h, a_, c0 + 4 * s:c0 + 4 * s + 4])
                            v = ppq[:, 0:128].rearrange("p (a b) -> p a b", a=2)
                            S.tt(RT[:, h, :], VT[:, h, c0:c0 + 64], v[:, 0, :], ALU.subtract)
                            evac(OIN[:, h, :], v[:, 1, :])
                            prt = ps()
                            prtb = prt[:, :].bitcast(BF16)
                            S.transpose(prtb[0:64, 0:128], RT[:, h, :], ident_b[:, :])
                            evac(Rtok[0:64, h, :], prtb[0:64, 0:128])
                            pw = ps()
                            S.mm(pw[0:64, 0:128], TTb[0:64, h, :], Rtok[0:64, h, :])
                            evac(Wb[0:64, h, :], pw[0:64, 0:128])
                            po = ps()
                            S.mm(po[:, 0:64], Wb[0:64, h, :], AT[0:64, h, :])
                            S.tt(OT[:, h, c0:c0 + 64], po[:, 0:64], OIN[:, h, :], ALU.add)
                            S.tt(BKm[0:64, :, :], BKtok[0:64, h, :].unsqueeze(1).to_broadcast([64, 16, 128]),
                                 seqmask_b[0:64, :].unsqueeze(2).to_broadcast([64, 16, 128]), ALU.mult, e="pool")
                            S.tt(Sn[:, :, :], Ss[:, :, :],
                                 decS[:, h, kbase:kbase + 16].unsqueeze(2).to_broadcast([128, 16, 128]), ALU.mult, e="pool")
                            for q4 in range(4):
                                pS = ps()
                                for s4 in range(4):
                                    s = q4 * 4 + s4
                                    S.mm(pS[:, s4 * 128:(s4 + 1) * 128], BKm[0:64, s, :], Wb[0:64, h, :])
                                S.tt(Sn[:, q4 * 4:q4 * 4 + 4, :], Sn[:, q4 * 4:q4 * 4 + 4, :],
                                     pS[:, :].rearrange("p (s d) -> p s d", s=4), ALU.add)
                            S.dma(sdelta_o[l][:, h].rearrange("s k v -> k s v"), Sn[:, :, :], is_out=True)
                        ga.reset(mk_)

                def step(g):
                    if g is None:
                        return None
                    try:
                        next(g)
                        return g
                    except StopIteration:
                        return None

                g = prep_gen(0)
                while g is not None:
                    g = step(g)
                for ci in range(nchk):
                    gp = prep_gen(ci + 1) if ci + 1 < nchk else None
                    gc = chain_gen(ci)
                    while gp is not None or gc is not None:
                        for _ in range(3):
                            gp = step(gp)
                        gc = step(gc)
                ga.reset(mA3)
                if last:
                    for h in range(4):
                        S.dma(pdelta_d[l][h], Sd[:, h, :], is_out=True)
                CP(9)
                for h in range(4):
                    gate_mix(h, OT[:, h, :], pcols[:, P_NA + l:P_NA + l + 1], True)
                flush()
                ga.reset(mA)

                CP(10)
                mB = ga.mark()
                XB = ga.alloc([TBL], F32)
                XBb = ga.alloc([TBL], BF16)
                Rg = ga.alloc([TBL], F32)
                Ig = ga.alloc([TBL], F32)
                Hh = ga.alloc([TBL], F32)
                prew = ga.alloc([3 + max(npc, 1)], F32)
                presw = ga.alloc([16, 7], F32)
                cstb = ga.alloc([4, 48], F32)
                h0 = ga.alloc([4, 16], F32)
                hs_out = ga.alloc([4, 16], F32)
                tmp16 = ga.alloc([16], F32)
                if last:
                    stg = sstg[0][0:48, 0:4, :]
                    S.dma(stg, slconv_d[l].rearrange("r (a p) -> r a p", p=128))
                    pb = ps()
                    for a in range(4):
                        S.transpose(pb[:, a * 48:(a + 1) * 48], sstg[0][0:48, a, :], ident_f[0:48, 0:48], inc=(a == 3))
                    evac(cstb[:, :, :], pb[:, 0:192].rearrange("p (a b) -> p a b", a=4))
                    stg = sstg[1][0:16, 0:4, :]
                    S.dma(stg, slru_d[l].rearrange("s (a p) -> s a p", p=128))
                    pb = ps()
                    for a in range(4):
                        S.transpose(pb[:, a * 16:(a + 1) * 16], sstg[1][0:16, a, :], ident_f[0:16, 0:16], inc=(a == 3))
                    evac(h0[:, :, :], pb[:, 0:64].rearrange("p (a b) -> p a b", a=4))
                for n_ in range(4):
                    wb = load_w(wl, C_XB + n_ * 128, 128, nwcol)
                    S.copy(prew[:, 0:3], carryB[:, n_, :], e="pool")
                    if last:
                        S.copy(presw[:, :, 0:3], cstb[:, n_, :].rearrange("p (s r) -> p s r", r=3), e="pool")

                    def cons(pb, t0, n, kind):
                        if kind == "p":
                            evac(prew[:, 3 + t0:3 + t0 + n], pb[:, 0:n])
                        else:
                            evac(presw[:, :, 3:7], pb[:, 0:64].rearrange("p (s t) -> p s t", t=4))

                    def post(n_=n_):
                        wcolk = [pcols[:, P_CONVB + (l * 4 + k) * 4 + n_: P_CONVB + (l * 4 + k) * 4 + n_ + 1] for k in range(4)]
                        bcol = pcols[:, P_CONVBB + l * 4 + n_: P_CONVBB + l * 4 + n_ + 1]
                        if npc > 0:
                            S.act(XB[:, 0:npc], prew[:, 3:3 + npc], AF.Identity, scale=wcolk[3], bias=bcol)
                            for k in (2, 1, 0):
                                S.stt(XB[:, 0:npc], prew[:, k:k + npc], wcolk[k], XB[:, 0:npc], ALU.mult, ALU.add)
                            S.copy(carryB[:, n_, :], prew[:, npc:npc + 3], e="pool")
                        if last:
                            a3 = bview(XB[:, npc:npc + 64], 16, 4)
                            S.act(a3, presw[:, :, 3:7], AF.Identity, scale=wcolk[3], bias=bcol)
                            for k in (2, 1, 0):
                                S.stt(a3, presw[:, :, k:k + 4], wcolk[k], a3, ALU.mult, ALU.add)
                            S.copy(cstb[:, n_, :].rearrange("p (s r) -> p s r", r=3), presw[:, :, 4:7], e="pool")
                        S.copy(XBb[:, 0:TBL], XB[:, 0:TBL], e="act")
                        for (t0, n, kind) in ttiles(L):
                            pr = ps()
                            S.mm(pr[:, 0:n], lwa[:, n_, :], XBb[:, t0:t0 + n])
                            S.act(Rg[:, t0:t0 + n], pr[:, 0:n], AF.Sigmoid,
                                  bias=pcols[:, P_LBA + l * 4 + n_:P_LBA + l * 4 + n_ + 1])
                            pi = ps()
                            S.mm(pi[:, 0:n], lwx[:, n_, :], XBb[:, t0:t0 + n])
                            S.act(Ig[:, t0:t0 + n], pi[:, 0:n], AF.Sigmoid,
                                  bias=pcols[:, P_LBX + l * 4 + n_:P_LBX + l * 4 + n_ + 1])
                        S.act(Rg[:, 0:TBL], Rg[:, 0:TBL], AF.Exp, scale=pcols[:, P_NSP8 + l * 4 + n_:P_NSP8 + l * 4 + n_ + 1])
                        S.tt(Hh[:, 0:TBL], Rg[:, 0:TBL], Rg[:, 0:TBL], ALU.mult)
                        S.ts(Hh[:, 0:TBL], Hh[:, 0:TBL], -1.0, 1.0, op0=ALU.mult, op1=ALU.add, e="pool")
                        S.act(Hh[:, 0:TBL], Hh[:, 0:TBL], AF.Sqrt)
                        S.tt(Ig[:, 0:TBL], Ig[:, 0:TBL], Hh[:, 0:TBL], ALU.mult)
                        S.tt(Ig[:, 0:TBL], Ig[:, 0:TBL], XB[:, 0:TBL], ALU.mult, e="pool")
                        if last:
                            A3 = bview(Rg[:, npc:npc + 64], 16, 4)
                            B3 = bview(Ig[:, npc:npc + 64], 16, 4)
                            S.tt(tmp16[:, :], A3[:, :, 0], h0[:, n_, :], ALU.mult)
                            S.tt(B3[:, :, 0], B3[:, :, 0], tmp16[:, :], ALU.add)
                            S.memset(A3[:, :, 0], 0.0)
                        if npc > 0:
                            S.scan(Hh[:, 0:npc], Rg[:, 0:npc], Ig[:, 0:npc], hl[:, n_:n_ + 1])
                            S.copy(hl[:, n_:n_ + 1], Hh[:, npc - 1:npc], e="pool")
                        if last:
                            S.scan(Hh[:, npc:npc + 64], Rg[:, npc:npc + 64], Ig[:, npc:npc + 64], 0.0)
                            S.copy(hs_out[:, n_, :], bview(Hh[:, npc:npc + 64], 16, 4)[:, :, 3], e="pool")

                    project(wb, 128, L, cons, post)
                    gate_mix(4 + n_, Hh, None, False)
                flush()
                if last:
                    pb = ps()
                    S.transpose(pb[0:4, 0:128], hl[:, :], ident_f[:, :])
                    t4 = ga.alloc([128], F32)
                    evac(t4[0:4, :], pb[0:4, 0:128])
                    S.dma(plru_d[l], t4[0:4, :], is_out=True)
                    pb = ps()
                    for a in range(4):
                        S.transpose(pb[0:16, a * 128:(a + 1) * 128], hs_out[:, a, :], ident_f[:, :], inc=(a == 3))
                    t5 = ga.alloc([512], F32)
                    evac(t5[0:16, :], pb[0:16, :])
                    S.dma(slru_o[l], t5[0:16, :], is_out=True)
                    pb = ps()
                    for a in range(4):
                        S.transpose(pb[0:48, a * 128:(a + 1) * 128], cstb[:, a, :], ident_f[:, :], inc=(a == 3))
                    t6 = ga.alloc([512], F32)
                    evac(t6[0:48, :], pb[0:48, :])
                    S.dma(slconv_o[l], t6[0:48, :], is_out=True)
                    pb = ps()
                    S.transpose(pb[0:12, 0:128], carryB[:, :, :].rearrange("p a r -> p (a r)"), ident_f[:, :])
                    t7 = ga.alloc([128], F32)
                    evac(t7[0:12, :], pb[0:12, 0:128])
                    for a in range(4):
                        S.dma(plconv_d[l][:, a * 128:(a + 1) * 128], t7[a * 3:a * 3 + 3, :], is_out=True)
                ga.reset(mB)

                CP(11)
                for grp in ("C", "D"):
                    mC = ga.mark()
                    QA = ga.alloc([2, TBL], BF16)
                    QS_ = ga.alloc([2, TBL], BF16)
                    KA = ga.alloc([2, TBL], BF16)
                    KS_ = ga.alloc([2, TBL], BF16)
                    VT2 = ga.alloc([4, TBL], BF16)
                    OT2 = ga.alloc([4, TBL], BF16)
                    nlast = len(chunk_last_cols(L))
                    decC = ga.alloc([2, nlast], F32)
                    Sx, Sxb = (Sg, Sgb) if grp == "C" else (Sr, Srb)
                    sst_d, sst_o, pst_d = (sgla_d, sgla_o, pgla_d) if grp == "C" else (sret_d, sret_o, pret_d)
                    cq, ck, cv = (C_QC, C_KC, C_VC) if grp == "C" else (C_QD, C_KD, C_VD)
                    mC2 = ga.mark()
                    if grp == "C":
                        RCT = ga.alloc([TBL], BF16)
                        LT = ga.alloc([TBL], F32)
                        CS = ga.alloc([2, TBL], F32)
                        CSL = ga.alloc([2, TBL], F32)
                        EB = ga.alloc([2, TBL], F32)
                        EBN = ga.alloc([2, TBL], F32)
                        EKS = ga.alloc([2, TBL], F32)
                        CLc = ga.alloc([2, nlast], F32)
                        wb = load_w(wl, C_RC, 16, nwcol)

                        def cons(pb, t0, n, kind):
                            evac(RCT[0:16, t0:t0 + n], pb[0:16, 0:n])

                        project(wb, 16, L, cons)
                        flush()
                        for t in range(2):
                            for (t0, n, kind) in ttiles(L):
                                pz = ps()
                                S.mm(pz[:, 0:n], w2b[0:16, t * 128:(t + 1) * 128], RCT[0:16, t0:t0 + n])
                                S.act(LT[:, t0:t0 + n], pz[:, 0:n], AF.Exp, scale=-1.0,
                                      bias=pcols[:, P_NB2 + l * 2 + t:P_NB2 + l * 2 + t + 1])
                            S.act(LT[:, 0:TBL], LT[:, 0:TBL], AF.Ln, bias=1.0)
                            S.scan(CS[:, t, :], reset_t[:, 0:TBL], LT[:, 0:TBL], 0.0)
                            chunk_last(CSL[:, t, :], CS[:, t, :], L)
                            compact_last(CLc[:, t, :], CS[:, t, :], L)
                        S.act(EB[:, :, :], CS[:, :, :], AF.Exp, scale=-1.0 / 16)
                        S.act(EBN[:, :, :], CS[:, :, :], AF.Exp, scale=1.0 / 16)
                        S.tt(EKS[:, :, :], CS[:, :, :], CSL[:, :, :], ALU.subtract)
                        S.act(EKS[:, :, :], EKS[:, :, :], AF.Exp, scale=1.0 / 16)
                        S.act(decC[:, :, :], CLc[:, :, :], AF.Exp, scale=-1.0 / 16)
                        for t in range(2):
                            wb = load_w(wl, cq + t * 128, 128, nwcol)

                            def cons(pb, t0, n, kind, t=t):
                                S.stt(QA[:, t, t0:t0 + n], pb[:, 0:n], 0.125, EB[:, t, t0:t0 + n], ALU.mult, ALU.mult)

                            project(wb, 128, L, cons)
                            wb = load_w(wl, ck + t * 128, 128, nwcol)

                            def cons(pb, t0, n, kind, t=t):
                                S.tt(KA[:, t, t0:t0 + n], pb[:, 0:n], EBN[:, t, t0:t0 + n], ALU.mult)
                                S.tt(KS_[:, t, t0:t0 + n], pb[:, 0:n], EKS[:, t, t0:t0 + n], ALU.mult)

                            project(wb, 128, L, cons)
                        QSt = QA
                    else:
                        QRf = ga.alloc([TBL], F32)
                        T1 = ga.alloc([512], F32)
                        T2 = ga.alloc([512], F32)
                        Qb = ga.alloc([512], BF16)
                        for which in range(2):
                            for t in range(2):
                                wb = load_w(wl, (cq if which == 0 else ck) + t * 128, 128, nwcol)

                                def cons(pb, t0, n, kind):
                                    S.copy(Qb[:, 0:n], pb[:, 0:n], e="act")
                                    pm = ps()
                                    S.mm(pm[:, 0:n], perm_b[:, :], Qb[:, 0:n])
                                    S.tt(T1[:, 0:n], pb[:, 0:n], cos_t[:, t0:t0 + n], ALU.mult)
                                    S.tt(T2[:, 0:n], pm[:, 0:n], sin_t[:, t0:t0 + n], ALU.mult)
                                    S.tt(QRf[:, t0:t0 + n], T1[:, 0:n], T2[:, 0:n], ALU.add, e="pool")

                                def post(which=which, t=t):
                                    dA = QA if which == 0 else KA
                                    dS = QS_ if which == 0 else KS_
                                    evac(dA[:, t, :], QRf[:, 0:TBL])
                                    c0 = 0
                                    if L["chunks"][0][1] == 16:
                                        tb = cm["fs64"][:, t, 0:16] if which == 0 else cm["ts16"][:, t, :]
                                        S.tt(dS[:, t, 0:16], QRf[:, 0:16], tb, ALU.mult)
                                        c0 = 16
                                    n64 = (npc - c0) // 64
                                    if n64 > 0:
                                        tb = cm["fs64"][:, t, :] if which == 0 else cm["ts64"][:, t, :]
                                        S.tt(bview(dS[:, t, c0:npc], n64, 64), bview(QRf[:, c0:npc], n64, 64),
                                             tb.unsqueeze(1).to_broadcast([128, n64, 64]), ALU.mult)
                                    if last:
                                        tb = cm["fss"][:, t, :] if which == 0 else cm["tss"][:, t, :]
                                        S.tt(dS[:, t, npc:npc + 64], QRf[:, npc:npc + 64], tb, ALU.mult)

                                project(wb, 128, L, cons, post)
                        QSt = QS_
                    for h in range(4):
                        wb = load_w(wl, cv + h * 128, 128, nwcol)

                        def cons(pb, t0, n, kind, h=h):
                            evac(VT2[:, h, t0:t0 + n], pb[:, 0:n])

                        project(wb, 128, L, cons)
                    flush()
                    ga.reset(mC2)
                    QAm = ga.alloc([2, 2, TBL], BF16)
                    S.memset(QAm[:, :, :, :], 0.0)
                    for t in range(2):
                        evac(QAm[0:64, t, 0, :], QA[0:64, t, :])
                        evac(QAm[64:128, t, 1, :], QA[64:128, t, :])
                    if grp == "C":
                        QSm = QAm
                    else:
                        QSm = ga.alloc([2, 2, TBL], BF16)
                        S.memset(QSm[:, :, :, :], 0.0)
                        for t in range(2):
                            evac(QSm[0:64, t, 0, :], QS_[0:64, t, :])
                            evac(QSm[64:128, t, 1, :], QS_[64:128, t, :])
                    ATs = [ga.alloc([4, 64], BF16) for _ in range(2)]
                    Vtoks = [ga.alloc([4, 128], BF16) for _ in range(2)]
                    KStoks = [ga.alloc([2, 128], BF16) for _ in range(2)]
                    mC3 = ga.mark()

                    def stage1(ci):
                        c0, c, kind, g0 = L["chunks"][ci]
                        isS = (kind == "s")
                        AT, Vtok, KStok = ATs[ci % 2], Vtoks[ci % 2], KStoks[ci % 2]
                        pa = ps()
                        for h in range(4):
                            t, e_ = h // 2, h % 2
                            S.mm(pa[0:c, h * 64:h * 64 + c], KA[:, t, c0:c0 + c], QAm[:, t, e_, c0:c0 + c])
                        pa3 = pa[0:c, 0:256].rearrange("p (h i) -> p h i", h=4)[:, :, 0:c]
                        if grp == "C":
                            m_ = cm["incl_s"] if isS else cm["incl_p"]
                            S.tt(AT[0:c, :, 0:c], pa3, m_[0:c, 0:c].unsqueeze(1).to_broadcast([c, 4, c]), ALU.mult)
                        else:
                            m_ = cm["retm_s"] if isS else cm["retm_p"]
                            S.tt(AT[0:c, :, 0:c], pa3, m_[0:c, :, 0:c], ALU.mult)
                        pv = ps()
                        pvb = pv[:, :].bitcast(BF16)
                        for h in range(4):
                            S.transpose(pvb[0:c, h * 128:(h + 1) * 128], VT2[:, h, c0:c0 + c], ident_b[:, :], inc=(h == 3))
                        evac(Vtok[0:c, :, :], pvb[0:c, 0:512].rearrange("p (h d) -> p h d", h=4))
                        pk = ps()
                        pkb = pk[:, :].bitcast(BF16)
                        for t in range(2):
                            S.transpose(pkb[0:c, t * 128:(t + 1) * 128], KS_[:, t, c0:c0 + c], ident_b[:, :], inc=(t == 1))
                        evac(KStok[0:c, :, :], pkb[0:c, 0:256].rearrange("p (t d) -> p t d", t=2))

                    def stage2(ci):
                        c0, c, kind, g0 = L["chunks"][ci]
                        isS = (kind == "s")
                        AT, Vtok, KStok = ATs[ci % 2], Vtoks[ci % 2], KStoks[ci % 2]
                        if not isS:
                            po = ps()
                            for h in range(4):
                                t, e_ = h // 2, h % 2
                                S.mm(po[:, h * 64:h * 64 + c], Sxb[:, t, :], QSm[:, t, e_, c0:c0 + c], start=True, stop=False)
                                S.mm(po[:, h * 64:h * 64 + c], Vtok[0:c, h, :], AT[0:c, h, 0:c], start=False, stop=True)
                            evac(OT2[:, :, c0:c0 + c], po[:, 0:256].rearrange("p (h i) -> p h i", h=4)[:, :, 0:c])
                            for e_ in range(2):
                                hp = 64 * e_
                                pS = ps()
                                for t in range(2):
                                    S.mm(pS[hp:hp + 64, t * 128:(t + 1) * 128], KStok[0:c, t, hp:hp + 64],
                                         Vtok[0:c, 2 * t + e_, :])
                                for t in range(2):
                                    if grp == "C":
                                        dcol = decC[hp:hp + 64, t, ci:ci + 1]
                                    else:
                                        dcol = cm["retdec"][hp:hp + 64, t, (0 if c == 64 else 1):(1 if c == 64 else 2)]
                                    S.stt(Sx[hp:hp + 64, t, :], Sx[hp:hp + 64, t, :], dcol,
                                          pS[hp:hp + 64, t * 128:(t + 1) * 128], ALU.mult, ALU.add)
                            S.copy(Sxb[:, :, :], Sx[:, :, :], e="act")
                        else:
                            ga.reset(mC3)
                            kbase = len(L["chunks"]) - 1
                            Ss = ga.alloc([16, 2, 128], F32)
                            Ssb = ga.alloc([16, 2, 128], BF16)
                            Sn = ga.alloc([16, 2, 128], F32)
                            KSm = ga.alloc([16, 2, 128], BF16)
                            for hp_ in range(2):
                                S.dma(Ss[hp_ * 64:(hp_ + 1) * 64, :, :, :],
                                      sst_d[l].rearrange("s (t e) k v -> e k s t v", e=2)[hp_])
                            S.copy(Ssb[:, :, :, :], Ss[:, :, :, :], e="pool")
                            S.tt(KSm[0:64, :, :, :], KStok[0:64, :, :].unsqueeze(1).to_broadcast([64, 16, 2, 128]),
                                 seqmask_b[0:64, :].unsqueeze(2).unsqueeze(3).to_broadcast([64, 16, 2, 128]), ALU.mult,
                                 e="pool")
                            if grp == "C":
                                S.tt(Sn[:, :, :, :], Ss[:, :, :, :],
                                     decC[:, :, kbase:kbase + 16].rearrange("p t s -> p s t").unsqueeze(3)
                                     .to_broadcast([128, 16, 2, 128]), ALU.mult, e="pool")
                            else:
                                S.tt(Sn[:, :, :, :], Ss[:, :, :, :],
                                     cm["retdec"][:, :, 2:3].unsqueeze(1).to_broadcast([128, 16, 2, 128]), ALU.mult,
                                     e="pool")
                            po = ps()
                            for h in range(4):
                                t, e_ = h // 2, h % 2
                                S.mm(po[:, h * 64:h * 64 + 64], Vtok[0:64, h, :], AT[0:64, h, :], start=True, stop=False)
                                for s_i in range(16):
                                    S.mm(po[:, h * 64 + 4 * s_i:h * 64 + 4 * s_i + 4], Ssb[:, s_i, t, :],
                                         QSm[:, t, e_, c0 + 4 * s_i:c0 + 4 * s_i + 4], start=False, stop=(s_i == 15))
                            evac(OT2[:, :, c0:c0 + 64], po[:, 0:256].rearrange("p (h i) -> p h i", h=4))
                            for e_ in range(2):
                                hp = 64 * e_
                                for s2 in range(8):
                                    pS = ps()
                                    for s_ in range(2):
                                        s_i = s2 * 2 + s_
                                        for t in range(2):
                                            S.mm(pS[hp:hp + 64, (s_ * 2 + t) * 128:(s_ * 2 + t + 1) * 128],
                                                 KSm[0:64, s_i, t, hp:hp + 64], Vtok[0:64, 2 * t + e_, :])
                                    S.tt(Sn[hp:hp + 64, s2 * 2:s2 * 2 + 2, :, :], Sn[hp:hp + 64, s2 * 2:s2 * 2 + 2, :, :],
                                         pS[hp:hp + 64, :].rearrange("p (s t d) -> p s t d", s=2, t=2), ALU.add)
                            for hp_ in range(2):
                                S.dma(sst_o[l].rearrange("s (t e) k v -> e k s t v", e=2)[hp_],
                                      Sn[hp_ * 64:(hp_ + 1) * 64, :, :, :], is_out=True)

                    nchk = len(L["chunks"])
                    stage1(0)
                    for ci in range(nchk):
                        if ci + 1 < nchk:
                            stage1(ci + 1)
                        stage2(ci)
                    ga.reset(mC3)
                    if last:
                        for h in range(4):
                            t, hp = h // 2, (h % 2) * 64
                            S.dma(pst_d[l][h], Sx[hp:hp + 64, t, :], is_out=True)
                    for h in range(4):
                        mt = (8 if grp == "C" else 12) + h
                        gate_mix(mt, OT2[:, h, :], pcols[:, P_NC + l:P_NC + l + 1] if grp == "C" else None, True)
                    flush()
                    ga.reset(mC)

                CP(12)
                flush()
                mO = ga.mark()
                xo2 = [ga.alloc([256], F32) for _ in range(4)]
                for cb in range(8):
                    e0 = w_get("out", l, cb * 256, 128)
                    w_get("out", l, cb * 256 + 128, 128)
                    wo = wo2[e0[4]]
                    for ti, (r0, n, c0) in enumerate(rows):
                        xo = xo2[(cb * len(rows) + ti) % 4]
                        S.dma(xo[0:n, :], xres[r0:r0 + n, cb * 256:(cb + 1) * 256])
                        pb = ps()
                        for kt in range(KT):
                            S.mm(pb[0:n, 0:256], mixT[:, kt, c0:c0 + n], wo[:, kt, :], start=(kt == 0), stop=(kt == KT - 1))
                        S.tt(xo[0:n, :], xo[0:n, :], pb[0:n, 0:256], ALU.add)
                        S.dma(xres[r0:r0 + n, cb * 256:(cb + 1) * 256], xo[0:n, :])
                ga.reset(mO)
                ga.reset(gm0)

        CP(13)
        fnb = ga.alloc([D], F32)
        S.dma(fnb[:, :], fnorm_d.partition_broadcast(128))
        xt2 = [ga.alloc([D], F32) for _ in range(2)]
        junk = ga.alloc([D], BF16)
        st = ga.alloc([8], F32)
        rows = []
        r = 0
        while r < TT:
            n = min(128, (TP if r < TP else TT) - r)
            rows.append((r, n))
            r += n
        for ti, (r0, n) in enumerate(rows):
            xt = xt2[ti % 2]
            S.dma(xt[0:n, :], xres[r0:r0 + n, :])
            S.act(junk[0:n, :], xt[0:n, :], AF.Square, accum_out=st[0:n, 0:1])
            S.act(st[0:n, 1:2], st[0:n, 0:1], AF.Sqrt, scale=1.0 / D, bias=EPS)
            S.recip(st[0:n, 2:3], st[0:n, 1:2])
            S.act(xt[0:n, :], xt[0:n, :], AF.Identity, scale=st[0:n, 2:3])
            S.tt(xt[0:n, :], xt[0:n, :], fnb[0:n, :], ALU.mult)
            if r0 >= TP:
                S.dma(ys_d[r0 - TP:r0 - TP + n, :], xt[0:n, :], is_out=True)
            else:
                a = max(r0, 16)
                if a < r0 + n:
                    S.dma(yp_d[a - 16:r0 + n - 16, :], xt[a - r0:n, :], is_out=True)

    except _Stop:
        pass
    S.finish()
    return nc, hc


_CACHE = {}


def kernel(**inputs):
    cfg = Cfg(nch=32, depth=4, nsb=4, nseq=16)
    if "nc" not in _CACHE:
        _CACHE["nc"] = build(cfg)
    nc, hc = _CACHE["nc"]
    f = np.float32
    g = lambda k: np.ascontiguousarray(np.asarray(inputs[k], dtype=f))
    shared = {k: g(k) for k in ("meta_tokens", "norm_w", "w_in", "conv_a", "a_log", "dt_bias", "norm_a", "conv_b",
                                "conv_b_bias", "lru_wa", "lru_ba", "lru_wx", "lru_bx", "lru_lambda", "gla_w2",
                                "gla_b2", "norm_c", "w_out", "final_norm")}
    for k, v in hc.items():
        shared["c_" + k] = v
    xp, xs = g("x_prompt"), g("x_sample")
    sd, sdc, sl, slc, sg_, sr_ = (g("state_delta"), g("state_delta_conv"), g("state_lru"), g("state_lru_conv"),
                                  g("state_gla"), g("state_ret"))
    in_maps = []
    for i in range(8):
        b = i % 4
        sl_ = slice(16 * i, 16 * i + 16)
        m = dict(shared)
        m["xp"] = xp[b]
        m["xs"] = np.ascontiguousarray(xs[sl_].reshape(64, D))
        m["sdelta"] = np.ascontiguousarray(sd[:, sl_])
        m["sdconv"] = np.ascontiguousarray(sdc[:, sl_].reshape(4, 48, 1536))
        m["slru"] = np.ascontiguousarray(sl[:, sl_])
        m["slconv"] = np.ascontiguousarray(slc[:, sl_].reshape(4, 48, 512))
        m["sgla"] = np.ascontiguousarray(sg_[:, sl_])
        m["sret"] = np.ascontiguousarray(sr_[:, sl_])
        in_maps.append(m)
    res = run_bass_kernel_spmd(nc, in_maps, core_ids=list(range(8))).results
    cat = lambda k, ax: np.concatenate([res[i][k] for i in range(8)], axis=ax)
    stk = lambda k: np.stack([res[i][k] for i in range(4)], axis=1)
    y_p = np.stack([res[i]["y_p"] for i in range(4)], axis=0)
    y_s = cat("y_s", 0).reshape(128, 4, D)
    outs = (y_p, y_s, stk("p_delta"), stk("p_dconv"), stk("p_lru").reshape(4, 4, 512), stk("p_lconv"),
            stk("p_gla"), stk("p_ret"),
            cat("s_delta", 1), cat("s_dconv", 1).reshape(4, 128, 3, 1536), cat("s_lru", 1),
            cat("s_lconv", 1).reshape(4, 128, 3, 512), cat("s_gla", 1), cat("s_ret", 1))
    return tuple(np.ascontiguousarray(o.astype(f)) for o in outs)
```

```python
import math
import numpy as np
import ml_dtypes
import concourse.bass as bass
import concourse.mybir as mybir
from concourse.bass_utils import run_bass_kernel_spmd

F32 = mybir.dt.float32
BF16 = mybir.dt.bfloat16
ALU = mybir.AluOpType
AF = mybir.ActivationFunctionType
ESZ = {F32: 4, BF16: 2}

D = 2048
KT = 16
INW = 6168
EPS = 1e-6
NEG = -30000.0
import os as _os
LOOKD = int(_os.environ.get("LOOKD", "2"))
LOOKC = int(_os.environ.get("LOOKC", "1"))
C_QKV, C_AL, C_XB, C_QC, C_KC, C_VC, C_RC, C_QD, C_KD, C_VD, C_GATE = 0, 1536, 1544, 2056, 2312, 2568, 3080, 3096, 3352, 3608, 4120


def _rng(ap):
    t = ap.tensor
    dims = list(ap.ap)
    off = int(ap.offset)
    es = ESZ.get(ap.dtype, 4)
    if type(t).__name__.startswith("DRam"):
        ext = 0
        for s, n in dims:
            ext += (int(n) - 1) * abs(int(s))
        return (t.name, off * es, (off + ext + 1) * es)
    if type(t).__name__.startswith("PSum"):
        return (t.name, 0, 1 << 30)
    pstep = int(dims[0][0])
    if pstep <= 0:
        pstep = 1 << 40
    lo = off % pstep
    ext = 0
    for s, n in dims[1:]:
        ext += (int(n) - 1) * abs(int(s))
    return (t.name, lo * es, (lo + ext + 1) * es)


class Sch:
    NDMA = 24

    def __init__(self, nc):
        self.nc = nc
        self.eng = {"pe": nc.tensor, "dve": nc.vector, "act": nc.scalar, "pool": nc.gpsimd, "sp": nc.sync}
        self.sem = {k: nc.alloc_semaphore("sem_" + k) for k in self.eng}
        self.cnt = {k: 0 for k in self.eng}
        self.seen = {k: {} for k in self.eng}
        self.dsem = [nc.alloc_semaphore("dsem%d" % i) for i in range(self.NDMA)]
        self.dcnt = [0] * self.NDMA
        self.drr = 0
        self.tr = {}
        self.nins = 0
        self.out_dmas = {}

    def _deps(self, reads, writes, e=None):
        deps = {}
        for ap in reads:
            name, lo, hi = _rng(ap)
            t = self.tr.get(name)
            if t is None:
                continue
            for (l, h, src, val) in t["w"]:
                if l < hi and lo < h and deps.get(src, 0) < val:
                    deps[src] = val
            if type(ap.tensor).__name__.startswith("PSum"):
                for (l, h, src, val) in t["r"]:
                    if src != e and deps.get(src, 0) < val:
                        deps[src] = val
        for ap in writes:
            name, lo, hi = _rng(ap)
            t = self.tr.get(name)
            if t is None:
                continue
            for (l, h, src, val) in t["w"]:
                if l < hi and lo < h and deps.get(src, 0) < val:
                    deps[src] = val
            for (l, h, src, val) in t["r"]:
                if l < hi and lo < h and deps.get(src, 0) < val:
                    deps[src] = val
        return deps

    def _record(self, reads, writes, src, val):
        for ap in writes:
            name, lo, hi = _rng(ap)
            t = self.tr.setdefault(name, {"w": [], "r": []})
            t["w"] = [e for e in t["w"] if not (lo <= e[0] and e[1] <= hi)]
            t["r"] = [e for e in t["r"] if not (lo <= e[0] and e[1] <= hi)]
            t["w"].append((lo, hi, src, val))
        for ap in reads:
            name, lo, hi = _rng(ap)
            t = self.tr.setdefault(name, {"w": [], "r": []})
            t["r"] = [e for e in t["r"] if not (e[2] == src and lo <= e[0] and e[1] <= hi)]
            t["r"].append((lo, hi, src, val))

    def _semof(self, src):
        return self.dsem[src[1]] if isinstance(src, tuple) else self.sem[src]

    def _wait(self, e, deps):
        for src, val in deps.items():
            if src == e and e == "pe":
                continue
            if self.seen[e].get(src, 0) >= val:
                continue
            self.eng[e].wait_ge(self._semof(src), val)
            self.seen[e][src] = val
            self.nins += 1

    def op(self, e, fn, reads, writes, inc=True):
        self._wait(e, self._deps(reads, writes, e))
        ins = fn()
        self.nins += 1
        if inc:
            self.cnt[e] += 1
            ins.then_inc(self.sem[e], 1)
            val = self.cnt[e]
        else:
            val = self.cnt[e] + 1
        self._record(reads, writes, e, val)
        return ins

    def dma(self, out, in_, q="sp", is_out=False):
        deps = self._deps([in_], [out])
        i = self.drr
        self.drr = (self.drr + 1) % self.NDMA
        if self.dcnt[i] > 0:
            deps[("dma", i)] = max(deps.get(("dma", i), 0), 16 * self.dcnt[i])
        self._wait(q, deps)
        ins = self.eng[q].dma_start(out=out, in_=in_, allow_slow_non_contiguous=True)
        self.nins += 1
        self.dcnt[i] += 1
        ins.then_inc(self.dsem[i], 16)
        val = 16 * self.dcnt[i]
        self._record([in_], [out], ("dma", i), val)
        if is_out:
            self.out_dmas[("dma", i)] = val

    def finish(self, q="sp"):
        deps = dict(self.out_dmas)
        for e in self.eng:
            if e != q and self.cnt[e] > 0:
                deps[e] = self.cnt[e]
        self._wait(q, deps)

    def mm(self, out, lhsT, rhs, start=True, stop=True):
        return self.op("pe", lambda: self.nc.tensor.matmul(out, lhsT, rhs, start=start, stop=stop),
                       [lhsT, rhs], [out], inc=stop)

    def transpose(self, out, in_, ident, inc=True):
        return self.op("pe", lambda: self.nc.tensor.transpose(out, in_, ident), [in_, ident], [out], inc=inc)

    def tt(self, out, in0, in1, op, e="dve"):
        return self.op(e, lambda: self.eng[e].tensor_tensor(out=out, in0=in0, in1=in1, op=op), [in0, in1], [out])

    def ts(self, out, in0, s1, s2=None, op0=ALU.mult, op1=None, e="dve"):
        rd = [in0] + [s for s in (s1, s2) if not isinstance(s, (int, float, type(None)))]
        if op1 is None:
            return self.op(e, lambda: self.eng[e].tensor_scalar(out=out, in0=in0, scalar1=s1, scalar2=None, op0=op0),
                           rd, [out])
        return self.op(e, lambda: self.eng[e].tensor_scalar(out=out, in0=in0, scalar1=s1, scalar2=s2, op0=op0,
                                                            op1=op1), rd, [out])

    def stt(self, out, in0, scalar, in1, op0, op1):
        rd = [in0, in1] + ([] if isinstance(scalar, (int, float)) else [scalar])
        return self.op("dve", lambda: self.nc.vector.scalar_tensor_tensor(out=out, in0=in0, scalar=scalar, in1=in1,
                                                                           op0=op0, op1=op1), rd, [out])

    def act(self, out, in_, func, bias=None, scale=None, accum_out=None):
        rd = [in_]
        wr = [out]
        kw = {}
        if bias is not None:
            kw["bias"] = bias
            if not isinstance(bias, (int, float)):
                rd.append(bias)
        if scale is not None:
            kw["scale"] = scale
            if not isinstance(scale, (int, float)):
                rd.append(scale)
        if accum_out is not None:
            kw["accum_out"] = accum_out
            wr.append(accum_out)
        return self.op("act", lambda: self.nc.scalar.activation(out=out, in_=in_, func=func, **kw), rd, wr)

    def copy(self, out, in_, e="dve"):
        if e == "act":
            return self.act(out, in_, AF.Copy)
        return self.op(e, lambda: self.eng[e].tensor_copy(out=out, in_=in_), [in_], [out])

    def scan(self, out, d0, d1, initial):
        rd = [d0, d1] + ([] if isinstance(initial, (int, float)) else [initial])
        return self.op("dve", lambda: self.nc.vector.tensor_tensor_scan(out=out, data0=d0, data1=d1, initial=initial,
                                                                         op0=ALU.mult, op1=ALU.add), rd, [out])

    def memset(self, ap, val, e="pool"):
        return self.op(e, lambda: self.eng[e].memset(ap, val), [], [ap])

    def recip(self, out, in_, fast=False):
        return self.op("dve", lambda: self.nc.vector.reciprocal(out=out, in_=in_), [in_], [out])


class Arena:
    def __init__(self, nc, name, nbytes):
        self.words = nbytes // 4
        self.t = nc.alloc_sbuf_tensor(name, [128, self.words], F32)
        self.off = 0
        self.peak = 0

    def alloc(self, shape, dtype, parts=128):
        n = 1
        for s in shape:
            n *= s
        nb = (n * ESZ[dtype] + 31) // 32 * 32
        w0 = self.off // 4
        self.off += nb
        self.peak = max(self.peak, self.off)
        assert self.off // 4 <= self.words, "arena overflow %s %d > %d" % (self.t.name, self.off, self.words * 4)
        v = self.t[0:parts, w0:w0 + nb // 4]
        if dtype != F32:
            v = v.bitcast(dtype)
        v = v[:, 0:n]
        if len(shape) == 2:
            v = v.rearrange("p (a b) -> p a b", a=shape[0])
        elif len(shape) == 3:
            v = v.rearrange("p (a b c) -> p a b c", a=shape[0], b=shape[1])
        elif len(shape) == 4:
            v = v.rearrange("p (a b c d) -> p a b c d", a=shape[0], b=shape[1], c=shape[2])
        return v

    def mark(self):
        return self.off

    def reset(self, m):
        self.off = m


class _Stop(Exception):
    pass


class Cfg:
    def __init__(self, nch=32, depth=4, nsb=4, nseq=16):
        self.nch, self.depth, self.nsb, self.nseq = nch, depth, nsb, nseq
        self.stop = 0


def host_consts(cfg):
    nch, nsb = cfg.nch, cfg.nsb
    f = np.float32
    c = {}
    c["ident"] = np.eye(128, dtype=f)
    j = np.arange(64)[:, None]
    i = np.arange(64)[None, :]
    same = (j // 4) == (i // 4)
    c["negmask_p"] = np.where(j <= i, 0.0, NEG).astype(f)
    c["negmask_s"] = np.where((j <= i) & same, 0.0, NEG).astype(f)
    c["strict_p"] = np.where(j < i, -1.0, 0.0).astype(f)
    c["strict_s"] = np.where((j < i) & same, -1.0, 0.0).astype(f)
    c["incl_p"] = np.where(j <= i, 1.0, 0.0).astype(f)
    c["incl_s"] = np.where((j <= i) & same, 1.0, 0.0).astype(f)
    lg = np.log(1.0 - 2.0 ** (-5.0 - np.arange(4, dtype=np.float64)))
    rp = np.zeros((64, 4, 64), np.float64)
    rs = np.zeros((64, 4, 64), np.float64)
    for h in range(4):
        rp[:, h, :] = np.where(j <= i, np.exp(lg[h] * np.maximum(i - j, 0)), 0.0) * 0.125
        rs[:, h, :] = np.where((j <= i) & same, np.exp(lg[h] * np.maximum(i - j, 0)), 0.0) * 0.125
    c["retm_p"] = rp.astype(f)
    c["retm_s"] = rs.astype(f)
    c["seqmask"] = (np.arange(64)[:, None] // 4 == np.arange(16)[None, :]).astype(f)
    sel = np.zeros((8, 8, 128), f)
    for k in range(8):
        sel[k, k, :] = 1.0
    c["sel"] = sel
    perm = np.zeros((128, 128), f)
    for m in range(128):
        k = m + 32 if (m % 64) < 32 else m - 32
        perm[k, m] = 1.0
    c["perm"] = perm
    hrow = np.arange(128) // 64
    fs64 = np.zeros((128, 2, 64), np.float64)
    ts64 = np.zeros((128, 2, 64), np.float64)
    ts16 = np.zeros((128, 2, 16), np.float64)
    fss = np.zeros((128, 2, 64), np.float64)
    tss = np.zeros((128, 2, 64), np.float64)
    dec = np.zeros((128, 2, 3), np.float64)
    pos = np.arange(64)
    for t in range(2):
        g = lg[2 * t + hrow][:, None]
        fs64[:, t, :] = np.exp(g * (pos[None, :] + 1.0))
        ts64[:, t, :] = np.exp(g * (63.0 - pos[None, :])) * 0.125
        ts16[:, t, :] = np.exp(g * (15.0 - pos[None, :16])) * 0.125
        fss[:, t, :] = np.exp(g * ((pos[None, :] % 4) + 1.0))
        tss[:, t, :] = np.exp(g * (3.0 - (pos[None, :] % 4))) * 0.125
        dec[:, t, 0] = np.exp(g[:, 0] * 64.0)
        dec[:, t, 1] = np.exp(g[:, 0] * 16.0)
        dec[:, t, 2] = np.exp(g[:, 0] * 4.0)
    c["fs64"], c["ts64"], c["ts16"], c["fss"], c["tss"], c["retdec"] = [a.astype(f) for a in (fs64, ts64, ts16, fss, tss, dec)]
    lay = layout(cfg)
    tbm = lay["tbmax"]
    reset = np.ones((nsb, 128, tbm), f)
    cos = np.zeros((nsb, 128, tbm), f)
    sin = np.zeros((nsb, 128, tbm), f)
    half = 32
    freqs = (10000.0 ** (-np.arange(half, dtype=np.float32) / half)).astype(np.float32)
    fr = freqs[np.arange(128) % 32]
    sgn = np.where((np.arange(128) % 64) < 32, -1.0, 1.0).astype(f)
    for sb in range(nsb):
        L = lay["sb"][sb]
        posv = np.zeros(tbm, np.float32)
        for (c0, cc, kind, g0) in L["chunks"]:
            if kind == "p":
                reset[sb, :, c0] = 0.0
                posv[c0:c0 + cc] = np.arange(g0, g0 + cc)
            else:
                for s in range(16):
                    reset[sb, :, c0 + 4 * s] = 0.0
                posv[c0:c0 + cc] = 16384 + (np.arange(64) % 4)
        ang = (posv[None, :].astype(np.float32) * fr[:, None]).astype(np.float32)
        cos[sb] = np.cos(ang)
        sin[sb] = np.sin(ang) * sgn[:, None]
    c["reset"], c["cos"], c["sin"] = reset, cos, sin
    return c


def layout(cfg):
    nch, nsb = cfg.nch, cfg.nsb
    tp = 16 + 64 * nch
    per = nch // nsb
    sbs = []
    for sb in range(nsb):
        chunks = []
        col = 0
        p0 = 0 if sb == 0 else 16 + 64 * per * sb
        if sb == 0:
            chunks.append((0, 16, "p", 0))
            col = 16
        for i in range(per * sb, per * (sb + 1)):
            chunks.append((col, 64, "p", 16 + 64 * i))
            col += 64
        npc = col
        if sb == nsb - 1:
            chunks.append((col, 64, "s", tp))
            col += 64
        sbs.append({"chunks": chunks, "np": npc, "tbl": col, "p0": p0, "last": sb == nsb - 1})
    return {"sb": sbs, "tbmax": max(s["tbl"] for s in sbs), "tp": tp, "tt": tp + 64}


def build(cfg, dbg=False):
    nc = bass.Bass("TRN2", target_bir_lowering=False)
    nc.allow_non_contiguous_dma(reason="small strided parameter / state loads").__enter__()
    S = Sch(nc)
    DEPTH, NCH, NSB, NSEQ = cfg.depth, cfg.nch, cfg.nsb, cfg.nseq
    lay = layout(cfg)
    TP, TT, TBM = lay["tp"], lay["tt"], lay["tbmax"]
    SEQ = 64 * NCH

    def din(name, shape):
        return nc.dram_tensor(name, list(shape), F32, kind="ExternalInput").ap()

    def dout(name, shape):
        return nc.dram_tensor(name, list(shape), F32, kind="ExternalOutput").ap()

    xp_d = din("xp", [SEQ, D])
    xs_d = din("xs", [64, D])
    meta_d = din("meta_tokens", [16, D])
    sdelta_d = din("sdelta", [DEPTH, NSEQ, 4, 128, 128])
    sdconv_d = din("sdconv", [DEPTH, NSEQ * 3, 1536])
    slru_d = din("slru", [DEPTH, NSEQ, 512])
    slconv_d = din("slconv", [DEPTH, NSEQ * 3, 512])
    sgla_d = din("sgla", [DEPTH, NSEQ, 4, 64, 128])
    sret_d = din("sret", [DEPTH, NSEQ, 4, 64, 128])
    normw_d = din("norm_w", [DEPTH, D])
    win_d = din("w_in", [DEPTH, D, INW])
    conva_d = din("conv_a", [DEPTH, 4, 1536])
    alog_d = din("a_log", [DEPTH, 4])
    dtb_d = din("dt_bias", [DEPTH, 4])
    norma_d = din("norm_a", [DEPTH, 128])
    convb_d = din("conv_b", [DEPTH, 4, 512])
    convbb_d = din("conv_b_bias", [DEPTH, 512])
    lruwa_d = din("lru_wa", [DEPTH, 4, 128, 128])
    lruba_d = din("lru_ba", [DEPTH, 512])
    lruwx_d = din("lru_wx", [DEPTH, 4, 128, 128])
    lrubx_d = din("lru_bx", [DEPTH, 512])
    lrulam_d = din("lru_lambda", [DEPTH, 512])
    glaw2_d = din("gla_w2", [DEPTH, 16, 256])
    glab2_d = din("gla_b2", [DEPTH, 256])
    normc_d = din("norm_c", [DEPTH, 128])
    wout_d = din("w_out", [DEPTH, D, D])
    fnorm_d = din("final_norm", [D])
    hc = host_consts(cfg)
    cd = {k: din("c_" + k, v.shape) for k, v in hc.items()}

    yp_d = dout("y_p", [SEQ, D])
    ys_d = dout("y_s", [64, D])
    pdelta_d = dout("p_delta", [DEPTH, 4, 128, 128])
    pdconv_d = dout("p_dconv", [DEPTH, 3, 1536])
    plru_d = dout("p_lru", [DEPTH, 4, 128])
    plconv_d = dout("p_lconv", [DEPTH, 3, 512])
    pgla_d = dout("p_gla", [DEPTH, 4, 64, 128])
    pret_d = dout("p_ret", [DEPTH, 4, 64, 128])
    sdelta_o = dout("s_delta", [DEPTH, NSEQ, 4, 128, 128])
    sdconv_o = dout("s_dconv", [DEPTH, NSEQ * 3, 1536])
    slru_o = dout("s_lru", [DEPTH, NSEQ, 512])
    slconv_o = dout("s_lconv", [DEPTH, NSEQ * 3, 512])
    sgla_o = dout("s_gla", [DEPTH, NSEQ, 4, 64, 128])
    sret_o = dout("s_ret", [DEPTH, NSEQ, 4, 64, 128])
    xres = nc.dram_tensor("xres", [TT, D], F32, kind="Internal").ap()

    def sb(name, shape, dt=F32):
        return nc.alloc_sbuf_tensor(name, list(shape), dt)

    ident_f = sb("ident_f", [128, 128])
    ident_b = sb("ident_b", [128, 128], BF16)
    ones_b = sb("ones_b", [128, 128], BF16)
    perm_b = sb("perm_b", [128, 128], BF16)
    perm_f = sb("perm_f", [128, 128])
    cm = {}
    for k in ("negmask_p", "negmask_s", "strict_p", "strict_s", "incl_p", "incl_s"):
        cm[k] = sb("m_" + k, [128, 64])
    for k in ("retm_p", "retm_s"):
        cm[k] = sb("m_" + k, [128, 4, 64])
    cm["seqmask"] = sb("m_seqmask", [128, 16])
    seqmask_b = sb("seqmask_b", [128, 16], BF16)
    sel = sb("sel", [8, 8, 128])
    for k in ("fs64", "ts64", "fss", "tss"):
        cm[k] = sb("m_" + k, [128, 2, 64])
    cm["ts16"] = sb("m_ts16", [128, 2, 16])
    cm["retdec"] = sb("m_retdec", [128, 2, 3])
    reset_t = sb("reset_t", [128, TBM])
    cos_t = sb("cos_t", [128, TBM])
    sin_t = sb("sin_t", [128, TBM])
    NPC = 100 * DEPTH + 8 * DEPTH + 8
    pcols = sb("pcols", [128, NPC])
    prm8 = sb("prm8", [8, DEPTH, 2])
    alg8 = sb("alg8", [8, DEPTH])
    xnT = sb("xnT", [128, KT, TBM], BF16)
    mixT = sb("mixT", [128, KT, TBM], BF16)
    NST = 2
    wstg = [sb("wstg%d" % i, [128, KT, 128]) for i in range(NST)]
    NWB = 3
    wbf = [sb("wbf%d" % i, [128, KT, 128], BF16) for i in range(NWB)]
    wo2 = [sb("wo%d" % i, [128, KT, 256], BF16) for i in range(2)]
    sstg = [sb("sstg%d" % i, [128, 6, 128]) for i in range(2)]
    Sd = sb("Sd", [128, 4, 128])
    Sdb = sb("Sdb", [128, 4, 128], BF16)
    Sg = sb("Sg", [128, 2, 128])
    Sgb = sb("Sgb", [128, 2, 128], BF16)
    Sr = sb("Sr", [128, 2, 128])
    Srb = sb("Srb", [128, 2, 128], BF16)
    hl = sb("hl", [128, 4])
    carryA = sb("carryA", [128, 12, 3])
    carryB = sb("carryB", [128, 4, 3])
    lwa = sb("lwa", [128, 4, 128], BF16)
    lwx = sb("lwx", [128, 4, 128], BF16)
    w2b = sb("w2b", [16, 256], BF16)
    ga = Arena(nc, "garena", cfg.ga_bytes if hasattr(cfg, "ga_bytes") else 86 * 1024)

    pbanks = [nc.alloc_psum_tensor("pb%d" % i, [128, 512], F32) for i in range(8)]
    pstate = {"i": 0}

    def ps():
        b = pbanks[pstate["i"] % 8]
        pstate["i"] += 1
        return b

    evs = {"i": 0}

    def evac(out, in_):
        evs["i"] += 1
        S.copy(out, in_, e=("act" if evs["i"] % 2 else "dve"))

    def CP(k):
        if cfg.stop == k:
            raise _Stop()

    try:
        S.dma(ident_f[:], cd["ident"])
        S.copy(ident_b[:], ident_f[:], e="dve")
        S.memset(ones_b[:], 1.0)
        S.dma(perm_f[:], cd["perm"])
        S.copy(perm_b[:], perm_f[:], e="dve")
        for k in cm:
            if cm[k].shape[0] == 128 and cd[k].shape[0] == 64:
                S.dma(cm[k][0:64], cd[k])
                S.dma(cm[k][64:128], cd[k])
            else:
                S.dma(cm[k][:], cd[k])
        S.copy(seqmask_b[:], cm["seqmask"][:], e="dve")
        S.dma(sel[:], cd["sel"])

        CP(1)
        pc = {"n": 0}

        def load_cols(dram2d, rows):
            base = pc["n"]
            r0 = 0
            while r0 < rows:
                r = min(128, rows - r0)
                stg = sstg[0][0:r, 0, :]
                S.dma(stg, dram2d[r0:r0 + r, :])
                pb = ps()
                S.transpose(pb[:, 0:r], stg, ident_f[0:r, 0:r])
                evac(pcols[:, base + r0: base + r0 + r], pb[:, 0:r])
                r0 += r
            pc["n"] += rows
            return base

        P_NORMW = load_cols(normw_d.rearrange("l (a p) -> (l a) p", p=128), DEPTH * 16)
        P_CONVA = load_cols(conva_d.rearrange("l k (a p) -> (l k a) p", p=128), DEPTH * 48)
        P_CONVB = load_cols(convb_d.rearrange("l k (a p) -> (l k a) p", p=128), DEPTH * 16)
        P_CONVBB = load_cols(convbb_d.rearrange("l (a p) -> (l a) p", p=128), DEPTH * 4)
        P_LBA = load_cols(lruba_d.rearrange("l (a p) -> (l a) p", p=128), DEPTH * 4)
        P_LBX = load_cols(lrubx_d.rearrange("l (a p) -> (l a) p", p=128), DEPTH * 4)
        P_LAM = load_cols(lrulam_d.rearrange("l (a p) -> (l a) p", p=128), DEPTH * 4)
        P_B2 = load_cols(glab2_d.rearrange("l (a p) -> (l a) p", p=128), DEPTH * 2)
        P_NA = load_cols(norma_d, DEPTH)
        P_NC = load_cols(normc_d, DEPTH)
        P_NSP8 = pc["n"]
        pc["n"] += DEPTH * 4
        P_NB2 = pc["n"]
        pc["n"] += DEPTH * 2
        assert pc["n"] <= NPC
        lamc = pcols[:, P_LAM:P_LAM + DEPTH * 4]
        nsp = pcols[:, P_NSP8:P_NSP8 + DEPTH * 4]
        S.act(nsp, lamc, AF.Exp, scale=-1.0)
        S.act(nsp, nsp, AF.Ln, bias=1.0)
        S.ts(nsp, nsp, -8.0, op0=ALU.mult)
        S.ts(pcols[:, P_NB2:P_NB2 + DEPTH * 2], pcols[:, P_B2:P_B2 + DEPTH * 2], -1.0, op0=ALU.mult)
        S.memset(prm8[:], 0.0)
        S.memset(alg8[:], 0.0)
        S.dma(prm8[0:4, :, 0], dtb_d.rearrange("l h -> h l"))
        S.dma(alg8[0:4, :], alog_d.rearrange("l h -> h l"))
        S.act(alg8[:], alg8[:], AF.Exp)
        S.ts(prm8[:, :, 1], alg8[:], -1.0, op0=ALU.mult)

        CP(2)
        S.dma(xres[0:16, :], meta_d)
        R = 0
        while R < SEQ:
            r = min(512, SEQ - R)
            S.dma(xres[16 + R:16 + R + r, :], xp_d[R:R + r, :])
            R += r
        S.dma(xres[TP:TP + 64, :], xs_d)

        CP(3)
        WSEQ = []
        nwb_ = 0
        ncb_ = 0
        for l_ in range(DEPTH):
            for sb_ in range(NSB):
                blocks = [(C_QKV + j * 128, 128) for j in range(12)] + [(C_AL, 8)] + [(C_GATE + m * 128, 128) for m in range(4)]
                for n_ in range(4):
                    blocks += [(C_XB + n_ * 128, 128), (C_GATE + (4 + n_) * 128, 128)]
                blocks += [(C_RC, 16)]
                for t in range(2):
                    blocks += [(C_QC + t * 128, 128), (C_KC + t * 128, 128)]
                blocks += [(C_VC + h * 128, 128) for h in range(4)] + [(C_GATE + (8 + h) * 128, 128) for h in range(4)]
                blocks += [(C_QD + t * 128, 128) for t in range(2)] + [(C_KD + t * 128, 128) for t in range(2)]
                blocks += [(C_VD + h * 128, 128) for h in range(4)] + [(C_GATE + (12 + h) * 128, 128) for h in range(4)]
                for (c0_, nco_) in blocks:
                    WSEQ.append(("in", l_, c0_, nco_, nwb_ % NWB, 0))
                    nwb_ += 1
                for cb in range(8):
                    for q in range(2):
                        WSEQ.append(("out", l_, cb * 256 + q * 128, 128, ncb_ % 2, q))
                    ncb_ += 1
        wst = {"dma": 0, "cast": 0, "used": 0}

        def w_dest(k):
            kind, l_, c0_, nco_, slot, q = WSEQ[k]
            if kind == "in":
                return wbf[slot][:, :, 0:nco_]
            return wo2[slot][:, :, q * 128:(q + 1) * 128]

        def w_dma(k):
            kind, l_, c0_, nco_, slot, q = WSEQ[k]
            src = (win_d if kind == "in" else wout_d)[l_]
            S.dma(wstg[k % NST][:, :, 0:nco_], src[:, c0_:c0_ + nco_].rearrange("(kt p) c -> p kt c", p=128))

        def w_cast(k):
            nco_ = WSEQ[k][3]
            S.copy(w_dest(k), wstg[k % NST][:, :, 0:nco_], e=("act" if k % 2 else "dve"))

        def w_get(kind, l_, c0_, nco_):
            k = wst["used"]
            assert WSEQ[k][0:4] == (kind, l_, c0_, nco_), (k, WSEQ[k], kind, l_, c0_, nco_)
            while True:
                prog = False
                j = wst["dma"]
                if j < min(k + 1 + LOOKD, len(WSEQ)) and wst["cast"] > j - NST:
                    w_dma(j)
                    wst["dma"] += 1
                    prog = True
                j = wst["cast"]
                if j < min(k + 1 + LOOKC, len(WSEQ)) and j < wst["dma"]:
                    w_cast(j)
                    wst["cast"] += 1
                    prog = True
                if not prog:
                    break
            assert wst["cast"] > k
            wst["used"] += 1
            return WSEQ[k]

        def load_w(w2d, c0, ncols, scale_base=None, rows=KT):
            e_ = w_get("in", l, c0, ncols)
            return wbf[e_[4]]

        def ttiles(L):
            res = []
            t0 = 0
            while t0 < L["np"]:
                n = min(512, L["np"] - t0)
                res.append((t0, n, "p"))
                t0 += n
            if L["last"]:
                res.append((L["np"], 64, "s"))
            return res

        pend = {"f": None, "f2": None}

        def _drain_one():
            f, f2 = pend["f"], pend["f2"]
            pend["f"] = None
            pend["f2"] = None
            if f is not None:
                pend["f2"] = f()
            if f2 is not None:
                f2()

        def flush():
            _drain_one()
            _drain_one()

        def project(wb, M, L, consumer, post=None, post2=None):
            banks = []
            for (t0, n, kind) in ttiles(L):
                pb = ps()
                for kt in range(KT):
                    S.mm(pb[0:M, 0:n], wb[:, kt, 0:M], xnT[:, kt, t0:t0 + n], start=(kt == 0), stop=(kt == KT - 1))
                banks.append((pb, t0, n, kind))
            _drain_one()

            def epi():
                for (pb, t0, n, kind) in banks:
                    consumer(pb, t0, n, kind)
                if post is not None:
                    post()
                return post2

            pend["f"] = epi

        def bview(ap2, a, b):
            return ap2.rearrange("p (a b) -> p a b", a=a)

        def chunk_last(dst, src, L):
            npc = L["np"]
            c0 = 0
            if L["chunks"][0][1] == 16:
                S.copy(dst[:, 0:16], src[:, 15:16].to_broadcast([dst.shape[0], 16]), e="dve")
                c0 = 16
            n64 = (npc - c0) // 64
            if n64 > 0:
                sv = bview(src[:, c0:npc], n64, 64)
                S.copy(bview(dst[:, c0:npc], n64, 64), sv[:, :, 63:64].to_broadcast([dst.shape[0], n64, 64]), e="dve")
            if L["last"]:
                sv = bview(src[:, npc:npc + 64], 16, 4)
                S.copy(bview(dst[:, npc:npc + 64], 16, 4), sv[:, :, 3:4].to_broadcast([dst.shape[0], 16, 4]), e="dve")

        def chunk_last_cols(L):
            cols = []
            for (c0, cc, kind, g0) in L["chunks"]:
                if kind == "p":
                    cols.append(c0 + cc - 1)
                else:
                    cols += [c0 + 4 * s + 3 for s in range(16)]
            return cols

        def compact_last(dst, src, L):
            k = 0
            npc = L["np"]
            c0 = 0
            if L["chunks"][0][1] == 16:
                S.copy(dst[:, 0:1], src[:, 15:16], e="dve")
                k = 1
                c0 = 16
            n64 = (npc - c0) // 64
            if n64 > 0:
                S.copy(dst[:, k:k + n64], bview(src[:, c0:npc], n64, 64)[:, :, 63], e="dve")
                k += n64
            if L["last"]:
                S.copy(dst[:, k:k + 16], bview(src[:, npc:npc + 64], 16, 4)[:, :, 3], e="dve")
                k += 16
            return k

        def conv_block(prew, presw, acc, wbase, L, bias_col=None):
            npc = L["np"]
            w = [pcols[:, wbase + k:wbase + k + 1] for k in range(4)]
            if bias_col is None:
                S.act(acc[:, 0:npc], prew[:, 3:3 + npc], AF.Identity, scale=w[3])
            else:
                S.act(acc[:, 0:npc], prew[:, 3:3 + npc], AF.Identity, scale=w[3], bias=bias_col)
            for k in (2, 1, 0):
                S.stt(acc[:, 0:npc], prew[:, k:k + npc], w[k], acc[:, 0:npc], ALU.mult, ALU.add)
            if L["last"]:
                a3 = bview(acc[:, npc:npc + 64], 16, 4)
                if bias_col is None:
                    S.act(a3, presw[:, :, 3:7], AF.Identity, scale=w[3])
                else:
                    S.act(a3, presw[:, :, 3:7], AF.Identity, scale=w[3], bias=bias_col)
                for k in (2, 1, 0):
                    S.stt(a3, presw[:, :, k:k + 4], w[k], a3, ALU.mult, ALU.add)

        def out_rows(dst2d, src, ncols):
            pb = ps()
            S.transpose(pb[0:ncols, 0:128], src, ident_f[:, :])
            tmp = ga.alloc([128], F32)
            evac(tmp[0:ncols, :], pb[0:ncols, 0:128])
            S.dma(dst2d, tmp[0:ncols, :], is_out=True)

        for l in range(DEPTH):
            wl = win_d[l]
            S.dma(sstg[0][:, 0:4, :], lruwa_d[l].rearrange("n c d -> c n d"))
            S.copy(lwa[:], sstg[0][:, 0:4, :], e="pool")
            S.dma(sstg[1][:, 0:4, :], lruwx_d[l].rearrange("n c d -> c n d"))
            S.copy(lwx[:], sstg[1][:, 0:4, :], e="pool")
            S.dma(sstg[0][0:16, 4, :], glaw2_d[l][:, 0:128])
            S.dma(sstg[0][0:16, 5, :], glaw2_d[l][:, 128:256])
            S.copy(w2b[:].rearrange("p (a b) -> p a b", a=2), sstg[0][0:16, 4:6, :], e="pool")
            S.memset(Sd[:], 0.0)
            S.memset(Sdb[:], 0.0)
            S.memset(Sg[:], 0.0)
            S.memset(Sgb[:], 0.0)
            S.memset(Sr[:], 0.0)
            S.memset(Srb[:], 0.0)
            S.memset(hl[:], 0.0)
            S.memset(carryA[:], 0.0)
            S.memset(carryB[:], 0.0)

            for sbi in range(NSB):
                L = lay["sb"][sbi]
                npc, TBL, last = L["np"], L["tbl"], L["last"]
                if NSB > 1 or l == 0:
                    S.dma(reset_t[:, 0:TBM], cd["reset"][sbi])
                    S.dma(cos_t[:, 0:TBM], cd["cos"][sbi])
                    S.dma(sin_t[:, 0:TBM], cd["sin"][sbi])
                gm0 = ga.mark()

                rows = []
                r = 0
                while r < npc:
                    n = min(128, npc - r)
                    rows.append((L["p0"] + r, n, r))
                    r += n
                if last:
                    rows.append((TP, 64, npc))

                m0 = ga.mark()
                xt2 = [ga.alloc([D], F32) for _ in range(2)]
                xs2 = [ga.alloc([D], BF16) for _ in range(2)]
                junk = ga.alloc([D], BF16)
                st = ga.alloc([8], F32)
                for ti, (r0, n, c0) in enumerate(rows):
                    xt = xt2[ti % 2]
                    xs = xs2[ti % 2]
                    S.dma(xt[0:n, :], xres[r0:r0 + n, :])
                    S.act(junk[0:n, :], xt[0:n, :], AF.Square, accum_out=st[0:n, 0:1])
                    S.act(st[0:n, 1:2], st[0:n, 0:1], AF.Sqrt, scale=1.0 / D, bias=EPS)
                    S.recip(st[0:n, 2:3], st[0:n, 1:2])
                    S.act(xs[0:n, :], xt[0:n, :], AF.Identity, scale=st[0:n, 2:3])
                    for half in range(2):
                        pb = ps()
                        pbb = pb[:, :].bitcast(BF16)
                        for j in range(8):
                            kt = half * 8 + j
                            S.transpose(pbb[:, j * 128:j * 128 + n], xs[0:n, kt * 128:(kt + 1) * 128], ident_b[0:n, 0:n],
                                        inc=(j == 7))
                        S.tt(xnT[:, half * 8:half * 8 + 8, c0:c0 + n], pbb.rearrange("p (a b) -> p a b", a=8)[:, :, 0:n],
                             pcols[:, P_NORMW + l * 16 + half * 8:P_NORMW + l * 16 + half * 8 + 8].unsqueeze(2)
                             .to_broadcast([128, 8, n]), ALU.mult)
                ga.reset(m0)

                CP(4)
                nwcol = P_NORMW + l * 16

                gcount = {"i": 0}

                def gate_mix(mt, o_ap, normcol, do_norm):
                    wb = load_w(wl, C_GATE + mt * 128, 128, nwcol)
                    mk = ga.mark()
                    SGs = [ga.alloc([TBL], F32) for _ in range(2)]
                    RSs = [ga.alloc([TBL], F32) for _ in range(2)]
                    SQg = ga.alloc([TBL], BF16)
                    TMg = ga.alloc([TBL], F32)
                    par = gcount["i"] % 2
                    gcount["i"] += 1
                    SG, RS = SGs[par], RSs[par]

                    def cons(pb, t0, n, kind):
                        S.act(SG[:, t0:t0 + n], pb[:, 0:n], AF.Silu)
                        o = o_ap[:, t0:t0 + n]
                        if do_norm:
                            S.tt(SQg[:, t0:t0 + n], o, o, ALU.mult, e="pool")
                            pq = ps()
                            S.mm(pq[:, 0:n], ones_b[:, :], SQg[:, t0:t0 + n])
                            S.act(RS[:, t0:t0 + n], pq[:, 0:n], AF.Sqrt, scale=1.0 / 128, bias=EPS)
                        else:
                            S.tt(mixT[:, mt, t0:t0 + n], o, SG[:, t0:t0 + n], ALU.mult)

                    def post2():
                        if not do_norm:
                            return
                        S.recip(RS[:, 0:TBL], RS[:, 0:TBL], fast=True)
                        S.tt(TMg[:, 0:TBL], o_ap[:, 0:TBL], RS[:, 0:TBL], ALU.mult)
                        if normcol is not None:
                            S.stt(mixT[:, mt, 0:TBL], TMg[:, 0:TBL], normcol, SG[:, 0:TBL], ALU.mult, ALU.mult)
                        else:
                            S.tt(mixT[:, mt, 0:TBL], TMg[:, 0:TBL], SG[:, 0:TBL], ALU.mult)

                    project(wb, 128, L, cons, None, post2)
                    ga.reset(mk)

                mA = ga.mark()
                QN = ga.alloc([4, TBL], BF16)
                KN = ga.alloc([4, TBL], BF16)
                BKN = ga.alloc([4, TBL], BF16)
                KQG = ga.alloc([4, 2, TBL], BF16)
                BKT = ga.alloc([4, TBL], BF16)
                VT = ga.alloc([4, TBL], BF16)
                OT = ga.alloc([4, TBL], BF16)
                mA2 = ga.mark()
                pre2 = [ga.alloc([3 + max(npc, 1)], F32) for _ in range(2)]
                pres2 = [ga.alloc([16, 7], F32) for _ in range(2)]
                acc2 = [ga.alloc([TBL], F32) for _ in range(2)]
                sqb = ga.alloc([TBL], BF16)
                rinA = [ga.alloc([TBL], F32) for _ in range(2)]
                cst = ga.alloc([12, 48], F32)
                if last:
                    for half in range(2):
                        stg = sstg[half][0:48, 0:6, :]
                        S.dma(stg, sdconv_d[l][:, half * 768:(half + 1) * 768].rearrange("r (a p) -> r a p", p=128))
                        pb = ps()
                        for a in range(6):
                            S.transpose(pb[:, a * 48:(a + 1) * 48], sstg[half][0:48, a, :], ident_f[0:48, 0:48], inc=(a == 5))
                        evac(cst[:, half * 6:half * 6 + 6, :], pb[:, 0:288].rearrange("p (a b) -> p a b", a=6))
                for j in range(12):
                    which, h = j // 4, j % 4
                    wb = load_w(wl, C_QKV + j * 128, 128, nwcol)
                    prew, presw, acc = pre2[j % 2], pres2[j % 2], acc2[j % 2]
                    S.copy(prew[:, 0:3], carryA[:, j, :], e="pool")
                    if last:
                        S.copy(presw[:, :, 0:3], cst[:, j, :].rearrange("p (s r) -> p s r", r=3), e="pool")

                    def cons(pb, t0, n, kind, prew=prew, presw=presw):
                        if kind == "p":
                            evac(prew[:, 3 + t0:3 + t0 + n], pb[:, 0:n])
                        else:
                            evac(presw[:, :, 3:7], pb[:, 0:64].rearrange("p (s t) -> p s t", t=4))

                    def post(j=j, which=which, h=h, prew=prew, presw=presw, acc=acc):
                        wcolk = [pcols[:, P_CONVA + (l * 4 + k) * 12 + j: P_CONVA + (l * 4 + k) * 12 + j + 1] for k in range(4)]
                        if npc > 0:
                            S.act(acc[:, 0:npc], prew[:, 3:3 + npc], AF.Identity, scale=wcolk[3])
                            for k in (2, 1, 0):
                                S.stt(acc[:, 0:npc], prew[:, k:k + npc], wcolk[k], acc[:, 0:npc], ALU.mult, ALU.add)
                            S.copy(carryA[:, j, :], prew[:, npc:npc + 3], e="pool")
                        if last:
                            a3 = bview(acc[:, npc:npc + 64], 16, 4)
                            S.act(a3, presw[:, :, 3:7], AF.Identity, scale=wcolk[3])
                            for k in (2, 1, 0):
                                S.stt(a3, presw[:, :, k:k + 4], wcolk[k], a3, ALU.mult, ALU.add)
                            S.copy(cst[:, j, :].rearrange("p (s r) -> p s r", r=3), presw[:, :, 4:7], e="pool")
                        if which == 2:
                            S.act(VT[:, h, :], acc[:, 0:TBL], AF.Silu)
                        else:
                            S.act(acc[:, 0:TBL], acc[:, 0:TBL], AF.Silu)
                            S.tt(sqb[:, 0:TBL], acc[:, 0:TBL], acc[:, 0:TBL], ALU.mult, e="pool")
                            dst = QN if which == 0 else KN
                            sc_ = 128.0 if which == 0 else 1.0
                            for (t0, n, kind) in ttiles(L):
                                pq = ps()
                                S.mm(pq[:, 0:n], ones_b[:, :], sqb[:, t0:t0 + n])
                                S.act(rinA[j % 2][:, t0:t0 + n], pq[:, 0:n], AF.Sqrt, scale=sc_, bias=sc_ * EPS)

                    def post2(j=j, which=which, h=h, acc=acc):
                        if which == 2:
                            return
                        dst = QN if which == 0 else KN
                        rn = rinA[j % 2]
                        S.recip(rn[:, 0:TBL], rn[:, 0:TBL], fast=True)
                        S.tt(dst[:, h, 0:TBL], acc[:, 0:TBL], rn[:, 0:TBL], ALU.mult)

                    project(wb, 128, L, cons, post, post2)
                flush()
                if last:
                    for half in range(2):
                        tmp = ga.alloc([768], F32)
                        pb = ps()
                        for a in range(4):
                            S.transpose(pb[0:48, a * 128:(a + 1) * 128], cst[:, half * 6 + a, :], ident_f[:, :], inc=(a == 3))
                        evac(tmp[0:48, 0:512], pb[0:48, 0:512])
                        pb2 = ps()
                        for a in range(2):
                            S.transpose(pb2[0:48, a * 128:(a + 1) * 128], cst[:, half * 6 + 4 + a, :], ident_f[:, :], inc=(a == 1))
                        evac(tmp[0:48, 512:768], pb2[0:48, 0:256])
                        S.dma(sdconv_o[l][:, half * 768:(half + 1) * 768], tmp[0:48, :], is_out=True)
                    pb = ps()
                    S.transpose(pb[0:36, 0:128], carryA[:, :, :].rearrange("p a r -> p (a r)"), ident_f[:, :])
                    tmp = ga.alloc([128], F32)
                    evac(tmp[0:36, :], pb[0:36, 0:128])
                    for a in range(12):
                        S.dma(pdconv_d[l][:, a * 128:(a + 1) * 128], tmp[a * 3:a * 3 + 3, :], is_out=True)
                ga.reset(mA2)

                CP(5)
                GC = ga.alloc([TBL], F32)
                nlast = len(chunk_last_cols(L))
                GLc = ga.alloc([nlast], F32)
                decS = ga.alloc([4, nlast], F32)
                colG = ga.alloc([len(L["chunks"]), 8], F32)
                mAs = ga.mark()
                AB = ga.alloc([TBL], F32)
                Bt = ga.alloc([TBL], F32)
                G = ga.alloc([TBL], F32)
                GL = ga.alloc([TBL], F32)
                EKT = ga.alloc([TBL], F32)
                EG = ga.alloc([512], F32)
                wb = load_w(wl, C_AL, 8, nwcol)

                def cons(pb, t0, n, kind):
                    evac(AB[0:8, t0:t0 + n], pb[0:8, 0:n])

                project(wb, 8, L, cons)
                flush()
                A8, B8, G8, GC8, GL8, EK8 = AB[0:8, :], Bt[0:8, :], G[0:8, :], GC[0:8, :], GL[0:8, :], EKT[0:8, :]
                S.act(B8, A8, AF.Sigmoid)
                S.act(G8, A8, AF.Exp, bias=prm8[:, l, 0:1])
                S.act(G8, G8, AF.Ln, bias=1.0)
                S.ts(G8, G8, prm8[:, l, 1:2], op0=ALU.mult)
                S.scan(GC8, reset_t[0:8, 0:TBL], G8, 0.0)
                chunk_last(GL8, GC8, L)
                S.tt(EK8, GL8, GC8, ALU.subtract)
                S.act(EK8, EK8, AF.Exp)
                compact_last(GLc[0:8, :], GC8, L)
                for h in range(4):
                    pb = ps()
                    S.mm(pb[:, 0:nlast], sel[:, h, :], GLc[0:8, :])
                    S.act(decS[:, h, :], pb[:, 0:nlast], AF.Exp)
                for (t0, n, kind) in ttiles(L):
                    for h in range(4):
                        pg = ps()
                        S.mm(pg[:, 0:n], sel[:, h, :], GC8[:, t0:t0 + n])
                        S.act(EG[:, 0:n], pg[:, 0:n], AF.Exp)
                        S.tt(KQG[:, h, 0, t0:t0 + n], KN[:, h, t0:t0 + n], EG[:, 0:n], ALU.mult)
                        S.tt(KQG[:, h, 1, t0:t0 + n], QN[:, h, t0:t0 + n], EG[:, 0:n], ALU.mult)
                        pbb = ps()
                        S.mm(pbb[:, 0:n], sel[:, 4 + h, :], B8[:, t0:t0 + n])
                        S.tt(BKN[:, h, t0:t0 + n], KN[:, h, t0:t0 + n], pbb[:, 0:n], ALU.mult)
                        pe_ = ps()
                        S.mm(pe_[:, 0:n], sel[:, h, :], EK8[:, t0:t0 + n])
                        S.tt(BKT[:, h, t0:t0 + n], BKN[:, h, t0:t0 + n], pe_[:, 0:n], ALU.mult)
                for ci, (c0, cc, kind, g0) in enumerate(L["chunks"]):
                    pb = ps()
                    S.transpose(pb[0:cc, 0:8], GC8[:, c0:c0 + cc], ident_f[0:8, 0:8])
                    evac(colG[0:cc, ci, :], pb[0:cc, 0:8])

                CP(6)
                ga.reset(mAs)
                mA3 = ga.mark()
                nchk = len(L["chunks"])
                RT = ga.alloc([4, 64], BF16)
                OIN = ga.alloc([4, 64], F32)
                Rtok = ga.alloc([4, 128], BF16)
                Wb = ga.alloc([4, 128], BF16)

                def mkset():
                    d = {}
                    d["ARG"] = ga.alloc([4, 64], F32)
                    d["EI"] = ga.alloc([4, 64], F32)
                    d["ES"] = ga.alloc([4, 64], F32)
                    d["X"] = [ga.alloc([4, 64], F32) for _ in range(2)]
                    d["XT"] = [ga.alloc([4, 64], F32) for _ in range(2)]
                    d["PP"] = [ga.alloc([4, 64], F32) for _ in range(2)]
                    d["Xb"] = [ga.alloc([4, 64], BF16) for _ in range(2)]
                    d["XTb"] = [ga.alloc([4, 64], BF16) for _ in range(2)]
                    d["PPb"] = ga.alloc([4, 64], BF16)
                    d["AT"] = ga.alloc([4, 64], BF16)
                    d["TTb"] = ga.alloc([4, 64], BF16)
                    d["BKtok"] = ga.alloc([4, 128], BF16)
                    return d

                psets = [None, None]
                psets[(nchk + 1) % 2] = mkset()
                msamp = ga.mark()
                psets[nchk % 2] = mkset()

                def prep_gen(ci):
                    c0, c, kind, g0 = L["chunks"][ci]
                    isS = (kind == "s")
                    negm = cm["negmask_s"] if isS else cm["negmask_p"]
                    strm = cm["strict_s"] if isS else cm["strict_p"]
                    levels = 2 if isS else (6 if c == 64 else 4)
                    d = psets[ci % 2]
                    ARG, EI, ES, X, XT, PP, AT, TTb, BKtok = (d["ARG"], d["EI"], d["ES"], d["X"], d["XT"], d["PP"],
                                                              d["AT"], d["TTb"], d["BKtok"])
                    pg = ps()
                    for h in range(4):
                        S.mm(pg[0:c, h * 64:h * 64 + c], sel[:, h, 0:c], GC8[:, c0:c0 + c])
                    pk = ps()
                    pq = ps()
                    for h in range(4):
                        S.mm(pk[0:c, h * 64:h * 64 + c], BKN[:, h, c0:c0 + c], KN[:, h, c0:c0 + c])
                    for h in range(4):
                        S.mm(pq[0:c, h * 64:h * 64 + c], BKN[:, h, c0:c0 + c], QN[:, h, c0:c0 + c])
                    pbt = ps()
                    pbtb = pbt[:, :].bitcast(BF16)
                    for h in range(4):
                        S.transpose(pbtb[0:c, h * 128:(h + 1) * 128], BKT[:, h, c0:c0 + c], ident_b[:, :], inc=(h == 3))
                    for h in range(4):
                        S.stt(ARG[0:c, h, 0:c], pg[0:c, h * 64:h * 64 + c], colG[0:c, ci, h:h + 1], negm[0:c, 0:c],
                              ALU.subtract, ALU.add)
                    S.act(EI[0:c, :, 0:c], ARG[0:c, :, 0:c], AF.Exp)
                    S.tt(ES[0:c, :, 0:c], EI[0:c, :, 0:c], strm[0:c, 0:c].unsqueeze(1).to_broadcast([c, 4, c]), ALU.mult,
                         e="pool")
                    pk3 = pk[0:c, 0:256].rearrange("p (h i) -> p h i", h=4)[:, :, 0:c]
                    pq3 = pq[0:c, 0:256].rearrange("p (h i) -> p h i", h=4)[:, :, 0:c]
                    S.tt(X[0][0:c, :, 0:c], pk3, ES[0:c, :, 0:c], ALU.mult)
                    S.tt(AT[0:c, :, 0:c], pq3, EI[0:c, :, 0:c], ALU.mult)
                    evac(BKtok[0:c, :, :], pbtb[0:c, 0:512].rearrange("p (h d) -> p h d", h=4))
                    yield
                    pt = ps()
                    for h in range(4):
                        S.transpose(pt[0:c, h * 64:h * 64 + c], X[0][0:c, h, 0:c], ident_f[0:c, 0:c], inc=(h == 3))
                    pt3 = pt[0:c, 0:256].rearrange("p (h i) -> p h i", h=4)[:, :, 0:c]
                    evac(XT[0][0:c, :, 0:c], pt3)
                    S.tt(PP[0][0:c, :, 0:c], X[0][0:c, :, 0:c], ident_f[0:c, 0:c].unsqueeze(1).to_broadcast([c, 4, c]),
                         ALU.add, e="pool")
                    Xb, XTb, PPb = d["Xb"], d["XTb"], d["PPb"]
                    cur = 0
                    nlev = levels - 1
                    NF32 = 1
                    v3 = lambda p_: p_[0:c, 0:256].rearrange("p (h i) -> p h i", h=4)[:, :, 0:c]
                    for lv in range(nlev):
                        nxt = 1 - cur
                        lastlv = (lv == nlev - 1)
                        lowp = (lv >= NF32)
                        nlow = (lv + 1 >= NF32) and not lastlv
                        Xc, XTc = (Xb[cur], XTb[cur]) if lowp else (X[cur], XT[cur])
                        yield
                        pxt = ps()
                        for h in range(4):
                            S.mm(pxt[0:c, h * 64:h * 64 + c], Xc[0:c, h, 0:c], XTc[0:c, h, 0:c])
                        if not lastlv:
                            px = ps()
                            for h in range(4):
                                S.mm(px[0:c, h * 64:h * 64 + c], XTc[0:c, h, 0:c], Xc[0:c, h, 0:c])
                        XTn = XTb[nxt] if lowp else XT[nxt]
                        evac(XTn[0:c, :, 0:c], v3(pxt))
                        if nlow and not lowp:
                            evac(XTb[nxt][0:c, :, 0:c], v3(pxt))
                        if not lastlv:
                            if lowp:
                                evac(Xb[nxt][0:c, :, 0:c], v3(px))
                            else:
                                if nlow:
                                    evac(Xb[nxt][0:c, :, 0:c], v3(px))
                                else:
                                    evac(X[nxt][0:c, :, 0:c], v3(px))
                        yield
                        pp = ps()
                        for h in range(4):
                            if lowp:
                                S.mm(pp[0:c, h * 64:h * 64 + c], XTb[nxt][0:c, h, 0:c], PPb[0:c, h, 0:c])
                            else:
                                S.mm(pp[0:c, h * 64:h * 64 + c], XT[nxt][0:c, h, 0:c], PP[cur][0:c, h, 0:c])
                        if lastlv:
                            S.tt(TTb[0:c, :, 0:c], v3(pp), PP[cur][0:c, :, 0:c], ALU.add)
                        else:
                            S.tt(PP[nxt][0:c, :, 0:c], v3(pp), PP[cur][0:c, :, 0:c], ALU.add)
                            if nlow:
                                S.copy(PPb[0:c, :, 0:c], PP[nxt][0:c, :, 0:c], e="act")
                        cur = nxt

                def chain_gen(ci):
                    c0, c, kind, g0 = L["chunks"][ci]
                    isS = (kind == "s")
                    d = psets[ci % 2]
                    AT, TTb, BKtok = d["AT"], d["TTb"], d["BKtok"]
                    if not isS:
                        ppq = ps()
                        for h in range(4):
                            S.mm(ppq[:, h * 128:h * 128 + 2 * c].rearrange("p (a b) -> p a b", a=2), Sdb[:, h, :],
                                 KQG[:, h, :, c0:c0 + c])
                        for h in range(4):
                            v = ppq[:, h * 128:h * 128 + 2 * c].rearrange("p (a b) -> p a b", a=2)
                            S.tt(RT[:, h, 0:c], VT[:, h, c0:c0 + c], v[:, 0, :], ALU.subtract)
                        for h in range(4):
                            v = ppq[:, h * 128:h * 128 + 2 * c].rearrange("p (a b) -> p a b", a=2)
                            S.copy(OIN[:, h, 0:c], v[:, 1, :], e="act")
                        yield
                        prt = ps()
                        prtb = prt[:, :].bitcast(BF16)
                        for h in range(4):
                            S.transpose(prtb[0:c, h * 128:(h + 1) * 128], RT[:, h, 0:c], ident_b[:, :], inc=(h == 3))
                        evac(Rtok[0:c, :, :], prtb[0:c, 0:512].rearrange("p (h d) -> p h d", h=4))
                        yield
                        pw = ps()
                        for h in range(4):
                            S.mm(pw[0:c, h * 128:(h + 1) * 128], TTb[0:c, h, 0:c], Rtok[0:c, h, :])
                        evac(Wb[0:c, :, :], pw[0:c, :].rearrange("p (h d) -> p h d", h=4))
                        yield
                        pS = ps()
                        for h in range(4):
                            S.mm(pS[:, h * 128:(h + 1) * 128], BKtok[0:c, h, :], Wb[0:c, h, :])
                        po = ps()
                        for h in range(4):
                            S.mm(po[:, h * 64:h * 64 + c], Wb[0:c, h, :], AT[0:c, h, 0:c])
                        S.tt(Sd[:, :, :], Sd[:, :, :], decS[:, :, ci:ci + 1].to_broadcast([128, 4, 128]), ALU.mult)
                        S.tt(Sd[:, :, :], Sd[:, :, :], pS[:, :].rearrange("p (h d) -> p h d", h=4), ALU.add)
                        S.copy(Sdb[:, :, :], Sd[:, :, :], e="act")
                        S.tt(OT[:, :, c0:c0 + c], po[:, 0:256].rearrange("p (h i) -> p h i", h=4)[:, :, 0:c],
                             OIN[:, :, 0:c], ALU.add)
                    else:
                        mk_ = ga.mark()
                        ga.reset(msamp)
                        kbase = len(L["chunks"]) - 1
                        Ss = ga.alloc([16, 128], F32)
                        Ssb = ga.alloc([16, 128], BF16)
                        Sn = ga.alloc([16, 128], F32)
                        BKm = ga.alloc([16, 128], BF16)
                        for h in range(4):
                            S.dma(Ss[:, :, :], sdelta_d[l][:, h].rearrange("s k v -> k s v"))
                            S.copy(Ssb[:, :, :], Ss[:, :, :], e="pool")
                            ppq = ps()
                            for s in range(16):
                                for a_ in range(2):
                                    S.mm(ppq[:, a_ * 64 + 4 * s:a_ * 64 + 4 * s + 4], Ssb[:, s, :],
                                         KQG[:, h, a_, c0 + 4 * s:c0 + 4 * s + 4])
                            v = ppq[:, 0:128].rearrange("p (a b) -> p a b", a=2)
                            S.tt(RT[:, h, :], VT[:, h, c0:c0 + 64], v[:, 0, :], ALU.subtract)
                            evac(OIN[:, h, :], v[:, 1, :])
                            prt = ps()
                            prtb = prt[:, :].bitcast(BF16)
                            S.transpose(prtb[0:64, 0:128], RT[:, h, :], ident_b[:, :])
                            evac(Rtok[0:64, h, :], prtb[0:64, 0:128])
                            pw = ps()
                            S.mm(pw[0:64, 0:128], TTb[0:64, h, :], Rtok[0:64, h, :])
                            evac(Wb[0:64, h, :], pw[0:64, 0:128])
                            po = ps()
                            S.mm(po[:, 0:64], Wb[0:64, h, :], AT[0:64, h, :])
                            S.tt(OT[:, h, c0:c0 + 64], po[:, 0:64], OIN[:, h, :], ALU.add)
                            S.tt(BKm[0:64, :, :], BKtok[0:64, h, :].unsqueeze(1).to_broadcast([64, 16, 128]),
                                 seqmask_b[0:64, :].unsqueeze(2).to_broadcast([64, 16, 128]), ALU.mult, e="pool")
                            S.tt(Sn[:, :, :], Ss[:, :, :],
                                 decS[:, h, kbase:kbase + 16].unsqueeze(2).to_broadcast([128, 16, 128]), ALU.mult, e="pool")
                            for q4 in range(4):
                                pS = ps()
                                for s4 in range(4):
                                    s = q4 * 4 + s4
                                    S.mm(pS[:, s4 * 128:(s4 + 1) * 128], BKm[0:64, s, :], Wb[0:64, h, :])
                                S.tt(Sn[:, q4 * 4:q4 * 4 + 4, :], Sn[:, q4 * 4:q4 * 4 + 4, :],
                                     pS[:, :].rearrange("p (s d) -> p s d", s=4), ALU.add)
                            S.dma(sdelta_o[l][:, h].rearrange("s k v -> k s v"), Sn[:, :, :], is_out=True)
                        ga.reset(mk_)

                def step(g):
                    if g is None:
                        return None
                    try:
                        next(g)
                        return g
                    except StopIteration:
                        return None

                g = prep_gen(0)
                while g is not None:
                    g = step(g)
                for ci in range(nchk):
                    gp = prep_gen(ci + 1) if ci + 1 < nchk else None
                    gc = chain_gen(ci)
                    while gp is not None or gc is not None:
                        for _ in range(3):
                            gp = step(gp)
                        gc = step(gc)
                ga.reset(mA3)
                if last:
                    for h in range(4):
                        S.dma(pdelta_d[l][h], Sd[:, h, :], is_out=True)
                CP(9)
                for h in range(4):
                    gate_mix(h, OT[:, h, :], pcols[:, P_NA + l:P_NA + l + 1], True)
                flush()
                ga.reset(mA)

                CP(10)
                mB = ga.mark()
                XB = ga.alloc([TBL], F32)
                XBb = ga.alloc([TBL], BF16)
                Rg = ga.alloc([TBL], F32)
                Ig = ga.alloc([TBL], F32)
                Hh = ga.alloc([TBL], F32)
                prew = ga.alloc([3 + max(npc, 1)], F32)
                presw = ga.alloc([16, 7], F32)
                cstb = ga.alloc([4, 48], F32)
                h0 = ga.alloc([4, 16], F32)
                hs_out = ga.alloc([4, 16], F32)
                tmp16 = ga.alloc([16], F32)
                if last:
                    stg = sstg[0][0:48, 0:4, :]
                    S.dma(stg, slconv_d[l].rearrange("r (a p) -> r a p", p=128))
                    pb = ps()
                    for a in range(4):
                        S.transpose(pb[:, a * 48:(a + 1) * 48], sstg[0][0:48, a, :], ident_f[0:48, 0:48], inc=(a == 3))
                    evac(cstb[:, :, :], pb[:, 0:192].rearrange("p (a b) -> p a b", a=4))
                    stg = sstg[1][0:16, 0:4, :]
                    S.dma(stg, slru_d[l].rearrange("s (a p) -> s a p", p=128))
                    pb = ps()
                    for a in range(4):
                        S.transpose(pb[:, a * 16:(a + 1) * 16], sstg[1][0:16, a, :], ident_f[0:16, 0:16], inc=(a == 3))
                    evac(h0[:, :, :], pb[:, 0:64].rearrange("p (a b) -> p a b", a=4))
                for n_ in range(4):
                    wb = load_w(wl, C_XB + n_ * 128, 128, nwcol)
                    S.copy(prew[:, 0:3], carryB[:, n_, :], e="pool")
                    if last:
                        S.copy(presw[:, :, 0:3], cstb[:, n_, :].rearrange("p (s r) -> p s r", r=3), e="pool")

                    def cons(pb, t0, n, kind):
                        if kind == "p":
                            evac(prew[:, 3 + t0:3 + t0 + n], pb[:, 0:n])
                        else:
                            evac(presw[:, :, 3:7], pb[:, 0:64].rearrange("p (s t) -> p s t", t=4))

                    def post(n_=n_):
                        wcolk = [pcols[:, P_CONVB + (l * 4 + k) * 4 + n_: P_CONVB + (l * 4 + k) * 4 + n_ + 1] for k in range(4)]
                        bcol = pcols[:, P_CONVBB + l * 4 + n_: P_CONVBB + l * 4 + n_ + 1]
                        if npc > 0:
                            S.act(XB[:, 0:npc], prew[:, 3:3 + npc], AF.Identity, scale=wcolk[3], bias=bcol)
                            for k in (2, 1, 0):
                                S.stt(XB[:, 0:npc], prew[:, k:k + npc], wcolk[k], XB[:, 0:npc], ALU.mult, ALU.add)
                            S.copy(carryB[:, n_, :], prew[:, npc:npc + 3], e="pool")
                        if last:
                            a3 = bview(XB[:, npc:npc + 64], 16, 4)
                            S.act(a3, presw[:, :, 3:7], AF.Identity, scale=wcolk[3], bias=bcol)
                            for k in (2, 1, 0):
                                S.stt(a3, presw[:, :, k:k + 4], wcolk[k], a3, ALU.mult, ALU.add)
                            S.copy(cstb[:, n_, :].rearrange("p (s r) -> p s r", r=3), presw[:, :, 4:7], e="pool")
                        S.copy(XBb[:, 0:TBL], XB[:, 0:TBL], e="act")
                        for (t0, n, kind) in ttiles(L):
                            pr = ps()
                            S.mm(pr[:, 0:n], lwa[:, n_, :], XBb[:, t0:t0 + n])
                            S.act(Rg[:, t0:t0 + n], pr[:, 0:n], AF.Sigmoid,
                                  bias=pcols[:, P_LBA + l * 4 + n_:P_LBA + l * 4 + n_ + 1])
                            pi = ps()
                            S.mm(pi[:, 0:n], lwx[:, n_, :], XBb[:, t0:t0 + n])
                            S.act(Ig[:, t0:t0 + n], pi[:, 0:n], AF.Sigmoid,
                                  bias=pcols[:, P_LBX + l * 4 + n_:P_LBX + l * 4 + n_ + 1])
                        S.act(Rg[:, 0:TBL], Rg[:, 0:TBL], AF.Exp, scale=pcols[:, P_NSP8 + l * 4 + n_:P_NSP8 + l * 4 + n_ + 1])
                        S.tt(Hh[:, 0:TBL], Rg[:, 0:TBL], Rg[:, 0:TBL], ALU.mult)
                        S.ts(Hh[:, 0:TBL], Hh[:, 0:TBL], -1.0, 1.0, op0=ALU.mult, op1=ALU.add, e="pool")
                        S.act(Hh[:, 0:TBL], Hh[:, 0:TBL], AF.Sqrt)
                        S.tt(Ig[:, 0:TBL], Ig[:, 0:TBL], Hh[:, 0:TBL], ALU.mult)
                        S.tt(Ig[:, 0:TBL], Ig[:, 0:TBL], XB[:, 0:TBL], ALU.mult, e="pool")
                        if last:
                            A3 = bview(Rg[:, npc:npc + 64], 16, 4)
                            B3 = bview(Ig[:, npc:npc + 64], 16, 4)
                            S.tt(tmp16[:, :], A3[:, :, 0], h0[:, n_, :], ALU.mult)
                            S.tt(B3[:, :, 0], B3[:, :, 0], tmp16[:, :], ALU.add)
                            S.memset(A3[:, :, 0], 0.0)
                        if npc > 0:
                            S.scan(Hh[:, 0:npc], Rg[:, 0:npc], Ig[:, 0:npc], hl[:, n_:n_ + 1])
                            S.copy(hl[:, n_:n_ + 1], Hh[:, npc - 1:npc], e="pool")
                        if last:
                            S.scan(Hh[:, npc:npc + 64], Rg[:, npc:npc + 64], Ig[:, npc:npc + 64], 0.0)
                            S.copy(hs_out[:, n_, :], bview(Hh[:, npc:npc + 64], 16, 4)[:, :, 3], e="pool")

                    project(wb, 128, L, cons, post)
                    gate_mix(4 + n_, Hh, None, False)
                flush()
                if last:
                    pb = ps()
                    S.transpose(pb[0:4, 0:128], hl[:, :], ident_f[:, :])
                    t4 = ga.alloc([128], F32)
                    evac(t4[0:4, :], pb[0:4, 0:128])
                    S.dma(plru_d[l], t4[0:4, :], is_out=True)
                    pb = ps()
                    for a in range(4):
                        S.transpose(pb[0:16, a * 128:(a + 1) * 128], hs_out[:, a, :], ident_f[:, :], inc=(a == 3))
                    t5 = ga.alloc([512], F32)
                    evac(t5[0:16, :], pb[0:16, :])
                    S.dma(slru_o[l], t5[0:16, :], is_out=True)
                    pb = ps()
                    for a in range(4):
                        S.transpose(pb[0:48, a * 128:(a + 1) * 128], cstb[:, a, :], ident_f[:, :], inc=(a == 3))
                    t6 = ga.alloc([512], F32)
                    evac(t6[0:48, :], pb[0:48, :])
                    S.dma(slconv_o[l], t6[0:48, :], is_out=True)
                    pb = ps()
                    S.transpose(pb[0:12, 0:128], carryB[:, :, :].rearrange("p a r -> p (a r)"), ident_f[:, :])
                    t7 = ga.alloc([128], F32)
                    evac(t7[0:12, :], pb[0:12, 0:128])
                    for a in range(4):
                        S.dma(plconv_d[l][:, a * 128:(a + 1) * 128], t7[a * 3:a * 3 + 3, :], is_out=True)
                ga.reset(mB)

                CP(11)
                for grp in ("C", "D"):
                    mC = ga.mark()
                    QA = ga.alloc([2, TBL], BF16)
                    QS_ = ga.alloc([2, TBL], BF16)
                    KA = ga.alloc([2, TBL], BF16)
                    KS_ = ga.alloc([2, TBL], BF16)
                    VT2 = ga.alloc([4, TBL], BF16)
                    OT2 = ga.alloc([4, TBL], BF16)
                    nlast = len(chunk_last_cols(L))
                    decC = ga.alloc([2, nlast], F32)
                    Sx, Sxb = (Sg, Sgb) if grp == "C" else (Sr, Srb)
                    sst_d, sst_o, pst_d = (sgla_d, sgla_o, pgla_d) if grp == "C" else (sret_d, sret_o, pret_d)
                    cq, ck, cv = (C_QC, C_KC, C_VC) if grp == "C" else (C_QD, C_KD, C_VD)
                    mC2 = ga.mark()
                    if grp == "C":
                        RCT = ga.alloc([TBL], BF16)
                        LT = ga.alloc([TBL], F32)
                        CS = ga.alloc([2, TBL], F32)
                        CSL = ga.alloc([2, TBL], F32)
                        EB = ga.alloc([2, TBL], F32)
                        EBN = ga.alloc([2, TBL], F32)
                        EKS = ga.alloc([2, TBL], F32)
                        CLc = ga.alloc([2, nlast], F32)
                        wb = load_w(wl, C_RC, 16, nwcol)

                        def cons(pb, t0, n, kind):
                            evac(RCT[0:16, t0:t0 + n], pb[0:16, 0:n])

                        project(wb, 16, L, cons)
                        flush()
                        for t in range(2):
                            for (t0, n, kind) in ttiles(L):
                                pz = ps()
                                S.mm(pz[:, 0:n], w2b[0:16, t * 128:(t + 1) * 128], RCT[0:16, t0:t0 + n])
                                S.act(LT[:, t0:t0 + n], pz[:, 0:n], AF.Exp, scale=-1.0,
                                      bias=pcols[:, P_NB2 + l * 2 + t:P_NB2 + l * 2 + t + 1])
                            S.act(LT[:, 0:TBL], LT[:, 0:TBL], AF.Ln, bias=1.0)
                            S.scan(CS[:, t, :], reset_t[:, 0:TBL], LT[:, 0:TBL], 0.0)
                            chunk_last(CSL[:, t, :], CS[:, t, :], L)
                            compact_last(CLc[:, t, :], CS[:, t, :], L)
                        S.act(EB[:, :, :], CS[:, :, :], AF.Exp, scale=-1.0 / 16)
                        S.act(EBN[:, :, :], CS[:, :, :], AF.Exp, scale=1.0 / 16)
                        S.tt(EKS[:, :, :], CS[:, :, :], CSL[:, :, :], ALU.subtract)
                        S.act(EKS[:, :, :], EKS[:, :, :], AF.Exp, scale=1.0 / 16)
                        S.act(decC[:, :, :], CLc[:, :, :], AF.Exp, scale=-1.0 / 16)
                        for t in range(2):
                            wb = load_w(wl, cq + t * 128, 128, nwcol)

                            def cons(pb, t0, n, kind, t=t):
                                S.stt(QA[:, t, t0:t0 + n], pb[:, 0:n], 0.125, EB[:, t, t0:t0 + n], ALU.mult, ALU.mult)

                            project(wb, 128, L, cons)
                            wb = load_w(wl, ck + t * 128, 128, nwcol)

                            def cons(pb, t0, n, kind, t=t):
                                S.tt(KA[:, t, t0:t0 + n], pb[:, 0:n], EBN[:, t, t0:t0 + n], ALU.mult)
                                S.tt(KS_[:, t, t0:t0 + n], pb[:, 0:n], EKS[:, t, t0:t0 + n], ALU.mult)

                            project(wb, 128, L, cons)
                        QSt = QA
                    else:
                        QRf = ga.alloc([TBL], F32)
                        T1 = ga.alloc([512], F32)
                        T2 = ga.alloc([512], F32)
                        Qb = ga.alloc([512], BF16)
                        for which in range(2):
                            for t in range(2):
                                wb = load_w(wl, (cq if which == 0 else ck) + t * 128, 128, nwcol)

                                def cons(pb, t0, n, kind):
                                    S.copy(Qb[:, 0:n], pb[:, 0:n], e="act")
                                    pm = ps()
                                    S.mm(pm[:, 0:n], perm_b[:, :], Qb[:, 0:n])
                                    S.tt(T1[:, 0:n], pb[:, 0:n], cos_t[:, t0:t0 + n], ALU.mult)
                                    S.tt(T2[:, 0:n], pm[:, 0:n], sin_t[:, t0:t0 + n], ALU.mult)
                                    S.tt(QRf[:, t0:t0 + n], T1[:, 0:n], T2[:, 0:n], ALU.add, e="pool")

                                def post(which=which, t=t):
                                    dA = QA if which == 0 else KA
                                    dS = QS_ if which == 0 else KS_
                                    evac(dA[:, t, :], QRf[:, 0:TBL])
                                    c0 = 0
                                    if L["chunks"][0][1] == 16:
                                        tb = cm["fs64"][:, t, 0:16] if which == 0 else cm["ts16"][:, t, :]
                                        S.tt(dS[:, t, 0:16], QRf[:, 0:16], tb, ALU.mult)
                                        c0 = 16
                                    n64 = (npc - c0) // 64
                                    if n64 > 0:
                                        tb = cm["fs64"][:, t, :] if which == 0 else cm["ts64"][:, t, :]
                                        S.tt(bview(dS[:, t, c0:npc], n64, 64), bview(QRf[:, c0:npc], n64, 64),
                                             tb.unsqueeze(1).to_broadcast([128, n64, 64]), ALU.mult)
                                    if last:
                                        tb = cm["fss"][:, t, :] if which == 0 else cm["tss"][:, t, :]
                                        S.tt(dS[:, t, npc:npc + 64], QRf[:, npc:npc + 64], tb, ALU.mult)

                                project(wb, 128, L, cons, post)
                        QSt = QS_
                    for h in range(4):
                        wb = load_w(wl, cv + h * 128, 128, nwcol)

                        def cons(pb, t0, n, kind, h=h):
                            evac(VT2[:, h, t0:t0 + n], pb[:, 0:n])

                        project(wb, 128, L, cons)
                    flush()
                    ga.reset(mC2)
                    QAm = ga.alloc([2, 2, TBL], BF16)
                    S.memset(QAm[:, :, :, :], 0.0)
                    for t in range(2):
                        evac(QAm[0:64, t, 0, :], QA[0:64, t, :])
                        evac(QAm[64:128, t, 1, :], QA[64:128, t, :])
                    if grp == "C":
                        QSm = QAm
                    else:
                        QSm = ga.alloc([2, 2, TBL], BF16)
                        S.memset(QSm[:, :, :, :], 0.0)
                        for t in range(2):
                            evac(QSm[0:64, t, 0, :], QS_[0:64, t, :])
                            evac(QSm[64:128, t, 1, :], QS_[64:128, t, :])
                    ATs = [ga.alloc([4, 64], BF16) for _ in range(2)]
                    Vtoks = [ga.alloc([4, 128], BF16) for _ in range(2)]
                    KStoks = [ga.alloc([2, 128], BF16) for _ in range(2)]
                    mC3 = ga.mark()

                    def stage1(ci):
                        c0, c, kind, g0 = L["chunks"][ci]
                        isS = (kind == "s")
                        AT, Vtok, KStok = ATs[ci % 2], Vtoks[ci % 2], KStoks[ci % 2]
                        pa = ps()
                        for h in range(4):
                            t, e_ = h // 2, h % 2
                            S.mm(pa[0:c, h * 64:h * 64 + c], KA[:, t, c0:c0 + c], QAm[:, t, e_, c0:c0 + c])
                        pa3 = pa[0:c, 0:256].rearrange("p (h i) -> p h i", h=4)[:, :, 0:c]
                        if grp == "C":
                            m_ = cm["incl_s"] if isS else cm["incl_p"]
                            S.tt(AT[0:c, :, 0:c], pa3, m_[0:c, 0:c].unsqueeze(1).to_broadcast([c, 4, c]), ALU.mult)
                        else:
                            m_ = cm["retm_s"] if isS else cm["retm_p"]
                            S.tt(AT[0:c, :, 0:c], pa3, m_[0:c, :, 0:c], ALU.mult)
                        pv = ps()
                        pvb = pv[:, :].bitcast(BF16)
                        for h in range(4):
                            S.transpose(pvb[0:c, h * 128:(h + 1) * 128], VT2[:, h, c0:c0 + c], ident_b[:, :], inc=(h == 3))
                        evac(Vtok[0:c, :, :], pvb[0:c, 0:512].rearrange("p (h d) -> p h d", h=4))
                        pk = ps()
                        pkb = pk[:, :].bitcast(BF16)
                        for t in range(2):
                            S.transpose(pkb[0:c, t * 128:(t + 1) * 128], KS_[:, t, c0:c0 + c], ident_b[:, :], inc=(t == 1))
                        evac(KStok[0:c, :, :], pkb[0:c, 0:256].rearrange("p (t d) -> p t d", t=2))

                    def stage2(ci):
                        c0, c, kind, g0 = L["chunks"][ci]
                        isS = (kind == "s")
                        AT, Vtok, KStok = ATs[ci % 2], Vtoks[ci % 2], KStoks[ci % 2]
                        if not isS:
                            po = ps()
                            for h in range(4):
                                t, e_ = h // 2, h % 2
                                S.mm(po[:, h * 64:h * 64 + c], Sxb[:, t, :], QSm[:, t, e_, c0:c0 + c], start=True, stop=False)
                                S.mm(po[:, h * 64:h * 64 + c], Vtok[0:c, h, :], AT[0:c, h, 0:c], start=False, stop=True)
                            evac(OT2[:, :, c0:c0 + c], po[:, 0:256].rearrange("p (h i) -> p h i", h=4)[:, :, 0:c])
                            for e_ in range(2):
                                hp = 64 * e_
                                pS = ps()
                                for t in range(2):
                                    S.mm(pS[hp:hp + 64, t * 128:(t + 1) * 128], KStok[0:c, t, hp:hp + 64],
                                         Vtok[0:c, 2 * t + e_, :])
                                for t in range(2):
                                    if grp == "C":
                                        dcol = decC[hp:hp + 64, t, ci:ci + 1]
                                    else:
                                        dcol = cm["retdec"][hp:hp + 64, t, (0 if c == 64 else 1):(1 if c == 64 else 2)]
                                    S.stt(Sx[hp:hp + 64, t, :], Sx[hp:hp + 64, t, :], dcol,
                                          pS[hp:hp + 64, t * 128:(t + 1) * 128], ALU.mult, ALU.add)
                            S.copy(Sxb[:, :, :], Sx[:, :, :], e="act")
                        else:
                            ga.reset(mC3)
                            kbase = len(L["chunks"]) - 1
                            Ss = ga.alloc([16, 2, 128], F32)
                            Ssb = ga.alloc([16, 2, 128], BF16)
                            Sn = ga.alloc([16, 2, 128], F32)
                            KSm = ga.alloc([16, 2, 128], BF16)
                            for hp_ in range(2):
                                S.dma(Ss[hp_ * 64:(hp_ + 1) * 64, :, :, :],
                                      sst_d[l].rearrange("s (t e) k v -> e k s t v", e=2)[hp_])
                            S.copy(Ssb[:, :, :, :], Ss[:, :, :, :], e="pool")
                            S.tt(KSm[0:64, :, :, :], KStok[0:64, :, :].unsqueeze(1).to_broadcast([64, 16, 2, 128]),
                                 seqmask_b[0:64, :].unsqueeze(2).unsqueeze(3).to_broadcast([64, 16, 2, 128]), ALU.mult,
                                 e="pool")
                            if grp == "C":
                                S.tt(Sn[:, :, :, :], Ss[:, :, :, :],
                                     decC[:, :, kbase:kbase + 16].rearrange("p t s -> p s t").unsqueeze(3)
                                     .to_broadcast([128, 16, 2, 128]), ALU.mult, e="pool")
                            else:
                                S.tt(Sn[:, :, :, :], Ss[:, :, :, :],
                                     cm["retdec"][:, :, 2:3].unsqueeze(1).to_broadcast([128, 16, 2, 128]), ALU.mult,
                                     e="pool")
                            po = ps()
                            for h in range(4):
                                t, e_ = h // 2, h % 2
                                S.mm(po[:, h * 64:h * 64 + 64], Vtok[0:64, h, :], AT[0:64, h, :], start=True, stop=False)
                                for s_i in range(16):
                                    S.mm(po[:, h * 64 + 4 * s_i:h * 64 + 4 * s_i + 4], Ssb[:, s_i, t, :],
                                         QSm[:, t, e_, c0 + 4 * s_i:c0 + 4 * s_i + 4], start=False, stop=(s_i == 15))
                            evac(OT2[:, :, c0:c0 + 64], po[:, 0:256].rearrange("p (h i) -> p h i", h=4))
                            for e_ in range(2):
                                hp = 64 * e_
                                for s2 in range(8):
                                    pS = ps()
                                    for s_ in range(2):
                                        s_i = s2 * 2 + s_
                                        for t in range(2):
                                            S.mm(pS[hp:hp + 64, (s_ * 2 + t) * 128:(s_ * 2 + t + 1) * 128],
                                                 KSm[0:64, s_i, t, hp:hp + 64], Vtok[0:64, 2 * t + e_, :])
                                    S.tt(Sn[hp:hp + 64, s2 * 2:s2 * 2 + 2, :, :], Sn[hp:hp + 64, s2 * 2:s2 * 2 + 2, :, :],
                                         pS[hp:hp + 64, :].rearrange("p (s t d) -> p s t d", s=2, t=2), ALU.add)
                            for hp_ in range(2):
                                S.dma(sst_o[l].rearrange("s (t e) k v -> e k s t v", e=2)[hp_],
                                      Sn[hp_ * 64:(hp_ + 1) * 64, :, :, :], is_out=True)

                    nchk = len(L["chunks"])
                    stage1(0)
                    for ci in range(nchk):
                        if ci + 1 < nchk:
                            stage1(ci + 1)
                        stage2(ci)
                    ga.reset(mC3)
                    if last:
                        for h in range(4):
                            t, hp = h // 2, (h % 2) * 64
                            S.dma(pst_d[l][h], Sx[hp:hp + 64, t, :], is_out=True)
                    for h in range(4):
                        mt = (8 if grp == "C" else 12) + h
                        gate_mix(mt, OT2[:, h, :], pcols[:, P_NC + l:P_NC + l + 1] if grp == "C" else None, True)
                    flush()
                    ga.reset(mC)

                CP(12)
                flush()
                mO = ga.mark()
                xo2 = [ga.alloc([256], F32) for _ in range(4)]
                for cb in range(8):
                    e0 = w_get("out", l, cb * 256, 128)
                    w_get("out", l, cb * 256 + 128, 128)
                    wo = wo2[e0[4]]
                    for ti, (r0, n, c0) in enumerate(rows):
                        xo = xo2[(cb * len(rows) + ti) % 4]
                        S.dma(xo[0:n, :], xres[r0:r0 + n, cb * 256:(cb + 1) * 256])
                        pb = ps()
                        for kt in range(KT):
                            S.mm(pb[0:n, 0:256], mixT[:, kt, c0:c0 + n], wo[:, kt, :], start=(kt == 0), stop=(kt == KT - 1))
                        S.tt(xo[0:n, :], xo[0:n, :], pb[0:n, 0:256], ALU.add)
                        S.dma(xres[r0:r0 + n, cb * 256:(cb + 1) * 256], xo[0:n, :])
                ga.reset(mO)
                ga.reset(gm0)

        CP(13)
        fnb = ga.alloc([D], F32)
        S.dma(fnb[:, :], fnorm_d.partition_broadcast(128))
        xt2 = [ga.alloc([D], F32) for _ in range(2)]
        junk = ga.alloc([D], BF16)
        st = ga.alloc([8], F32)
        rows = []
        r = 0
        while r < TT:
            n = min(128, (TP if r < TP else TT) - r)
            rows.append((r, n))
            r += n
        for ti, (r0, n) in enumerate(rows):
            xt = xt2[ti % 2]
            S.dma(xt[0:n, :], xres[r0:r0 + n, :])
            S.act(junk[0:n, :], xt[0:n, :], AF.Square, accum_out=st[0:n, 0:1])
            S.act(st[0:n, 1:2], st[0:n, 0:1], AF.Sqrt, scale=1.0 / D, bias=EPS)
            S.recip(st[0:n, 2:3], st[0:n, 1:2])
            S.act(xt[0:n, :], xt[0:n, :], AF.Identity, scale=st[0:n, 2:3])
            S.tt(xt[0:n, :], xt[0:n, :], fnb[0:n, :], ALU.mult)
            if r0 >= TP:
                S.dma(ys_d[r0 - TP:r0 - TP + n, :], xt[0:n, :], is_out=True)
            else:
                a = max(r0, 16)
                if a < r0 + n:
                    S.dma(yp_d[a - 16:r0 + n - 16, :], xt[a - r0:n, :], is_out=True)

    except _Stop:
        pass
    S.finish()
    return nc, hc


_CACHE = {}


def kernel(**inputs):
    cfg = Cfg(nch=32, depth=4, nsb=4, nseq=16)
    if "nc" not in _CACHE:
        _CACHE["nc"] = build(cfg)
    nc, hc = _CACHE["nc"]
    f = np.float32
    g = lambda k: np.ascontiguousarray(np.asarray(inputs[k], dtype=f))
    shared = {k: g(k) for k in ("meta_tokens", "norm_w", "w_in", "conv_a", "a_log", "dt_bias", "norm_a", "conv_b",
                                "conv_b_bias", "lru_wa", "lru_ba", "lru_wx", "lru_bx", "lru_lambda", "gla_w2",
                                "gla_b2", "norm_c", "w_out", "final_norm")}
    for k, v in hc.items():
        shared["c_" + k] = v
    xp, xs = g("x_prompt"), g("x_sample")
    sd, sdc, sl, slc, sg_, sr_ = (g("state_delta"), g("state_delta_conv"), g("state_lru"), g("state_lru_conv"),
                                  g("state_gla"), g("state_ret"))
    in_maps = []
    for i in range(8):
        b = i % 4
        sl_ = slice(16 * i, 16 * i + 16)
        m = dict(shared)
        m["xp"] = xp[b]
        m["xs"] = np.ascontiguousarray(xs[sl_].reshape(64, D))
        m["sdelta"] = np.ascontiguousarray(sd[:, sl_])
        m["sdconv"] = np.ascontiguousarray(sdc[:, sl_].reshape(4, 48, 1536))
        m["slru"] = np.ascontiguousarray(sl[:, sl_])
        m["slconv"] = np.ascontiguousarray(slc[:, sl_].reshape(4, 48, 512))
        m["sgla"] = np.ascontiguousarray(sg_[:, sl_])
        m["sret"] = np.ascontiguousarray(sr_[:, sl_])
        in_maps.append(m)
    res = run_bass_kernel_spmd(nc, in_maps, core_ids=list(range(8))).results
    cat = lambda k, ax: np.concatenate([res[i][k] for i in range(8)], axis=ax)
    stk = lambda k: np.stack([res[i][k] for i in range(4)], axis=1)
    y_p = np.stack([res[i]["y_p"] for i in range(4)], axis=0)
    y_s = cat("y_s", 0).reshape(128, 4, D)
    outs = (y_p, y_s, stk("p_delta"), stk("p_dconv"), stk("p_lru").reshape(4, 4, 512), stk("p_lconv"),
            stk("p_gla"), stk("p_ret"),
            cat("s_delta", 1), cat("s_dconv", 1).reshape(4, 128, 3, 1536), cat("s_lru", 1),
            cat("s_lconv", 1).reshape(4, 128, 3, 512), cat("s_gla", 1), cat("s_ret", 1))
    return tuple(np.ascontiguousarray(o.astype(f)) for o in outs)
```

```python
import math
import numpy as np
import ml_dtypes
import concourse.bass as bass
import concourse.mybir as mybir
from concourse.bass_utils import run_bass_kernel_spmd

F32 = mybir.dt.float32
BF16 = mybir.dt.bfloat16
ALU = mybir.AluOpType
AF = mybir.ActivationFunctionType
ESZ = {F32: 4, BF16: 2}

D = 2048
KT = 16
INW = 6168
EPS = 1e-6
NEG = -30000.0
import os as _os
LOOKD = int(_os.environ.get("LOOKD", "2"))
LOOKC = int(_os.environ.get("LOOKC", "1"))
C_QKV, C_AL, C_XB, C_QC, C_KC, C_VC, C_RC, C_QD, C_KD, C_VD, C_GATE = 0, 1536, 1544, 2056, 2312, 2568, 3080, 3096, 3352, 3608, 4120


def _rng(ap):
    t = ap.tensor
    dims = list(ap.ap)
    off = int(ap.offset)
    es = ESZ.get(ap.dtype, 4)
    if type(t).__name__.startswith("DRam"):
        ext = 0
        for s, n in dims:
            ext += (int(n) - 1) * abs(int(s))
        return (t.name, off * es, (off + ext + 1) * es)
    if type(t).__name__.startswith("PSum"):
        return (t.name, 0, 1 << 30)
    pstep = int(dims[0][0])
    if pstep <= 0:
        pstep = 1 << 40
    lo = off % pstep
    ext = 0
    for s, n in dims[1:]:
        ext += (int(n) - 1) * abs(int(s))
    return (t.name, lo * es, (lo + ext + 1) * es)


class Sch:
    NDMA = 24

    def __init__(self, nc):
        self.nc = nc
        self.eng = {"pe": nc.tensor, "dve": nc.vector, "act": nc.scalar, "pool": nc.gpsimd, "sp": nc.sync}
        self.sem = {k: nc.alloc_semaphore("sem_" + k) for k in self.eng}
        self.cnt = {k: 0 for k in self.eng}
        self.seen = {k: {} for k in self.eng}
        self.dsem = [nc.alloc_semaphore("dsem%d" % i) for i in range(self.NDMA)]
        self.dcnt = [0] * self.NDMA
        self.drr = 0
        self.tr = {}
        self.nins = 0
        self.out_dmas = {}

    def _deps(self, reads, writes, e=None):
        deps = {}
        for ap in reads:
            name, lo, hi = _rng(ap)
            t = self.tr.get(name)
            if t is None:
                continue
            for (l, h, src, val) in t["w"]:
                if l < hi and lo < h and deps.get(src, 0) < val:
                    deps[src] = val
            if type(ap.tensor).__name__.startswith("PSum"):
                for (l, h, src, val) in t["r"]:
                    if src != e and deps.get(src, 0) < val:
                        deps[src] = val
        for ap in writes:
            name, lo, hi = _rng(ap)
            t = self.tr.get(name)
            if t is None:
                continue
            for (l, h, src, val) in t["w"]:
                if l < hi and lo < h and deps.get(src, 0) < val:
                    deps[src] = val
            for (l, h, src, val) in t["r"]:
                if l < hi and lo < h and deps.get(src, 0) < val:
                    deps[src] = val
        return deps

    def _record(self, reads, writes, src, val):
        for ap in writes:
            name, lo, hi = _rng(ap)
            t = self.tr.setdefault(name, {"w": [], "r": []})
            t["w"] = [e for e in t["w"] if not (lo <= e[0] and e[1] <= hi)]
            t["r"] = [e for e in t["r"] if not (lo <= e[0] and e[1] <= hi)]
            t["w"].append((lo, hi, src, val))
        for ap in reads:
            name, lo, hi = _rng(ap)
            t = self.tr.setdefault(name, {"w": [], "r": []})
            t["r"] = [e for e in t["r"] if not (e[2] == src and lo <= e[0] and e[1] <= hi)]
            t["r"].append((lo, hi, src, val))

    def _semof(self, src):
        return self.dsem[src[1]] if isinstance(src, tuple) else self.sem[src]

    def _wait(self, e, deps):
        for src, val in deps.items():
            if src == e and e == "pe":
                continue
            if self.seen[e].get(src, 0) >= val:
                continue
            self.eng[e].wait_ge(self._semof(src), val)
            self.seen[e][src] = val
            self.nins += 1

    def op(self, e, fn, reads, writes, inc=True):
        self._wait(e, self._deps(reads, writes, e))
        ins = fn()
        self.nins += 1
        if inc:
            self.cnt[e] += 1
            ins.then_inc(self.sem[e], 1)
            val = self.cnt[e]
        else:
            val = self.cnt[e] + 1
        self._record(reads, writes, e, val)
        return ins

    def dma(self, out, in_, q="sp", is_out=False):
        deps = self._deps([in_], [out])
        i = self.drr
        self.drr = (self.drr + 1) % self.NDMA
        if self.dcnt[i] > 0:
            deps[("dma", i)] = max(deps.get(("dma", i), 0), 16 * self.dcnt[i])
        self._wait(q, deps)
        ins = self.eng[q].dma_start(out=out, in_=in_, allow_slow_non_contiguous=True)
        self.nins += 1
        self.dcnt[i] += 1
        ins.then_inc(self.dsem[i], 16)
        val = 16 * self.dcnt[i]
        self._record([in_], [out], ("dma", i), val)
        if is_out:
            self.out_dmas[("dma", i)] = val

    def finish(self, q="sp"):
        deps = dict(self.out_dmas)
        for e in self.eng:
            if e != q and self.cnt[e] > 0:
                deps[e] = self.cnt[e]
        self._wait(q, deps)

    def mm(self, out, lhsT, rhs, start=True, stop=True):
        return self.op("pe", lambda: self.nc.tensor.matmul(out, lhsT, rhs, start=start, stop=stop),
                       [lhsT, rhs], [out], inc=stop)

    def transpose(self, out, in_, ident, inc=True):
        return self.op("pe", lambda: self.nc.tensor.transpose(out, in_, ident), [in_, ident], [out], inc=inc)

    def tt(self, out, in0, in1, op, e="dve"):
        return self.op(e, lambda: self.eng[e].tensor_tensor(out=out, in0=in0, in1=in1, op=op), [in0, in1], [out])

    def ts(self, out, in0, s1, s2=None, op0=ALU.mult, op1=None, e="dve"):
        rd = [in0] + [s for s in (s1, s2) if not isinstance(s, (int, float, type(None)))]
        if op1 is None:
            return self.op(e, lambda: self.eng[e].tensor_scalar(out=out, in0=in0, scalar1=s1, scalar2=None, op0=op0),
                           rd, [out])
        return self.op(e, lambda: self.eng[e].tensor_scalar(out=out, in0=in0, scalar1=s1, scalar2=s2, op0=op0,
                                                            op1=op1), rd, [out])

    def stt(self, out, in0, scalar, in1, op0, op1):
        rd = [in0, in1] + ([] if isinstance(scalar, (int, float)) else [scalar])
        return self.op("dve", lambda: self.nc.vector.scalar_tensor_tensor(out=out, in0=in0, scalar=scalar, in1=in1,
                                                                           op0=op0, op1=op1), rd, [out])

    def act(self, out, in_, func, bias=None, scale=None, accum_out=None):
        rd = [in_]
        wr = [out]
        kw = {}
        if bias is not None:
            kw["bias"] = bias
            if not isinstance(bias, (int, float)):
                rd.append(bias)
        if scale is not None:
            kw["scale"] = scale
            if not isinstance(scale, (int, float)):
                rd.append(scale)
        if accum_out is not None:
            kw["accum_out"] = accum_out
            wr.append(accum_out)
        return self.op("act", lambda: self.nc.scalar.activation(out=out, in_=in_, func=func, **kw), rd, wr)

    def copy(self, out, in_, e="dve"):
        if e == "act":
            return self.act(out, in_, AF.Copy)
        return self.op(e, lambda: self.eng[e].tensor_copy(out=out, in_=in_), [in_], [out])

    def scan(self, out, d0, d1, initial):
        rd = [d0, d1] + ([] if isinstance(initial, (int, float)) else [initial])
        return self.op("dve", lambda: self.nc.vector.tensor_tensor_scan(out=out, data0=d0, data1=d1, initial=initial,
                                                                         op0=ALU.mult, op1=ALU.add), rd, [out])

    def memset(self, ap, val, e="pool"):
        return self.op(e, lambda: self.eng[e].memset(ap, val), [], [ap])

    def recip(self, out, in_, fast=False):
        return self.op("dve", lambda: self.nc.vector.reciprocal(out=out, in_=in_), [in_], [out])


class Arena:
    def __init__(self, nc, name, nbytes):
        self.words = nbytes // 4
        self.t = nc.alloc_sbuf_tensor(name, [128, self.words], F32)
        self.off = 0
        self.peak = 0

    def alloc(self, shape, dtype, parts=128):
        n = 1
        for s in shape:
            n *= s
        nb = (n * ESZ[dtype] + 31) // 32 * 32
        w0 = self.off // 4
        self.off += nb
        self.peak = max(self.peak, self.off)
        assert self.off // 4 <= self.words, "arena overflow %s %d > %d" % (self.t.name, self.off, self.words * 4)
        v = self.t[0:parts, w0:w0 + nb // 4]
        if dtype != F32:
            v = v.bitcast(dtype)
        v = v[:, 0:n]
        if len(shape) == 2:
            v = v.rearrange("p (a b) -> p a b", a=shape[0])
        elif len(shape) == 3:
            v = v.rearrange("p (a b c) -> p a b c", a=shape[0], b=shape[1])
        elif len(shape) == 4:
            v = v.rearrange("p (a b c d) -> p a b c d", a=shape[0], b=shape[1], c=shape[2])
        return v

    def mark(self):
        return self.off

    def reset(self, m):
        self.off = m


class _Stop(Exception):
    pass


class Cfg:
    def __init__(self, nch=32, depth=4, nsb=4, nseq=16):
        self.nch, self.depth, self.nsb, self.nseq = nch, depth, nsb, nseq
        self.stop = 0


def host_consts(cfg):
    nch, nsb = cfg.nch, cfg.nsb
    f = np.float32
    c = {}
    c["ident"] = np.eye(128, dtype=f)
    j = np.arange(64)[:, None]
    i = np.arange(64)[None, :]
    same = (j // 4) == (i // 4)
    c["negmask_p"] = np.where(j <= i, 0.0, NEG).astype(f)
    c["negmask_s"] = np.where((j <= i) & same, 0.0, NEG).astype(f)
    c["strict_p"] = np.where(j < i, -1.0, 0.0).astype(f)
    c["strict_s"] = np.where((j < i) & same, -1.0, 0.0).astype(f)
    c["incl_p"] = np.where(j <= i, 1.0, 0.0).astype(f)
    c["incl_s"] = np.where((j <= i) & same, 1.0, 0.0).astype(f)
    lg = np.log(1.0 - 2.0 ** (-5.0 - np.arange(4, dtype=np.float64)))
    rp = np.zeros((64, 4, 64), np.float64)
    rs = np.zeros((64, 4, 64), np.float64)
    for h in range(4):
        rp[:, h, :] = np.where(j <= i, np.exp(lg[h] * np.maximum(i - j, 0)), 0.0) * 0.125
        rs[:, h, :] = np.where((j <= i) & same, np.exp(lg[h] * np.maximum(i - j, 0)), 0.0) * 0.125
    c["retm_p"] = rp.astype(f)
    c["retm_s"] = rs.astype(f)
    c["seqmask"] = (np.arange(64)[:, None] // 4 == np.arange(16)[None, :]).astype(f)
    sel = np.zeros((8, 8, 128), f)
    for k in range(8):
        sel[k, k, :] = 1.0
    c["sel"] = sel
    perm = np.zeros((128, 128), f)
    for m in range(128):
        k = m + 32 if (m % 64) < 32 else m - 32
        perm[k, m] = 1.0
    c["perm"] = perm
    hrow = np.arange(128) // 64
    fs64 = np.zeros((128, 2, 64), np.float64)
    ts64 = np.zeros((128, 2, 64), np.float64)
    ts16 = np.zeros((128, 2, 16), np.float64)
    fss = np.zeros((128, 2, 64), np.float64)
    tss = np.zeros((128, 2, 64), np.float64)
    dec = np.zeros((128, 2, 3), np.float64)
    pos = np.arange(64)
    for t in range(2):
        g = lg[2 * t + hrow][:, None]
        fs64[:, t, :] = np.exp(g * (pos[None, :] + 1.0))
        ts64[:, t, :] = np.exp(g * (63.0 - pos[None, :])) * 0.125
        ts16[:, t, :] = np.exp(g * (15.0 - pos[None, :16])) * 0.125
        fss[:, t, :] = np.exp(g * ((pos[None, :] % 4) + 1.0))
        tss[:, t, :] = np.exp(g * (3.0 - (pos[None, :] % 4))) * 0.125
        dec[:, t, 0] = np.exp(g[:, 0] * 64.0)
        dec[:, t, 1] = np.exp(g[:, 0] * 16.0)
        dec[:, t, 2] = np.exp(g[:, 0] * 4.0)
    c["fs64"], c["ts64"], c["ts16"], c["fss"], c["tss"], c["retdec"] = [a.astype(f) for a in (fs64, ts64, ts16, fss, tss, dec)]
    lay = layout(cfg)
    tbm = lay["tbmax"]
    reset = np.ones((nsb, 128, tbm), f)
    cos = np.zeros((nsb, 128, tbm), f)
    sin = np.zeros((nsb, 128, tbm), f)
    half = 32
    freqs = (10000.0 ** (-np.arange(half, dtype=np.float32) / half)).astype(np.float32)
    fr = freqs[np.arange(128) % 32]
    sgn = np.where((np.arange(128) % 64) < 32, -1.0, 1.0).astype(f)
    for sb in range(nsb):
        L = lay["sb"][sb]
        posv = np.zeros(tbm, np.float32)
        for (c0, cc, kind, g0) in L["chunks"]:
            if kind == "p":
                reset[sb, :, c0] = 0.0
                posv[c0:c0 + cc] = np.arange(g0, g0 + cc)
            else:
                for s in range(16):
                    reset[sb, :, c0 + 4 * s] = 0.0
                posv[c0:c0 + cc] = 16384 + (np.arange(64) % 4)
        ang = (posv[None, :].astype(np.float32) * fr[:, None]).astype(np.float32)
        cos[sb] = np.cos(ang)
        sin[sb] = np.sin(ang) * sgn[:, None]
    c["reset"], c["cos"], c["sin"] = reset, cos, sin
    return c


def layout(cfg):
    nch, nsb = cfg.nch, cfg.nsb
    tp = 16 + 64 * nch
    per = nch // nsb
    sbs = []
    for sb in range(nsb):
        chunks = []
        col = 0
        p0 = 0 if sb == 0 else 16 + 64 * per * sb
        if sb == 0:
            chunks.append((0, 16, "p", 0))
            col = 16
        for i in range(per * sb, per * (sb + 1)):
            chunks.append((col, 64, "p", 16 + 64 * i))
            col += 64
        npc = col
        if sb == nsb - 1:
            chunks.append((col, 64, "s", tp))
            col += 64
        sbs.append({"chunks": chunks, "np": npc, "tbl": col, "p0": p0, "last": sb == nsb - 1})
    return {"sb": sbs, "tbmax": max(s["tbl"] for s in sbs), "tp": tp, "tt": tp + 64}


def build(cfg, dbg=False):
    nc = bass.Bass("TRN2", target_bir_lowering=False)
    nc.allow_non_contiguous_dma(reason="small strided parameter / state loads").__enter__()
    S = Sch(nc)
    DEPTH, NCH, NSB, NSEQ = cfg.depth, cfg.nch, cfg.nsb, cfg.nseq
    lay = layout(cfg)
    TP, TT, TBM = lay["tp"], lay["tt"], lay["tbmax"]
    SEQ = 64 * NCH

    def din(name, shape):
        return nc.dram_tensor(name, list(shape), F32, kind="ExternalInput").ap()

    def dout(name, shape):
        return nc.dram_tensor(name, list(shape), F32, kind="ExternalOutput").ap()

    xp_d = din("xp", [SEQ, D])
    xs_d = din("xs", [64, D])
    meta_d = din("meta_tokens", [16, D])
    sdelta_d = din("sdelta", [DEPTH, NSEQ, 4, 128, 128])
    sdconv_d = din("sdconv", [DEPTH, NSEQ * 3, 1536])
    slru_d = din("slru", [DEPTH, NSEQ, 512])
    slconv_d = din("slconv", [DEPTH, NSEQ * 3, 512])
    sgla_d = din("sgla", [DEPTH, NSEQ, 4, 64, 128])
    sret_d = din("sret", [DEPTH, NSEQ, 4, 64, 128])
    normw_d = din("norm_w", [DEPTH, D])
    win_d = din("w_in", [DEPTH, D, INW])
    conva_d = din("conv_a", [DEPTH, 4, 1536])
    alog_d = din("a_log", [DEPTH, 4])
    dtb_d = din("dt_bias", [DEPTH, 4])
    norma_d = din("norm_a", [DEPTH, 128])
    convb_d = din("conv_b", [DEPTH, 4, 512])
    convbb_d = din("conv_b_bias", [DEPTH, 512])
    lruwa_d = din("lru_wa", [DEPTH, 4, 128, 128])
    lruba_d = din("lru_ba", [DEPTH, 512])
    lruwx_d = din("lru_wx", [DEPTH, 4, 128, 128])
    lrubx_d = din("lru_bx", [DEPTH, 512])
    lrulam_d = din("lru_lambda", [DEPTH, 512])
    glaw2_d = din("gla_w2", [DEPTH, 16, 256])
    glab2_d = din("gla_b2", [DEPTH, 256])
    normc_d = din("norm_c", [DEPTH, 128])
    wout_d = din("w_out", [DEPTH, D, D])
    fnorm_d = din("final_norm", [D])
    hc = host_consts(cfg)
    cd = {k: din("c_" + k, v.shape) for k, v in hc.items()}

    yp_d = dout("y_p", [SEQ, D])
    ys_d = dout("y_s", [64, D])
    pdelta_d = dout("p_delta", [DEPTH, 4, 128, 128])
    pdconv_d = dout("p_dconv", [DEPTH, 3, 1536])
    plru_d = dout("p_lru", [DEPTH, 4, 128])
    plconv_d = dout("p_lconv", [DEPTH, 3, 512])
    pgla_d = dout("p_gla", [DEPTH, 4, 64, 128])
    pret_d = dout("p_ret", [DEPTH, 4, 64, 128])
    sdelta_o = dout("s_delta", [DEPTH, NSEQ, 4, 128, 128])
    sdconv_o = dout("s_dconv", [DEPTH, NSEQ * 3, 1536])
    slru_o = dout("s_lru", [DEPTH, NSEQ, 512])
    slconv_o = dout("s_lconv", [DEPTH, NSEQ * 3, 512])
    sgla_o = dout("s_gla", [DEPTH, NSEQ, 4, 64, 128])
    sret_o = dout("s_ret", [DEPTH, NSEQ, 4, 64, 128])
    xres = nc.dram_tensor("xres", [TT, D], F32, kind="Internal").ap()

    def sb(name, shape, dt=F32):
        return nc.alloc_sbuf_tensor(name, list(shape), dt)

    ident_f = sb("ident_f", [128, 128])
    ident_b = sb("ident_b", [128, 128], BF16)
    ones_b = sb("ones_b", [128, 128], BF16)
    perm_b = sb("perm_b", [128, 128], BF16)
    perm_f = sb("perm_f", [128, 128])
    cm = {}
    for k in ("negmask_p", "negmask_s", "strict_p", "strict_s", "incl_p", "incl_s"):
        cm[k] = sb("m_" + k, [128, 64])
    for k in ("retm_p", "retm_s"):
        cm[k] = sb("m_" + k, [128, 4, 64])
    cm["seqmask"] = sb("m_seqmask", [128, 16])
    seqmask_b = sb("seqmask_b", [128, 16], BF16)
    sel = sb("sel", [8, 8, 128])
    for k in ("fs64", "ts64", "fss", "tss"):
        cm[k] = sb("m_" + k, [128, 2, 64])
    cm["ts16"] = sb("m_ts16", [128, 2, 16])
    cm["retdec"] = sb("m_retdec", [128, 2, 3])
    reset_t = sb("reset_t", [128, TBM])
    cos_t = sb("cos_t", [128, TBM])
    sin_t = sb("sin_t", [128, TBM])
    NPC = 100 * DEPTH + 8 * DEPTH + 8
    pcols = sb("pcols", [128, NPC])
    prm8 = sb("prm8", [8, DEPTH, 2])
    alg8 = sb("alg8", [8, DEPTH])
    xnT = sb("xnT", [128, KT, TBM], BF16)
    mixT = sb("mixT", [128, KT, TBM], BF16)
    NST = 2
    wstg = [sb("wstg%d" % i, [128, KT, 128]) for i in range(NST)]
    NWB = 3
    wbf = [sb("wbf%d" % i, [128, KT, 128], BF16) for i in range(NWB)]
    wo2 = [sb("wo%d" % i, [128, KT, 256], BF16) for i in range(2)]
    sstg = [sb("sstg%d" % i, [128, 6, 128]) for i in range(2)]
    Sd = sb("Sd", [128, 4, 128])
    Sdb = sb("Sdb", [128, 4, 128], BF16)
    Sg = sb("Sg", [128, 2, 128])
    Sgb = sb("Sgb", [128, 2, 128], BF16)
    Sr = sb("Sr", [128, 2, 128])
    Srb = sb("Srb", [128, 2, 128], BF16)
    hl = sb("hl", [128, 4])
    carryA = sb("carryA", [128, 12, 3])
    carryB = sb("carryB", [128, 4, 3])
    lwa = sb("lwa", [128, 4, 128], BF16)
    lwx = sb("lwx", [128, 4, 128], BF16)
    w2b = sb("w2b", [16, 256], BF16)
    ga = Arena(nc, "garena", cfg.ga_bytes if hasattr(cfg, "ga_bytes") else 86 * 1024)

    pbanks = [nc.alloc_psum_tensor("pb%d" % i, [128, 512], F32) for i in range(8)]
    pstate = {"i": 0}

    def ps():
        b = pbanks[pstate["i"] % 8]
        pstate["i"] += 1
        return b

    evs = {"i": 0}

    def evac(out, in_):
        evs["i"] += 1
        S.copy(out, in_, e=("act" if evs["i"] % 2 else "dve"))

    def CP(k):
        if cfg.stop == k:
            raise _Stop()

    try:
        S.dma(ident_f[:], cd["ident"])
        S.copy(ident_b[:], ident_f[:], e="dve")
        S.memset(ones_b[:], 1.0)
        S.dma(perm_f[:], cd["perm"])
        S.copy(perm_b[:], perm_f[:], e="dve")
        for k in cm:
            if cm[k].shape[0] == 128 and cd[k].shape[0] == 64:
                S.dma(cm[k][0:64], cd[k])
                S.dma(cm[k][64:128], cd[k])
            else:
                S.dma(cm[k][:], cd[k])
        S.copy(seqmask_b[:], cm["seqmask"][:], e="dve")
        S.dma(sel[:], cd["sel"])

        CP(1)
        pc = {"n": 0}

        def load_cols(dram2d, rows):
            base = pc["n"]
            r0 = 0
            while r0 < rows:
                r = min(128, rows - r0)
                stg = sstg[0][0:r, 0, :]
                S.dma(stg, dram2d[r0:r0 + r, :])
                pb = ps()
                S.transpose(pb[:, 0:r], stg, ident_f[0:r, 0:r])
                evac(pcols[:, base + r0: base + r0 + r], pb[:, 0:r])
                r0 += r
            pc["n"] += rows
            return base

        P_NORMW = load_cols(normw_d.rearrange("l (a p) -> (l a) p", p=128), DEPTH * 16)
        P_CONVA = load_cols(conva_d.rearrange("l k (a p) -> (l k a) p", p=128), DEPTH * 48)
        P_CONVB = load_cols(convb_d.rearrange("l k (a p) -> (l k a) p", p=128), DEPTH * 16)
        P_CONVBB = load_cols(convbb_d.rearrange("l (a p) -> (l a) p", p=128), DEPTH * 4)
        P_LBA = load_cols(lruba_d.rearrange("l (a p) -> (l a) p", p=128), DEPTH * 4)
        P_LBX = load_cols(lrubx_d.rearrange("l (a p) -> (l a) p", p=128), DEPTH * 4)
        P_LAM = load_cols(lrulam_d.rearrange("l (a p) -> (l a) p", p=128), DEPTH * 4)
        P_B2 = load_cols(glab2_d.rearrange("l (a p) -> (l a) p", p=128), DEPTH * 2)
        P_NA = load_cols(norma_d, DEPTH)
        P_NC = load_cols(normc_d, DEPTH)
        P_NSP8 = pc["n"]
        pc["n"] += DEPTH * 4
        P_NB2 = pc["n"]
        pc["n"] += DEPTH * 2
        assert pc["n"] <= NPC
        lamc = pcols[:, P_LAM:P_LAM + DEPTH * 4]
        nsp = pcols[:, P_NSP8:P_NSP8 + DEPTH * 4]
        S.act(nsp, lamc, AF.Exp, scale=-1.0)
        S.act(nsp, nsp, AF.Ln, bias=1.0)
        S.ts(nsp, nsp, -8.0, op0=ALU.mult)
        S.ts(pcols[:, P_NB2:P_NB2 + DEPTH * 2], pcols[:, P_B2:P_B2 + DEPTH * 2], -1.0, op0=ALU.mult)
        S.memset(prm8[:], 0.0)
        S.memset(alg8[:], 0.0)
        S.dma(prm8[0:4, :, 0], dtb_d.rearrange("l h -> h l"))
        S.dma(alg8[0:4, :], alog_d.rearrange("l h -> h l"))
        S.act(alg8[:], alg8[:], AF.Exp)
        S.ts(prm8[:, :, 1], alg8[:], -1.0, op0=ALU.mult)

        CP(2)
        S.dma(xres[0:16, :], meta_d)
        R = 0
        while R < SEQ:
            r = min(512, SEQ - R)
            S.dma(xres[16 + R:16 + R + r, :], xp_d[R:R + r, :])
            R += r
        S.dma(xres[TP:TP + 64, :], xs_d)

        CP(3)
        WSEQ = []
        nwb_ = 0
        ncb_ = 0
        for l_ in range(DEPTH):
            for sb_ in range(NSB):
                blocks = [(C_QKV + j * 128, 128) for j in range(12)] + [(C_AL, 8)] + [(C_GATE + m * 128, 128) for m in range(4)]
                for n_ in range(4):
                    blocks += [(C_XB + n_ * 128, 128), (C_GATE + (4 + n_) * 128, 128)]
                blocks += [(C_RC, 16)]
                for t in range(2):
                    blocks += [(C_QC + t * 128, 128), (C_KC + t * 128, 128)]
                blocks += [(C_VC + h * 128, 128) for h in range(4)] + [(C_GATE + (8 + h) * 128, 128) for h in range(4)]
                blocks += [(C_QD + t * 128, 128) for t in range(2)] + [(C_KD + t * 128, 128) for t in range(2)]
                blocks += [(C_VD + h * 128, 128) for h in range(4)] + [(C_GATE + (12 + h) * 128, 128) for h in range(4)]
                for (c0_, nco_) in blocks:
                    WSEQ.append(("in", l_, c0_, nco_, nwb_ % NWB, 0))
                    nwb_ += 1
                for cb in range(8):
                    for q in range(2):
                        WSEQ.append(("out", l_, cb * 256 + q * 128, 128, ncb_ % 2, q))
                    ncb_ += 1
        wst = {"dma": 0, "cast": 0, "used": 0}

        def w_dest(k):
            kind, l_, c0_, nco_, slot, q = WSEQ[k]
            if kind == "in":
                return wbf[slot][:, :, 0:nco_]
            return wo2[slot][:, :, q * 128:(q + 1) * 128]

        def w_dma(k):
            kind, l_, c0_, nco_, slot, q = WSEQ[k]
            src = (win_d if kind == "in" else wout_d)[l_]
            S.dma(wstg[k % NST][:, :, 0:nco_], src[:, c0_:c0_ + nco_].rearrange("(kt p) c -> p kt c", p=128))

        def w_cast(k):
            nco_ = WSEQ[k][3]
            S.copy(w_dest(k), wstg[k % NST][:, :, 0:nco_], e=("act" if (WSEQ[k][0] == "out" and k % 2) else "dve"))

        def w_get(kind, l_, c0_, nco_):
            k = wst["used"]
            assert WSEQ[k][0:4] == (kind, l_, c0_, nco_), (k, WSEQ[k], kind, l_, c0_, nco_)
            while True:
                prog = False
                j = wst["dma"]
                if j < min(k + 1 + LOOKD, len(WSEQ)) and wst["cast"] > j - NST:
                    w_dma(j)
                    wst["dma"] += 1
                    prog = True
                j = wst["cast"]
                if j < min(k + 1 + LOOKC, len(WSEQ)) and j < wst["dma"]:
                    w_cast(j)
                    wst["cast"] += 1
                    prog = True
                if not prog:
                    break
            assert wst["cast"] > k
            wst["used"] += 1
            return WSEQ[k]

        def load_w(w2d, c0, ncols, scale_base=None, rows=KT):
            e_ = w_get("in", l, c0, ncols)
            return wbf[e_[4]]

        def ttiles(L):
            res = []
            t0 = 0
            while t0 < L["np"]:
                n = min(512, L["np"] - t0)
                res.append((t0, n, "p"))
                t0 += n
            if L["last"]:
                res.append((L["np"], 64, "s"))
            return res

        pend = {"f": None, "f2": None}

        def _drain_one():
            f, f2 = pend["f"], pend["f2"]
            pend["f"] = None
            pend["f2"] = None
            if f is not None:
                pend["f2"] = f()
            if f2 is not None:
                f2()

        def flush():
            _drain_one()
            _drain_one()

        def project(wb, M, L, consumer, post=None, post2=None):
            banks = []
            for (t0, n, kind) in ttiles(L):
                pb = ps()
                for kt in range(KT):
                    S.mm(pb[0:M, 0:n], wb[:, kt, 0:M], xnT[:, kt, t0:t0 + n], start=(kt == 0), stop=(kt == KT - 1))
                banks.append((pb, t0, n, kind))
            _drain_one()

            def epi():
                for (pb, t0, n, kind) in banks:
                    consumer(pb, t0, n, kind)
                if post is not None:
                    post()
                return post2

            pend["f"] = epi

        def bview(ap2, a, b):
            return ap2.rearrange("p (a b) -> p a b", a=a)

        def chunk_last(dst, src, L):
            npc = L["np"]
            c0 = 0
            if L["chunks"][0][1] == 16:
                S.copy(dst[:, 0:16], src[:, 15:16].to_broadcast([dst.shape[0], 16]), e="dve")
                c0 = 16
            n64 = (npc - c0) // 64
            if n64 > 0:
                sv = bview(src[:, c0:npc], n64, 64)
                S.copy(bview(dst[:, c0:npc], n64, 64), sv[:, :, 63:64].to_broadcast([dst.shape[0], n64, 64]), e="dve")
            if L["last"]:
                sv = bview(src[:, npc:npc + 64], 16, 4)
                S.copy(bview(dst[:, npc:npc + 64], 16, 4), sv[:, :, 3:4].to_broadcast([dst.shape[0], 16, 4]), e="dve")

        def chunk_last_cols(L):
            cols = []
            for (c0, cc, kind, g0) in L["chunks"]:
                if kind == "p":
                    cols.append(c0 + cc - 1)
                else:
                    cols += [c0 + 4 * s + 3 for s in range(16)]
            return cols

        def compact_last(dst, src, L):
            k = 0
            npc = L["np"]
            c0 = 0
            if L["chunks"][0][1] == 16:
                S.copy(dst[:, 0:1], src[:, 15:16], e="dve")
                k = 1
                c0 = 16
            n64 = (npc - c0) // 64
            if n64 > 0:
                S.copy(dst[:, k:k + n64], bview(src[:, c0:npc], n64, 64)[:, :, 63], e="dve")
                k += n64
            if L["last"]:
                S.copy(dst[:, k:k + 16], bview(src[:, npc:npc + 64], 16, 4)[:, :, 3], e="dve")
                k += 16
            return k

        def conv_block(prew, presw, acc, wbase, L, bias_col=None):
            npc = L["np"]
            w = [pcols[:, wbase + k:wbase + k + 1] for k in range(4)]
            if bias_col is None:
                S.act(acc[:, 0:npc], prew[:, 3:3 + npc], AF.Identity, scale=w[3])
            else:
                S.act(acc[:, 0:npc], prew[:, 3:3 + npc], AF.Identity, scale=w[3], bias=bias_col)
            for k in (2, 1, 0):
                S.stt(acc[:, 0:npc], prew[:, k:k + npc], w[k], acc[:, 0:npc], ALU.mult, ALU.add)
            if L["last"]:
                a3 = bview(acc[:, npc:npc + 64], 16, 4)
                if bias_col is None:
                    S.act(a3, presw[:, :, 3:7], AF.Identity, scale=w[3])
                else:
                    S.act(a3, presw[:, :, 3:7], AF.Identity, scale=w[3], bias=bias_col)
                for k in (2, 1, 0):
                    S.stt(a3, presw[:, :, k:k + 4], w[k], a3, ALU.mult, ALU.add)

        def out_rows(dst2d, src, ncols):
            pb = ps()
            S.transpose(pb[0:ncols, 0:128], src, ident_f[:, :])
            tmp = ga.alloc([128], F32)
            evac(tmp[0:ncols, :], pb[0:ncols, 0:128])
            S.dma(dst2d, tmp[0:ncols, :], is_out=True)

        for l in range(DEPTH):
            wl = win_d[l]
            S.dma(sstg[0][:, 0:4, :], lruwa_d[l].rearrange("n c d -> c n d"))
            S.copy(lwa[:], sstg[0][:, 0:4, :], e="pool")
            S.dma(sstg[1][:, 0:4, :], lruwx_d[l].rearrange("n c d -> c n d"))
            S.copy(lwx[:], sstg[1][:, 0:4, :], e="pool")
            S.dma(sstg[0][0:16, 4, :], glaw2_d[l][:, 0:128])
            S.dma(sstg[0][0:16, 5, :], glaw2_d[l][:, 128:256])
            S.copy(w2b[:].rearrange("p (a b) -> p a b", a=2), sstg[0][0:16, 4:6, :], e="pool")
            S.memset(Sd[:], 0.0)
            S.memset(Sdb[:], 0.0)
            S.memset(Sg[:], 0.0)
            S.memset(Sgb[:], 0.0)
            S.memset(Sr[:], 0.0)
            S.memset(Srb[:], 0.0)
            S.memset(hl[:], 0.0)
            S.memset(carryA[:], 0.0)
            S.memset(carryB[:], 0.0)

            for sbi in range(NSB):
                L = lay["sb"][sbi]
                npc, TBL, last = L["np"], L["tbl"], L["last"]
                if NSB > 1 or l == 0:
                    S.dma(reset_t[:, 0:TBM], cd["reset"][sbi])
                    S.dma(cos_t[:, 0:TBM], cd["cos"][sbi])
                    S.dma(sin_t[:, 0:TBM], cd["sin"][sbi])
                gm0 = ga.mark()

                rows = []
                r = 0
                while r < npc:
                    n = min(128, npc - r)
                    rows.append((L["p0"] + r, n, r))
                    r += n
                if last:
                    rows.append((TP, 64, npc))

                m0 = ga.mark()
                xt2 = [ga.alloc([D], F32) for _ in range(2)]
                xs2 = [ga.alloc([D], BF16) for _ in range(2)]
                junk = ga.alloc([D], BF16)
                st = ga.alloc([8], F32)
                for ti, (r0, n, c0) in enumerate(rows):
                    xt = xt2[ti % 2]
                    xs = xs2[ti % 2]
                    S.dma(xt[0:n, :], xres[r0:r0 + n, :])
                    S.act(junk[0:n, :], xt[0:n, :], AF.Square, accum_out=st[0:n, 0:1])
                    S.act(st[0:n, 1:2], st[0:n, 0:1], AF.Sqrt, scale=1.0 / D, bias=EPS)
                    S.recip(st[0:n, 2:3], st[0:n, 1:2])
                    S.act(xs[0:n, :], xt[0:n, :], AF.Identity, scale=st[0:n, 2:3])
                    for half in range(2):
                        pb = ps()
                        pbb = pb[:, :].bitcast(BF16)
                        for j in range(8):
                            kt = half * 8 + j
                            S.transpose(pbb[:, j * 128:j * 128 + n], xs[0:n, kt * 128:(kt + 1) * 128], ident_b[0:n, 0:n],
                                        inc=(j == 7))
                        S.tt(xnT[:, half * 8:half * 8 + 8, c0:c0 + n], pbb.rearrange("p (a b) -> p a b", a=8)[:, :, 0:n],
                             pcols[:, P_NORMW + l * 16 + half * 8:P_NORMW + l * 16 + half * 8 + 8].unsqueeze(2)
                             .to_broadcast([128, 8, n]), ALU.mult)
                ga.reset(m0)

                CP(4)
                nwcol = P_NORMW + l * 16

                gcount = {"i": 0}

                def gate_mix(mt, o_ap, normcol, do_norm):
                    wb = load_w(wl, C_GATE + mt * 128, 128, nwcol)
                    mk = ga.mark()
                    SGs = [ga.alloc([TBL], F32) for _ in range(2)]
                    RSs = [ga.alloc([TBL], F32) for _ in range(2)]
                    SQg = ga.alloc([TBL], BF16)
                    TMg = ga.alloc([TBL], F32)
                    par = gcount["i"] % 2
                    gcount["i"] += 1
                    SG, RS = SGs[par], RSs[par]

                    def cons(pb, t0, n, kind):
                        S.act(SG[:, t0:t0 + n], pb[:, 0:n], AF.Silu)
                        o = o_ap[:, t0:t0 + n]
                        if do_norm:
                            S.tt(SQg[:, t0:t0 + n], o, o, ALU.mult, e="pool")
                            pq = ps()
                            S.mm(pq[:, 0:n], ones_b[:, :], SQg[:, t0:t0 + n])
                            S.act(RS[:, t0:t0 + n], pq[:, 0:n], AF.Sqrt, scale=1.0 / 128, bias=EPS)
                        else:
                            S.tt(mixT[:, mt, t0:t0 + n], o, SG[:, t0:t0 + n], ALU.mult)

                    def post2():
                        if not do_norm:
                            return
                        S.recip(RS[:, 0:TBL], RS[:, 0:TBL], fast=True)
                        S.tt(TMg[:, 0:TBL], o_ap[:, 0:TBL], RS[:, 0:TBL], ALU.mult)
                        if normcol is not None:
                            S.stt(mixT[:, mt, 0:TBL], TMg[:, 0:TBL], normcol, SG[:, 0:TBL], ALU.mult, ALU.mult)
                        else:
                            S.tt(mixT[:, mt, 0:TBL], TMg[:, 0:TBL], SG[:, 0:TBL], ALU.mult)

                    project(wb, 128, L, cons, None, post2)
                    ga.reset(mk)

                mA = ga.mark()
                QN = ga.alloc([4, TBL], BF16)
                KN = ga.alloc([4, TBL], BF16)
                BKN = ga.alloc([4, TBL], BF16)
                KQG = ga.alloc([4, 2, TBL], BF16)
                BKT = ga.alloc([4, TBL], BF16)
                VT = ga.alloc([4, TBL], BF16)
                OT = ga.alloc([4, TBL], BF16)
                mA2 = ga.mark()
                pre2 = [ga.alloc([3 + max(npc, 1)], F32) for _ in range(2)]
                pres2 = [ga.alloc([16, 7], F32) for _ in range(2)]
                acc2 = [ga.alloc([TBL], F32) for _ in range(2)]
                sqb = ga.alloc([TBL], BF16)
                rinA = [ga.alloc([TBL], F32) for _ in range(2)]
                cst = ga.alloc([12, 48], F32)
                if last:
                    for half in range(2):
                        stg = sstg[half][0:48, 0:6, :]
                        S.dma(stg, sdconv_d[l][:, half * 768:(half + 1) * 768].rearrange("r (a p) -> r a p", p=128))
                        pb = ps()
                        for a in range(6):
                            S.transpose(pb[:, a * 48:(a + 1) * 48], sstg[half][0:48, a, :], ident_f[0:48, 0:48], inc=(a == 5))
                        evac(cst[:, half * 6:half * 6 + 6, :], pb[:, 0:288].rearrange("p (a b) -> p a b", a=6))
                for j in range(12):
                    which, h = j // 4, j % 4
                    wb = load_w(wl, C_QKV + j * 128, 128, nwcol)
                    prew, presw, acc = pre2[j % 2], pres2[j % 2], acc2[j % 2]
                    S.copy(prew[:, 0:3], carryA[:, j, :], e="pool")
                    if last:
                        S.copy(presw[:, :, 0:3], cst[:, j, :].rearrange("p (s r) -> p s r", r=3), e="pool")

                    def cons(pb, t0, n, kind, prew=prew, presw=presw):
                        if kind == "p":
                            evac(prew[:, 3 + t0:3 + t0 + n], pb[:, 0:n])
                        else:
                            evac(presw[:, :, 3:7], pb[:, 0:64].rearrange("p (s t) -> p s t", t=4))

                    def post(j=j, which=which, h=h, prew=prew, presw=presw, acc=acc):
                        wcolk = [pcols[:, P_CONVA + (l * 4 + k) * 12 + j: P_CONVA + (l * 4 + k) * 12 + j + 1] for k in range(4)]
                        if npc > 0:
                            S.ts(acc[:, 0:npc], prew[:, 3:3 + npc], wcolk[3], op0=ALU.mult)
                            for k in (2, 1, 0):
                                S.stt(acc[:, 0:npc], prew[:, k:k + npc], wcolk[k], acc[:, 0:npc], ALU.mult, ALU.add)
                            S.copy(carryA[:, j, :], prew[:, npc:npc + 3], e="pool")
                        if last:
                            a3 = bview(acc[:, npc:npc + 64], 16, 4)
                            S.ts(a3, presw[:, :, 3:7], wcolk[3], op0=ALU.mult)
                            for k in (2, 1, 0):
                                S.stt(a3, presw[:, :, k:k + 4], wcolk[k], a3, ALU.mult, ALU.add)
                            S.copy(cst[:, j, :].rearrange("p (s r) -> p s r", r=3), presw[:, :, 4:7], e="pool")
                        if which == 2:
                            S.act(VT[:, h, :], acc[:, 0:TBL], AF.Silu)
                        else:
                            S.act(acc[:, 0:TBL], acc[:, 0:TBL], AF.Silu)
                            S.tt(sqb[:, 0:TBL], acc[:, 0:TBL], acc[:, 0:TBL], ALU.mult, e="pool")
                            dst = QN if which == 0 else KN
                            sc_ = 128.0 if which == 0 else 1.0
                            for (t0, n, kind) in ttiles(L):
                                pq = ps()
                                S.mm(pq[:, 0:n], ones_b[:, :], sqb[:, t0:t0 + n])
                                S.act(rinA[j % 2][:, t0:t0 + n], pq[:, 0:n], AF.Sqrt, scale=sc_, bias=sc_ * EPS)

                    def post2(j=j, which=which, h=h, acc=acc):
                        if which == 2:
                            return
                        dst = QN if which == 0 else KN
                        rn = rinA[j % 2]
                        S.recip(rn[:, 0:TBL], rn[:, 0:TBL], fast=True)
                        S.tt(dst[:, h, 0:TBL], acc[:, 0:TBL], rn[:, 0:TBL], ALU.mult)

                    project(wb, 128, L, cons, post, post2)
                flush()
                if last:
                    for half in range(2):
                        tmp = ga.alloc([768], F32)
                        pb = ps()
                        for a in range(4):
                            S.transpose(pb[0:48, a * 128:(a + 1) * 128], cst[:, half * 6 + a, :], ident_f[:, :], inc=(a == 3))
                        evac(tmp[0:48, 0:512], pb[0:48, 0:512])
                        pb2 = ps()
                        for a in range(2):
                            S.transpose(pb2[0:48, a * 128:(a + 1) * 128], cst[:, half * 6 + 4 + a, :], ident_f[:, :], inc=(a == 1))
                        evac(tmp[0:48, 512:768], pb2[0:48, 0:256])
                        S.dma(sdconv_o[l][:, half * 768:(half + 1) * 768], tmp[0:48, :], is_out=True)
                    pb = ps()
                    S.transpose(pb[0:36, 0:128], carryA[:, :, :].rearrange("p a r -> p (a r)"), ident_f[:, :])
                    tmp = ga.alloc([128], F32)
                    evac(tmp[0:36, :], pb[0:36, 0:128])
                    for a in range(12):
                        S.dma(pdconv_d[l][:, a * 128:(a + 1) * 128], tmp[a * 3:a * 3 + 3, :], is_out=True)
                ga.reset(mA2)

                CP(5)
                GC = ga.alloc([TBL], F32)
                nlast = len(chunk_last_cols(L))
                GLc = ga.alloc([nlast], F32)
                decS = ga.alloc([4, nlast], F32)
                colG = ga.alloc([len(L["chunks"]), 8], F32)
                mAs = ga.mark()
                AB = ga.alloc([TBL], F32)
                Bt = ga.alloc([TBL], F32)
                G = ga.alloc([TBL], F32)
                GL = ga.alloc([TBL], F32)
                EKT = ga.alloc([TBL], F32)
                EG = ga.alloc([512], F32)
                wb = load_w(wl, C_AL, 8, nwcol)

                def cons(pb, t0, n, kind):
                    evac(AB[0:8, t0:t0 + n], pb[0:8, 0:n])

                project(wb, 8, L, cons)
                flush()
                A8, B8, G8, GC8, GL8, EK8 = AB[0:8, :], Bt[0:8, :], G[0:8, :], GC[0:8, :], GL[0:8, :], EKT[0:8, :]
                S.act(B8, A8, AF.Sigmoid)
                S.act(G8, A8, AF.Exp, bias=prm8[:, l, 0:1])
                S.act(G8, G8, AF.Ln, bias=1.0)
                S.ts(G8, G8, prm8[:, l, 1:2], op0=ALU.mult)
                S.scan(GC8, reset_t[0:8, 0:TBL], G8, 0.0)
                chunk_last(GL8, GC8, L)
                S.tt(EK8, GL8, GC8, ALU.subtract)
                S.act(EK8, EK8, AF.Exp)
                compact_last(GLc[0:8, :], GC8, L)
                for h in range(4):
                    pb = ps()
                    S.mm(pb[:, 0:nlast], sel[:, h, :], GLc[0:8, :])
                    S.act(decS[:, h, :], pb[:, 0:nlast], AF.Exp)
                for (t0, n, kind) in ttiles(L):
                    for h in range(4):
                        pg = ps()
                        S.mm(pg[:, 0:n], sel[:, h, :], GC8[:, t0:t0 + n])
                        S.act(EG[:, 0:n], pg[:, 0:n], AF.Exp)
                        S.tt(KQG[:, h, 0, t0:t0 + n], KN[:, h, t0:t0 + n], EG[:, 0:n], ALU.mult)
                        S.tt(KQG[:, h, 1, t0:t0 + n], QN[:, h, t0:t0 + n], EG[:, 0:n], ALU.mult)
                        pbb = ps()
                        S.mm(pbb[:, 0:n], sel[:, 4 + h, :], B8[:, t0:t0 + n])
                        S.tt(BKN[:, h, t0:t0 + n], KN[:, h, t0:t0 + n], pbb[:, 0:n], ALU.mult)
                        pe_ = ps()
                        S.mm(pe_[:, 0:n], sel[:, h, :], EK8[:, t0:t0 + n])
                        S.tt(BKT[:, h, t0:t0 + n], BKN[:, h, t0:t0 + n], pe_[:, 0:n], ALU.mult)
                for ci, (c0, cc, kind, g0) in enumerate(L["chunks"]):
                    pb = ps()
                    S.transpose(pb[0:cc, 0:8], GC8[:, c0:c0 + cc], ident_f[0:8, 0:8])
                    evac(colG[0:cc, ci, :], pb[0:cc, 0:8])

                CP(6)
                ga.reset(mAs)
                mA3 = ga.mark()
                nchk = len(L["chunks"])
                RT = ga.alloc([4, 64], BF16)
                OIN = ga.alloc([4, 64], F32)
                Rtok = ga.alloc([4, 128], BF16)
                Wb = ga.alloc([4, 128], BF16)

                def mkset():
                    d = {}
                    d["ARG"] = ga.alloc([4, 64], F32)
                    d["EI"] = ga.alloc([4, 64], F32)
                    d["ES"] = ga.alloc([4, 64], F32)
                    d["X"] = [ga.alloc([4, 64], F32) for _ in range(2)]
                    d["XT"] = [ga.alloc([4, 64], F32) for _ in range(2)]
                    d["PP"] = [ga.alloc([4, 64], F32) for _ in range(2)]
                    d["Xb"] = [ga.alloc([4, 64], BF16) for _ in range(2)]
                    d["XTb"] = [ga.alloc([4, 64], BF16) for _ in range(2)]
                    d["PPb"] = ga.alloc([4, 64], BF16)
                    d["AT"] = ga.alloc([4, 64], BF16)
                    d["TTb"] = ga.alloc([4, 64], BF16)
                    d["BKtok"] = ga.alloc([4, 128], BF16)
                    return d

                psets = [None, None]
                psets[(nchk + 1) % 2] = mkset()
                msamp = ga.mark()
                psets[nchk % 2] = mkset()

                def prep_gen(ci):
                    c0, c, kind, g0 = L["chunks"][ci]
                    isS = (kind == "s")
                    negm = cm["negmask_s"] if isS else cm["negmask_p"]
                    strm = cm["strict_s"] if isS else cm["strict_p"]
                    levels = 2 if isS else (6 if c == 64 else 4)
                    d = psets[ci % 2]
                    ARG, EI, ES, X, XT, PP, AT, TTb, BKtok = (d["ARG"], d["EI"], d["ES"], d["X"], d["XT"], d["PP"],
                                                              d["AT"], d["TTb"], d["BKtok"])
                    pg = ps()
                    for h in range(4):
                        S.mm(pg[0:c, h * 64:h * 64 + c], sel[:, h, 0:c], GC8[:, c0:c0 + c])
                    pk = ps()
                    pq = ps()
                    for h in range(4):
                        S.mm(pk[0:c, h * 64:h * 64 + c], BKN[:, h, c0:c0 + c], KN[:, h, c0:c0 + c])
                    for h in range(4):
                        S.mm(pq[0:c, h * 64:h * 64 + c], BKN[:, h, c0:c0 + c], QN[:, h, c0:c0 + c])
                    pbt = ps()
                    pbtb = pbt[:, :].bitcast(BF16)
                    for h in range(4):
                        S.transpose(pbtb[0:c, h * 128:(h + 1) * 128], BKT[:, h, c0:c0 + c], ident_b[:, :], inc=(h == 3))
                    for h in range(4):
                        S.stt(ARG[0:c, h, 0:c], pg[0:c, h * 64:h * 64 + c], colG[0:c, ci, h:h + 1], negm[0:c, 0:c],
                              ALU.subtract, ALU.add)
                    S.act(EI[0:c, :, 0:c], ARG[0:c, :, 0:c], AF.Exp)
                    S.tt(ES[0:c, :, 0:c], EI[0:c, :, 0:c], strm[0:c, 0:c].unsqueeze(1).to_broadcast([c, 4, c]), ALU.mult,
                         e="pool")
                    pk3 = pk[0:c, 0:256].rearrange("p (h i) -> p h i", h=4)[:, :, 0:c]
                    pq3 = pq[0:c, 0:256].rearrange("p (h i) -> p h i", h=4)[:, :, 0:c]
                    S.tt(X[0][0:c, :, 0:c], pk3, ES[0:c, :, 0:c], ALU.mult)
                    S.tt(AT[0:c, :, 0:c], pq3, EI[0:c, :, 0:c], ALU.mult)
                    evac(BKtok[0:c, :, :], pbtb[0:c, 0:512].rearrange("p (h d) -> p h d", h=4))
                    yield
                    pt = ps()
                    for h in range(4):
                        S.transpose(pt[0:c, h * 64:h * 64 + c], X[0][0:c, h, 0:c], ident_f[0:c, 0:c], inc=(h == 3))
                    pt3 = pt[0:c, 0:256].rearrange("p (h i) -> p h i", h=4)[:, :, 0:c]
                    evac(XT[0][0:c, :, 0:c], pt3)
                    S.tt(PP[0][0:c, :, 0:c], X[0][0:c, :, 0:c], ident_f[0:c, 0:c].unsqueeze(1).to_broadcast([c, 4, c]),
                         ALU.add, e="pool")
                    Xb, XTb, PPb = d["Xb"], d["XTb"], d["PPb"]
                    cur = 0
                    nlev = levels - 1
                    NF32 = 1
                    v3 = lambda p_: p_[0:c, 0:256].rearrange("p (h i) -> p h i", h=4)[:, :, 0:c]
                    for lv in range(nlev):
                        nxt = 1 - cur
                        lastlv = (lv == nlev - 1)
                        lowp = (lv >= NF32)
                        nlow = (lv + 1 >= NF32) and not lastlv
                        Xc, XTc = (Xb[cur], XTb[cur]) if lowp else (X[cur], XT[cur])
                        yield
                        pxt = ps()
                        for h in range(4):
                            S.mm(pxt[0:c, h * 64:h * 64 + c], Xc[0:c, h, 0:c], XTc[0:c, h, 0:c])
                        if not lastlv:
                            px = ps()
                            for h in range(4):
                                S.mm(px[0:c, h * 64:h * 64 + c], XTc[0:c, h, 0:c], Xc[0:c, h, 0:c])
                        XTn = XTb[nxt] if lowp else XT[nxt]
                        evac(XTn[0:c, :, 0:c], v3(pxt))
                        if nlow and not lowp:
                            evac(XTb[nxt][0:c, :, 0:c], v3(pxt))
                        if not lastlv:
                            if lowp:
                                evac(Xb[nxt][0:c, :, 0:c], v3(px))
                            else:
                                if nlow:
                                    evac(Xb[nxt][0:c, :, 0:c], v3(px))
                                else:
                                    evac(X[nxt][0:c, :, 0:c], v3(px))
                        yield
                        pp = ps()
                        for h in range(4):
                            if lowp:
                                S.mm(pp[0:c, h * 64:h * 64 + c], XTb[nxt][0:c, h, 0:c], PPb[0:c, h, 0:c])
                            else:
                                S.mm(pp[0:c, h * 64:h * 64 + c], XT[nxt][0:c, h, 0:c], PP[cur][0:c, h, 0:c])
                        if lastlv:
                            S.tt(TTb[0:c, :, 0:c], v3(pp), PP[cur][0:c, :, 0:c], ALU.add)
                        else:
                            S.tt(PP[nxt][0:c, :, 0:c], v3(pp), PP[cur][0:c, :, 0:c], ALU.add)
                            if nlow:
                                S.copy(PPb[0:c, :, 0:c], PP[nxt][0:c, :, 0:c], e="act")
                        cur = nxt

                def chain_gen(ci):
                    c0, c, kind, g0 = L["chunks"][ci]
                    isS = (kind == "s")
                    d = psets[ci % 2]
                    AT, TTb, BKtok = d["AT"], d["TTb"], d["BKtok"]
                    if not isS:
                        ppq = ps()
                        for h in range(4):
                            S.mm(ppq[:, h * 128:h * 128 + 2 * c].rearrange("p (a b) -> p a b", a=2), Sdb[:, h, :],
                                 KQG[:, h, :, c0:c0 + c])
                        for h in range(4):
                            v = ppq[:, h * 128:h * 128 + 2 * c].rearrange("p (a b) -> p a b", a=2)
                            S.tt(RT[:, h, 0:c], VT[:, h, c0:c0 + c], v[:, 0, :], ALU.subtract)
                        for h in range(4):
                            v = ppq[:, h * 128:h * 128 + 2 * c].rearrange("p (a b) -> p a b", a=2)
                            S.copy(OIN[:, h, 0:c], v[:, 1, :], e="act")
                        yield
                        prt = ps()
                        prtb = prt[:, :].bitcast(BF16)
                        for h in range(4):
                            S.transpose(prtb[0:c, h * 128:(h + 1) * 128], RT[:, h, 0:c], ident_b[:, :], inc=(h == 3))
                        evac(Rtok[0:c, :, :], prtb[0:c, 0:512].rearrange("p (h d) -> p h d", h=4))
                        yield
                        pw = ps()
                        for h in range(4):
                            S.mm(pw[0:c, h * 128:(h + 1) * 128], TTb[0:c, h, 0:c], Rtok[0:c, h, :])
                        evac(Wb[0:c, :, :], pw[0:c, :].rearrange("p (h d) -> p h d", h=4))
                        yield
                        pS = ps()
                        for h in range(4):
                            S.mm(pS[:, h * 128:(h + 1) * 128], BKtok[0:c, h, :], Wb[0:c, h, :])
                        po = ps()
                        for h in range(4):
                            S.mm(po[:, h * 64:h * 64 + c], Wb[0:c, h, :], AT[0:c, h, 0:c])
                        S.tt(Sd[:, :, :], Sd[:, :, :], decS[:, :, ci:ci + 1].to_broadcast([128, 4, 128]), ALU.mult)
                        S.tt(Sd[:, :, :], Sd[:, :, :], pS[:, :].rearrange("p (h d) -> p h d", h=4), ALU.add)
                        S.copy(Sdb[:, :, :], Sd[:, :, :], e="act")
                        S.tt(OT[:, :, c0:c0 + c], po[:, 0:256].rearrange("p (h i) -> p h i", h=4)[:, :, 0:c],
                             OIN[:, :, 0:c], ALU.add)
                    else:
                        mk_ = ga.mark()
                        ga.reset(msamp)
                        kbase = len(L["chunks"]) - 1
                        Ss = ga.alloc([16, 128], F32)
                        Ssb = ga.alloc([16, 128], BF16)
                        Sn = ga.alloc([16, 128], F32)
                        BKm = ga.alloc([16, 128], BF16)
                        for h in range(4):
                            S.dma(Ss[:, :, :], sdelta_d[l][:, h].rearrange("s k v -> k s v"))
                            S.copy(Ssb[:, :, :], Ss[:, :, :], e="pool")
                            ppq = ps()
                            for s in range(16):
                                for a_ in range(2):
                                    S.mm(ppq[:, a_ * 64 + 4 * s:a_ * 64 + 4 * s + 4], Ssb[:, s, :],
                                         KQG[:, h, a_, c0 + 4 * s:c0 + 4 * s + 4])
                            v = ppq[:, 0:128].rearrange("p (a b) -> p a b", a=2)
                            S.tt(RT[:, h, :], VT[:, h, c0:c0 + 64], v[:, 0, :], ALU.subtract)
                            evac(OIN[:, h, :], v[:, 1, :])
                            prt = ps()
                            prtb = prt[:, :].bitcast(BF16)
                            S.transpose(prtb[0:64, 0:128], RT[:, h, :], ident_b[:, :])
                            evac(Rtok[0:64, h, :], prtb[0:64, 0:128])
                            pw = ps()
                            S.mm(pw[0:64, 0:128], TTb[0:64, h, :], Rtok[0:64, h, :])
                            evac(Wb[0:64, h, :], pw[0:64, 0:128])
                            po = ps()
                            S.mm(po[:, 0:64], Wb[0:64, h, :], AT[0:64, h, :])
                            S.tt(OT[:, h, c0:c0 + 64], po[:, 0:64], OIN[:, h, :], ALU.add)
                            S.tt(BKm[0:64, :, :], BKtok[0:64, h, :].unsqueeze(1).to_broadcast([64, 16, 128]),
                                 seqmask_b[0:64, :].unsqueeze(2).to_broadcast([64, 16, 128]), ALU.mult, e="pool")
                            S.tt(Sn[:, :, :], Ss[:, :, :],
                                 decS[:, h, kbase:kbase + 16].unsqueeze(2).to_broadcast([128, 16, 128]), ALU.mult, e="pool")
                            for q4 in range(4):
                                pS = ps()
                                for s4 in range(4):
                                    s = q4 * 4 + s4
                                    S.mm(pS[:, s4 * 128:(s4 + 1) * 128], BKm[0:64, s, :], Wb[0:64, h, :])
                                S.tt(Sn[:, q4 * 4:q4 * 4 + 4, :], Sn[:, q4 * 4:q4 * 4 + 4, :],
                                     pS[:, :].rearrange("p (s d) -> p s d", s=4), ALU.add)
                            S.dma(sdelta_o[l][:, h].rearrange("s k v -> k s v"), Sn[:, :, :], is_out=True)
                        ga.reset(mk_)

                def step(g):
                    if g is None:
                        return None
                    try:
                        next(g)
                        return g
                    except StopIteration:
                        return None

                g = prep_gen(0)
                while g is not None:
                    g = step(g)
                for ci in range(nchk):
                    gp = prep_gen(ci + 1) if ci + 1 < nchk else None
                    gc = chain_gen(ci)
                    while gp is not None or gc is not None:
                        for _ in range(3):
                            gp = step(gp)
                        gc = step(gc)
                ga.reset(mA3)
                if last:
                    for h in range(4):
                        S.dma(pdelta_d[l][h], Sd[:, h, :], is_out=True)
                CP(9)
                for h in range(4):
                    gate_mix(h, OT[:, h, :], pcols[:, P_NA + l:P_NA + l + 1], True)
                flush()
                ga.reset(mA)

                CP(10)
                mB = ga.mark()
                XB = ga.alloc([TBL], F32)
                XBb = ga.alloc([TBL], BF16)
                Rg = ga.alloc([TBL], F32)
                Ig = ga.alloc([TBL], F32)
                Hh = ga.alloc([TBL], F32)
                prew = ga.alloc([3 + max(npc, 1)], F32)
                presw = ga.alloc([16, 7], F32)
                cstb = ga.alloc([4, 48], F32)
                h0 = ga.alloc([4, 16], F32)
                hs_out = ga.alloc([4, 16], F32)
                tmp16 = ga.alloc([16], F32)
                if last:
                    stg = sstg[0][0:48, 0:4, :]
                    S.dma(stg, slconv_d[l].rearrange("r (a p) -> r a p", p=128))
                    pb = ps()
                    for a in range(4):
                        S.transpose(pb[:, a * 48:(a + 1) * 48], sstg[0][0:48, a, :], ident_f[0:48, 0:48], inc=(a == 3))
                    evac(cstb[:, :, :], pb[:, 0:192].rearrange("p (a b) -> p a b", a=4))
                    stg = sstg[1][0:16, 0:4, :]
                    S.dma(stg, slru_d[l].rearrange("s (a p) -> s a p", p=128))
                    pb = ps()
                    for a in range(4):
                        S.transpose(pb[:, a * 16:(a + 1) * 16], sstg[1][0:16, a, :], ident_f[0:16, 0:16], inc=(a == 3))
                    evac(h0[:, :, :], pb[:, 0:64].rearrange("p (a b) -> p a b", a=4))
                for n_ in range(4):
                    wb = load_w(wl, C_XB + n_ * 128, 128, nwcol)
                    S.copy(prew[:, 0:3], carryB[:, n_, :], e="pool")
                    if last:
                        S.copy(presw[:, :, 0:3], cstb[:, n_, :].rearrange("p (s r) -> p s r", r=3), e="pool")

                    def cons(pb, t0, n, kind):
                        if kind == "p":
                            evac(prew[:, 3 + t0:3 + t0 + n], pb[:, 0:n])
                        else:
                            evac(presw[:, :, 3:7], pb[:, 0:64].rearrange("p (s t) -> p s t", t=4))

                    def post(n_=n_):
                        wcolk = [pcols[:, P_CONVB + (l * 4 + k) * 4 + n_: P_CONVB + (l * 4 + k) * 4 + n_ + 1] for k in range(4)]
                        bcol = pcols[:, P_CONVBB + l * 4 + n_: P_CONVBB + l * 4 + n_ + 1]
                        if npc > 0:
                            S.act(XB[:, 0:npc], prew[:, 3:3 + npc], AF.Identity, scale=wcolk[3], bias=bcol)
                            for k in (2, 1, 0):
                                S.stt(XB[:, 0:npc], prew[:, k:k + npc], wcolk[k], XB[:, 0:npc], ALU.mult, ALU.add)
                            S.copy(carryB[:, n_, :], prew[:, npc:npc + 3], e="pool")
                        if last:
                            a3 = bview(XB[:, npc:npc + 64], 16, 4)
                            S.act(a3, presw[:, :, 3:7], AF.Identity, scale=wcolk[3], bias=bcol)
                            for k in (2, 1, 0):
                                S.stt(a3, presw[:, :, k:k + 4], wcolk[k], a3, ALU.mult, ALU.add)
                            S.copy(cstb[:, n_, :].rearrange("p (s r) -> p s r", r=3), presw[:, :, 4:7], e="pool")
                        S.copy(XBb[:, 0:TBL], XB[:, 0:TBL], e="act")
                        for (t0, n, kind) in ttiles(L):
                            pr = ps()
                            S.mm(pr[:, 0:n], lwa[:, n_, :], XBb[:, t0:t0 + n])
                            S.act(Rg[:, t0:t0 + n], pr[:, 0:n], AF.Sigmoid,
                                  bias=pcols[:, P_LBA + l * 4 + n_:P_LBA + l * 4 + n_ + 1])
                            pi = ps()
                            S.mm(pi[:, 0:n], lwx[:, n_, :], XBb[:, t0:t0 + n])
                            S.act(Ig[:, t0:t0 + n], pi[:, 0:n], AF.Sigmoid,
                                  bias=pcols[:, P_LBX + l * 4 + n_:P_LBX + l * 4 + n_ + 1])
                        S.act(Rg[:, 0:TBL], Rg[:, 0:TBL], AF.Exp, scale=pcols[:, P_NSP8 + l * 4 + n_:P_NSP8 + l * 4 + n_ + 1])
                        S.tt(Hh[:, 0:TBL], Rg[:, 0:TBL], Rg[:, 0:TBL], ALU.mult)
                        S.ts(Hh[:, 0:TBL], Hh[:, 0:TBL], -1.0, 1.0, op0=ALU.mult, op1=ALU.add, e="pool")
                        S.act(Hh[:, 0:TBL], Hh[:, 0:TBL], AF.Sqrt)
                        S.tt(Ig[:, 0:TBL], Ig[:, 0:TBL], Hh[:, 0:TBL], ALU.mult)
                        S.tt(Ig[:, 0:TBL], Ig[:, 0:TBL], XB[:, 0:TBL], ALU.mult, e="pool")
                        if last:
                            A3 = bview(Rg[:, npc:npc + 64], 16, 4)
                            B3 = bview(Ig[:, npc:npc + 64], 16, 4)
                            S.tt(tmp16[:, :], A3[:, :, 0], h0[:, n_, :], ALU.mult)
                            S.tt(B3[:, :, 0], B3[:, :, 0], tmp16[:, :], ALU.add)
                            S.memset(A3[:, :, 0], 0.0)
                        if npc > 0:
                            S.scan(Hh[:, 0:npc], Rg[:, 0:npc], Ig[:, 0:npc], hl[:, n_:n_ + 1])
                            S.copy(hl[:, n_:n_ + 1], Hh[:, npc - 1:npc], e="pool")
                        if last:
                            S.scan(Hh[:, npc:npc + 64], Rg[:, npc:npc + 64], Ig[:, npc:npc + 64], 0.0)
                            S.copy(hs_out[:, n_, :], bview(Hh[:, npc:npc + 64], 16, 4)[:, :, 3], e="pool")

                    project(wb, 128, L, cons, post)
                    gate_mix(4 + n_, Hh, None, False)
                flush()
                if last:
                    pb = ps()
                    S.transpose(pb[0:4, 0:128], hl[:, :], ident_f[:, :])
                    t4 = ga.alloc([128], F32)
                    evac(t4[0:4, :], pb[0:4, 0:128])
                    S.dma(plru_d[l], t4[0:4, :], is_out=True)
                    pb = ps()
                    for a in range(4):
                        S.transpose(pb[0:16, a * 128:(a + 1) * 128], hs_out[:, a, :], ident_f[:, :], inc=(a == 3))
                    t5 = ga.alloc([512], F32)
                    evac(t5[0:16, :], pb[0:16, :])
                    S.dma(slru_o[l], t5[0:16, :], is_out=True)
                    pb = ps()
                    for a in range(4):
                        S.transpose(pb[0:48, a * 128:(a + 1) * 128], cstb[:, a, :], ident_f[:, :], inc=(a == 3))
                    t6 = ga.alloc([512], F32)
                    evac(t6[0:48, :], pb[0:48, :])
                    S.dma(slconv_o[l], t6[0:48, :], is_out=True)
                    pb = ps()
                    S.transpose(pb[0:12, 0:128], carryB[:, :, :].rearrange("p a r -> p (a r)"), ident_f[:, :])
                    t7 = ga.alloc([128], F32)
                    evac(t7[0:12, :], pb[0:12, 0:128])
                    for a in range(4):
                        S.dma(plconv_d[l][:, a * 128:(a + 1) * 128], t7[a * 3:a * 3 + 3, :], is_out=True)
                ga.reset(mB)

                CP(11)
                for grp in ("C", "D"):
                    mC = ga.mark()
                    QA = ga.alloc([2, TBL], BF16)
                    QS_ = ga.alloc([2, TBL], BF16)
                    KA = ga.alloc([2, TBL], BF16)
                    KS_ = ga.alloc([2, TBL], BF16)
                    VT2 = ga.alloc([4, TBL], BF16)
                    OT2 = ga.alloc([4, TBL], BF16)
                    nlast = len(chunk_last_cols(L))
                    decC = ga.alloc([2, nlast], F32)
                    Sx, Sxb = (Sg, Sgb) if grp == "C" else (Sr, Srb)
                    sst_d, sst_o, pst_d = (sgla_d, sgla_o, pgla_d) if grp == "C" else (sret_d, sret_o, pret_d)
                    cq, ck, cv = (C_QC, C_KC, C_VC) if grp == "C" else (C_QD, C_KD, C_VD)
                    mC2 = ga.mark()
                    if grp == "C":
                        RCT = ga.alloc([TBL], BF16)
                        LT = ga.alloc([TBL], F32)
                        CS = ga.alloc([2, TBL], F32)
                        CSL = ga.alloc([2, TBL], F32)
                        EB = ga.alloc([2, TBL], F32)
                        EBN = ga.alloc([2, TBL], F32)
                        EKS = ga.alloc([2, TBL], F32)
                        CLc = ga.alloc([2, nlast], F32)
                        wb = load_w(wl, C_RC, 16, nwcol)

                        def cons(pb, t0, n, kind):
                            evac(RCT[0:16, t0:t0 + n], pb[0:16, 0:n])

                        project(wb, 16, L, cons)
                        flush()
                        for t in range(2):
                            for (t0, n, kind) in ttiles(L):
                                pz = ps()
                                S.mm(pz[:, 0:n], w2b[0:16, t * 128:(t + 1) * 128], RCT[0:16, t0:t0 + n])
                                S.act(LT[:, t0:t0 + n], pz[:, 0:n], AF.Exp, scale=-1.0,
                                      bias=pcols[:, P_NB2 + l * 2 + t:P_NB2 + l * 2 + t + 1])
                            S.act(LT[:, 0:TBL], LT[:, 0:TBL], AF.Ln, bias=1.0)
                            S.scan(CS[:, t, :], reset_t[:, 0:TBL], LT[:, 0:TBL], 0.0)
                            chunk_last(CSL[:, t, :], CS[:, t, :], L)
                            compact_last(CLc[:, t, :], CS[:, t, :], L)
                        S.act(EB[:, :, :], CS[:, :, :], AF.Exp, scale=-1.0 / 16)
                        S.act(EBN[:, :, :], CS[:, :, :], AF.Exp, scale=1.0 / 16)
                        S.tt(EKS[:, :, :], CS[:, :, :], CSL[:, :, :], ALU.subtract)
                        S.act(EKS[:, :, :], EKS[:, :, :], AF.Exp, scale=1.0 / 16)
                        S.act(decC[:, :, :], CLc[:, :, :], AF.Exp, scale=-1.0 / 16)
                        for t in range(2):
                            wb = load_w(wl, cq + t * 128, 128, nwcol)

                            def cons(pb, t0, n, kind, t=t):
                                S.stt(QA[:, t, t0:t0 + n], pb[:, 0:n], 0.125, EB[:, t, t0:t0 + n], ALU.mult, ALU.mult)

                            project(wb, 128, L, cons)
                            wb = load_w(wl, ck + t * 128, 128, nwcol)

                            def cons(pb, t0, n, kind, t=t):
                                S.tt(KA[:, t, t0:t0 + n], pb[:, 0:n], EBN[:, t, t0:t0 + n], ALU.mult)
                                S.tt(KS_[:, t, t0:t0 + n], pb[:, 0:n], EKS[:, t, t0:t0 + n], ALU.mult)

                            project(wb, 128, L, cons)
                        QSt = QA
                    else:
                        QRf = ga.alloc([TBL], F32)
                        T1 = ga.alloc([512], F32)
                        T2 = ga.alloc([512], F32)
                        Qb = ga.alloc([512], BF16)
                        for which in range(2):
                            for t in range(2):
                                wb = load_w(wl, (cq if which == 0 else ck) + t * 128, 128, nwcol)

                                def cons(pb, t0, n, kind):
                                    S.copy(Qb[:, 0:n], pb[:, 0:n], e="act")
                                    pm = ps()
                                    S.mm(pm[:, 0:n], perm_b[:, :], Qb[:, 0:n])
                                    S.tt(T1[:, 0:n], pb[:, 0:n], cos_t[:, t0:t0 + n], ALU.mult)
                                    S.tt(T2[:, 0:n], pm[:, 0:n], sin_t[:, t0:t0 + n], ALU.mult)
                                    S.tt(QRf[:, t0:t0 + n], T1[:, 0:n], T2[:, 0:n], ALU.add, e="pool")

                                def post(which=which, t=t):
                                    dA = QA if which == 0 else KA
                                    dS = QS_ if which == 0 else KS_
                                    evac(dA[:, t, :], QRf[:, 0:TBL])
                                    c0 = 0
                                    if L["chunks"][0][1] == 16:
                                        tb = cm["fs64"][:, t, 0:16] if which == 0 else cm["ts16"][:, t, :]
                                        S.tt(dS[:, t, 0:16], QRf[:, 0:16], tb, ALU.mult)
                                        c0 = 16
                                    n64 = (npc - c0) // 64
                                    if n64 > 0:
                                        tb = cm["fs64"][:, t, :] if which == 0 else cm["ts64"][:, t, :]
                                        S.tt(bview(dS[:, t, c0:npc], n64, 64), bview(QRf[:, c0:npc], n64, 64),
                                             tb.unsqueeze(1).to_broadcast([128, n64, 64]), ALU.mult)
                                    if last:
                                        tb = cm["fss"][:, t, :] if which == 0 else cm["tss"][:, t, :]
                                        S.tt(dS[:, t, npc:npc + 64], QRf[:, npc:npc + 64], tb, ALU.mult)

                                project(wb, 128, L, cons, post)
                        QSt = QS_
                    for h in range(4):
                        wb = load_w(wl, cv + h * 128, 128, nwcol)

                        def cons(pb, t0, n, kind, h=h):
                            evac(VT2[:, h, t0:t0 + n], pb[:, 0:n])

                        project(wb, 128, L, cons)
                    flush()
                    ga.reset(mC2)
                    QAm = ga.alloc([2, 2, TBL], BF16)
                    S.memset(QAm[:, :, :, :], 0.0)
                    for t in range(2):
                        evac(QAm[0:64, t, 0, :], QA[0:64, t, :])
                        evac(QAm[64:128, t, 1, :], QA[64:128, t, :])
                    if grp == "C":
                        QSm = QAm
                    else:
                        QSm = ga.alloc([2, 2, TBL], BF16)
                        S.memset(QSm[:, :, :, :], 0.0)
                        for t in range(2):
                            evac(QSm[0:64, t, 0, :], QS_[0:64, t, :])
                            evac(QSm[64:128, t, 1, :], QS_[64:128, t, :])
                    ATs = [ga.alloc([4, 64], BF16) for _ in range(2)]
                    Vtoks = [ga.alloc([4, 128], BF16) for _ in range(2)]
                    KStoks = [ga.alloc([2, 128], BF16) for _ in range(2)]
                    mC3 = ga.mark()

                    def stage1(ci):
                        c0, c, kind, g0 = L["chunks"][ci]
                        isS = (kind == "s")
                        AT, Vtok, KStok = ATs[ci % 2], Vtoks[ci % 2], KStoks[ci % 2]
                        pa = ps()
                        for h in range(4):
                            t, e_ = h // 2, h % 2
                            S.mm(pa[0:c, h * 64:h * 64 + c], KA[:, t, c0:c0 + c], QAm[:, t, e_, c0:c0 + c])
                        pa3 = pa[0:c, 0:256].rearrange("p (h i) -> p h i", h=4)[:, :, 0:c]
                        if grp == "C":
                            m_ = cm["incl_s"] if isS else cm["incl_p"]
                            S.tt(AT[0:c, :, 0:c], pa3, m_[0:c, 0:c].unsqueeze(1).to_broadcast([c, 4, c]), ALU.mult)
                        else:
                            m_ = cm["retm_s"] if isS else cm["retm_p"]
                            S.tt(AT[0:c, :, 0:c], pa3, m_[0:c, :, 0:c], ALU.mult)
                        pv = ps()
                        pvb = pv[:, :].bitcast(BF16)
                        for h in range(4):
                            S.transpose(pvb[0:c, h * 128:(h + 1) * 128], VT2[:, h, c0:c0 + c], ident_b[:, :], inc=(h == 3))
                        evac(Vtok[0:c, :, :], pvb[0:c, 0:512].rearrange("p (h d) -> p h d", h=4))
                        pk = ps()
                        pkb = pk[:, :].bitcast(BF16)
                        for t in range(2):
                            S.transpose(pkb[0:c, t * 128:(t + 1) * 128], KS_[:, t, c0:c0 + c], ident_b[:, :], inc=(t == 1))
                        evac(KStok[0:c, :, :], pkb[0:c, 0:256].rearrange("p (t d) -> p t d", t=2))

                    def stage2(ci):
                        c0, c, kind, g0 = L["chunks"][ci]
                        isS = (kind == "s")
                        AT, Vtok, KStok = ATs[ci % 2], Vtoks[ci % 2], KStoks[ci % 2]
                        if not isS:
                            po = ps()
                            for h in range(4):
                                t, e_ = h // 2, h % 2
                                S.mm(po[:, h * 64:h * 64 + c], Sxb[:, t, :], QSm[:, t, e_, c0:c0 + c], start=True, stop=False)
                                S.mm(po[:, h * 64:h * 64 + c], Vtok[0:c, h, :], AT[0:c, h, 0:c], start=False, stop=True)
                            evac(OT2[:, :, c0:c0 + c], po[:, 0:256].rearrange("p (h i) -> p h i", h=4)[:, :, 0:c])
                            for e_ in range(2):
                                hp = 64 * e_
                                pS = ps()
                                for t in range(2):
                                    S.mm(pS[hp:hp + 64, t * 128:(t + 1) * 128], KStok[0:c, t, hp:hp + 64],
                                         Vtok[0:c, 2 * t + e_, :])
                                for t in range(2):
                                    if grp == "C":
                                        dcol = decC[hp:hp + 64, t, ci:ci + 1]
                                    else:
                                        dcol = cm["retdec"][hp:hp + 64, t, (0 if c == 64 else 1):(1 if c == 64 else 2)]
                                    S.stt(Sx[hp:hp + 64, t, :], Sx[hp:hp + 64, t, :], dcol,
                                          pS[hp:hp + 64, t * 128:(t + 1) * 128], ALU.mult, ALU.add)
                            S.copy(Sxb[:, :, :], Sx[:, :, :], e="act")
                        else:
                            ga.reset(mC3)
                            kbase = len(L["chunks"]) - 1
                            Ss = ga.alloc([16, 2, 128], F32)
                            Ssb = ga.alloc([16, 2, 128], BF16)
                            Sn = ga.alloc([16, 2, 128], F32)
                            KSm = ga.alloc([16, 2, 128], BF16)
                            for hp_ in range(2):
                                S.dma(Ss[hp_ * 64:(hp_ + 1) * 64, :, :, :],
                                      sst_d[l].rearrange("s (t e) k v -> e k s t v", e=2)[hp_])
                            S.copy(Ssb[:, :, :, :], Ss[:, :, :, :], e="pool")
                            S.tt(KSm[0:64, :, :, :], KStok[0:64, :, :].unsqueeze(1).to_broadcast([64, 16, 2, 128]),
                                 seqmask_b[0:64, :].unsqueeze(2).unsqueeze(3).to_broadcast([64, 16, 2, 128]), ALU.mult,
                                 e="pool")
                            if grp == "C":
                                S.tt(Sn[:, :, :, :], Ss[:, :, :, :],
                                     decC[:, :, kbase:kbase + 16].rearrange("p t s -> p s t").unsqueeze(3)
                                     .to_broadcast([128, 16, 2, 128]), ALU.mult, e="pool")
                            else:
                                S.tt(Sn[:, :, :, :], Ss[:, :, :, :],
                                     cm["retdec"][:, :, 2:3].unsqueeze(1).to_broadcast([128, 16, 2, 128]), ALU.mult,
                                     e="pool")
                            po = ps()
                            for h in range(4):
                                t, e_ = h // 2, h % 2
                                S.mm(po[:, h * 64:h * 64 + 64], Vtok[0:64, h, :], AT[0:64, h, :], start=True, stop=False)
                                for s_i in range(16):
                                    S.mm(po[:, h * 64 + 4 * s_i:h * 64 + 4 * s_i + 4], Ssb[:, s_i, t, :],
                                         QSm[:, t, e_, c0 + 4 * s_i:c0 + 4 * s_i + 4], start=False, stop=(s_i == 15))
                            evac(OT2[:, :, c0:c0 + 64], po[:, 0:256].rearrange("p (h i) -> p h i", h=4))
                            for e_ in range(2):
                                hp = 64 * e_
                                for s2 in range(8):
                                    pS = ps()
                                    for s_ in range(2):
                                        s_i = s2 * 2 + s_
                                        for t in range(2):
                                            S.mm(pS[hp:hp + 64, (s_ * 2 + t) * 128:(s_ * 2 + t + 1) * 128],
                                                 KSm[0:64, s_i, t, hp:hp + 64], Vtok[0:64, 2 * t + e_, :])
                                    S.tt(Sn[hp:hp + 64, s2 * 2:s2 * 2 + 2, :, :], Sn[hp:hp + 64, s2 * 2:s2 * 2 + 2, :, :],
                                         pS[hp:hp + 64, :].rearrange("p (s t d) -> p s t d", s=2, t=2), ALU.add)
                            for hp_ in range(2):
                                S.dma(sst_o[l].rearrange("s (t e) k v -> e k s t v", e=2)[hp_],
                                      Sn[hp_ * 64:(hp_ + 1) * 64, :, :, :], is_out=True)

                    nchk = len(L["chunks"])
                    stage1(0)
                    for ci in range(nchk):
                        if ci + 1 < nchk:
                            stage1(ci + 1)
                        stage2(ci)
                    ga.reset(mC3)
                    if last:
                        for h in range(4):
                            t, hp = h // 2, (h % 2) * 64
                            S.dma(pst_d[l][h], Sx[hp:hp + 64, t, :], is_out=True)
                    for h in range(4):
                        mt = (8 if grp == "C" else 12) + h
                        gate_mix(mt, OT2[:, h, :], pcols[:, P_NC + l:P_NC + l + 1] if grp == "C" else None, True)
                    flush()
                    ga.reset(mC)

                CP(12)
                flush()
                mO = ga.mark()
                xo2 = [ga.alloc([256], F32) for _ in range(4)]
                for cb in range(8):
                    e0 = w_get("out", l, cb * 256, 128)
                    w_get("out", l, cb * 256 + 128, 128)
                    wo = wo2[e0[4]]
                    for ti, (r0, n, c0) in enumerate(rows):
                        xo = xo2[(cb * len(rows) + ti) % 4]
                        S.dma(xo[0:n, :], xres[r0:r0 + n, cb * 256:(cb + 1) * 256])
                        pb = ps()
                        for kt in range(KT):
                            S.mm(pb[0:n, 0:256], mixT[:, kt, c0:c0 + n], wo[:, kt, :], start=(kt == 0), stop=(kt == KT - 1))
                        S.tt(xo[0:n, :], xo[0:n, :], pb[0:n, 0:256], ALU.add)
                        S.dma(xres[r0:r0 + n, cb * 256:(cb + 1) * 256], xo[0:n, :])
                ga.reset(mO)
                ga.reset(gm0)

        CP(13)
        fnb = ga.alloc([D], F32)
        S.dma(fnb[:, :], fnorm_d.partition_broadcast(128))
        xt2 = [ga.alloc([D], F32) for _ in range(2)]
        junk = ga.alloc([D], BF16)
        st = ga.alloc([8], F32)
        rows = []
        r = 0
        while r < TT:
            n = min(128, (TP if r < TP else TT) - r)
            rows.append((r, n))
            r += n
        for ti, (r0, n) in enumerate(rows):
            xt = xt2[ti % 2]
            S.dma(xt[0:n, :], xres[r0:r0 + n, :])
            S.act(junk[0:n, :], xt[0:n, :], AF.Square, accum_out=st[0:n, 0:1])
            S.act(st[0:n, 1:2], st[0:n, 0:1], AF.Sqrt, scale=1.0 / D, bias=EPS)
            S.recip(st[0:n, 2:3], st[0:n, 1:2])
            S.act(xt[0:n, :], xt[0:n, :], AF.Identity, scale=st[0:n, 2:3])
            S.tt(xt[0:n, :], xt[0:n, :], fnb[0:n, :], ALU.mult)
            if r0 >= TP:
                S.dma(ys_d[r0 - TP:r0 - TP + n, :], xt[0:n, :], is_out=True)
            else:
                a = max(r0, 16)
                if a < r0 + n:
                    S.dma(yp_d[a - 16:r0 + n - 16, :], xt[a - r0:n, :], is_out=True)

    except _Stop:
        pass
    S.finish()
    return nc, hc


_CACHE = {}


def kernel(**inputs):
    cfg = Cfg(nch=32, depth=4, nsb=4, nseq=16)
    if "nc" not in _CACHE:
        _CACHE["nc"] = build(cfg)
    nc, hc = _CACHE["nc"]
    f = np.float32
    g = lambda k: np.ascontiguousarray(np.asarray(inputs[k], dtype=f))
    shared = {k: g(k) for k in ("meta_tokens", "norm_w", "w_in", "conv_a", "a_log", "dt_bias", "norm_a", "conv_b",
                                "conv_b_bias", "lru_wa", "lru_ba", "lru_wx", "lru_bx", "lru_lambda", "gla_w2",
                                "gla_b2", "norm_c", "w_out", "final_norm")}
    for k, v in hc.items():
        shared["c_" + k] = v
    xp, xs = g("x_prompt"), g("x_sample")
    sd, sdc, sl, slc, sg_, sr_ = (g("state_delta"), g("state_delta_conv"), g("state_lru"), g("state_lru_conv"),
                                  g("state_gla"), g("state_ret"))
    in_maps = []
    for i in range(8):
        b = i % 4
        sl_ = slice(16 * i, 16 * i + 16)
        m = dict(shared)
        m["xp"] = xp[b]
        m["xs"] = np.ascontiguousarray(xs[sl_].reshape(64, D))
        m["sdelta"] = np.ascontiguousarray(sd[:, sl_])
        m["sdconv"] = np.ascontiguousarray(sdc[:, sl_].reshape(4, 48, 1536))
        m["slru"] = np.ascontiguousarray(sl[:, sl_])
        m["slconv"] = np.ascontiguousarray(slc[:, sl_].reshape(4, 48, 512))
        m["sgla"] = np.ascontiguousarray(sg_[:, sl_])
        m["sret"] = np.ascontiguousarray(sr_[:, sl_])
        in_maps.append(m)
    res = run_bass_kernel_spmd(nc, in_maps, core_ids=list(range(8))).results
    cat = lambda k, ax: np.concatenate([res[i][k] for i in range(8)], axis=ax)
    stk = lambda k: np.stack([res[i][k] for i in range(4)], axis=1)
    y_p = np.stack([res[i]["y_p"] for i in range(4)], axis=0)
    y_s = cat("y_s", 0).reshape(128, 4, D)
    outs = (y_p, y_s, stk("p_delta"), stk("p_dconv"), stk("p_lru").reshape(4, 4, 512), stk("p_lconv"),
            stk("p_gla"), stk("p_ret"),
            cat("s_delta", 1), cat("s_dconv", 1).reshape(4, 128, 3, 1536), cat("s_lru", 1),
            cat("s_lconv", 1).reshape(4, 128, 3, 512), cat("s_gla", 1), cat("s_ret", 1))
    return tuple(np.ascontiguousarray(o.astype(f)) for o in outs)
```

```python
import math
import numpy as np
import ml_dtypes
import concourse.bass as bass
import concourse.mybir as mybir
from concourse.bass_utils import run_bass_kernel_spmd

F32 = mybir.dt.float32
BF16 = mybir.dt.bfloat16
ALU = mybir.AluOpType
AF = mybir.ActivationFunctionType
ESZ = {F32: 4, BF16: 2}

D = 2048
KT = 16
INW = 6168
EPS = 1e-6
NEG = -30000.0
import os as _os
LOOKD = int(_os.environ.get("LOOKD", "3"))
LOOKC = int(_os.environ.get("LOOKC", "1"))
C_QKV, C_AL, C_XB, C_QC, C_KC, C_VC, C_RC, C_QD, C_KD, C_VD, C_GATE = 0, 1536, 1544, 2056, 2312, 2568, 3080, 3096, 3352, 3608, 4120


def _rng(ap):
    t = ap.tensor
    dims = list(ap.ap)
    off = int(ap.offset)
    es = ESZ.get(ap.dtype, 4)
    if type(t).__name__.startswith("DRam"):
        ext = 0
        for s, n in dims:
            ext += (int(n) - 1) * abs(int(s))
        return (t.name, off * es, (off + ext + 1) * es)
    if type(t).__name__.startswith("PSum"):
        return (t.name, 0, 1 << 30)
    pstep = int(dims[0][0])
    if pstep <= 0:
        pstep = 1 << 40
    lo = off % pstep
    ext = 0
    for s, n in dims[1:]:
        ext += (int(n) - 1) * abs(int(s))
    return (t.name, lo * es, (lo + ext + 1) * es)


class Sch:
    NDMA = 24

    def __init__(self, nc):
        self.nc = nc
        self.eng = {"pe": nc.tensor, "dve": nc.vector, "act": nc.scalar, "pool": nc.gpsimd, "sp": nc.sync}
        self.sem = {k: nc.alloc_semaphore("sem_" + k) for k in self.eng}
        self.cnt = {k: 0 for k in self.eng}
        self.seen = {k: {} for k in self.eng}
        self.dsem = [nc.alloc_semaphore("dsem%d" % i) for i in range(self.NDMA)]
        self.dcnt = [0] * self.NDMA
        self.drr = 0
        self.tr = {}
        self.nins = 0
        self.out_dmas = {}

    def _deps(self, reads, writes, e=None):
        deps = {}
        for ap in reads:
            name, lo, hi = _rng(ap)
            t = self.tr.get(name)
            if t is None:
                continue
            for (l, h, src, val) in t["w"]:
                if l < hi and lo < h and deps.get(src, 0) < val:
                    deps[src] = val
            if type(ap.tensor).__name__.startswith("PSum"):
                for (l, h, src, val) in t["r"]:
                    if src != e and deps.get(src, 0) < val:
                        deps[src] = val
        for ap in writes:
            name, lo, hi = _rng(ap)
            t = self.tr.get(name)
            if t is None:
                continue
            for (l, h, src, val) in t["w"]:
                if l < hi and lo < h and deps.get(src, 0) < val:
                    deps[src] = val
            for (l, h, src, val) in t["r"]:
                if l < hi and lo < h and deps.get(src, 0) < val:
                    deps[src] = val
        return deps

    def _record(self, reads, writes, src, val):
        for ap in writes:
            name, lo, hi = _rng(ap)
            t = self.tr.setdefault(name, {"w": [], "r": []})
            t["w"] = [e for e in t["w"] if not (lo <= e[0] and e[1] <= hi)]
            t["r"] = [e for e in t["r"] if not (lo <= e[0] and e[1] <= hi)]
            t["w"].append((lo, hi, src, val))
        for ap in reads:
            name, lo, hi = _rng(ap)
            t = self.tr.setdefault(name, {"w": [], "r": []})
            t["r"] = [e for e in t["r"] if not (e[2] == src and lo <= e[0] and e[1] <= hi)]
            t["r"].append((lo, hi, src, val))

    def _semof(self, src):
        return self.dsem[src[1]] if isinstance(src, tuple) else self.sem[src]

    def _wait(self, e, deps):
        for src, val in deps.items():
            if src == e and e == "pe":
                continue
            if self.seen[e].get(src, 0) >= val:
                continue
            self.eng[e].wait_ge(self._semof(src), val)
            self.seen[e][src] = val
            self.nins += 1

    def op(self, e, fn, reads, writes, inc=True):
        self._wait(e, self._deps(reads, writes, e))
        ins = fn()
        self.nins += 1
        if inc:
            self.cnt[e] += 1
            ins.then_inc(self.sem[e], 1)
            val = self.cnt[e]
        else:
            val = self.cnt[e] + 1
        self._record(reads, writes, e, val)
        return ins

    def dma(self, out, in_, q="sp", is_out=False):
        deps = self._deps([in_], [out])
        i = self.drr
        self.drr = (self.drr + 1) % self.NDMA
        if self.dcnt[i] > 0:
            deps[("dma", i)] = max(deps.get(("dma", i), 0), 16 * self.dcnt[i])
        self._wait(q, deps)
        ins = self.eng[q].dma_start(out=out, in_=in_, allow_slow_non_contiguous=True)
        self.nins += 1
        self.dcnt[i] += 1
        ins.then_inc(self.dsem[i], 16)
        val = 16 * self.dcnt[i]
        self._record([in_], [out], ("dma", i), val)
        if is_out:
            self.out_dmas[("dma", i)] = val

    def finish(self, q="sp"):
        deps = dict(self.out_dmas)
        for e in self.eng:
            if e != q and self.cnt[e] > 0:
                deps[e] = self.cnt[e]
        self._wait(q, deps)

    def mm(self, out, lhsT, rhs, start=True, stop=True):
        return self.op("pe", lambda: self.nc.tensor.matmul(out, lhsT, rhs, start=start, stop=stop),
                       [lhsT, rhs], [out], inc=stop)

    def transpose(self, out, in_, ident, inc=True):
        return self.op("pe", lambda: self.nc.tensor.transpose(out, in_, ident), [in_, ident], [out], inc=inc)

    def tt(self, out, in0, in1, op, e="dve"):
        return self.op(e, lambda: self.eng[e].tensor_tensor(out=out, in0=in0, in1=in1, op=op), [in0, in1], [out])

    def ts(self, out, in0, s1, s2=None, op0=ALU.mult, op1=None, e="dve"):
        rd = [in0] + [s for s in (s1, s2) if not isinstance(s, (int, float, type(None)))]
        if op1 is None:
            return self.op(e, lambda: self.eng[e].tensor_scalar(out=out, in0=in0, scalar1=s1, scalar2=None, op0=op0),
                           rd, [out])
        return self.op(e, lambda: self.eng[e].tensor_scalar(out=out, in0=in0, scalar1=s1, scalar2=s2, op0=op0,
                                                            op1=op1), rd, [out])

    def stt(self, out, in0, scalar, in1, op0, op1):
        rd = [in0, in1] + ([] if isinstance(scalar, (int, float)) else [scalar])
        return self.op("dve", lambda: self.nc.vector.scalar_tensor_tensor(out=out, in0=in0, scalar=scalar, in1=in1,
                                                                           op0=op0, op1=op1), rd, [out])

    def act(self, out, in_, func, bias=None, scale=None, accum_out=None):
        rd = [in_]
        wr = [out]
        kw = {}
        if bias is not None:
            kw["bias"] = bias
            if not isinstance(bias, (int, float)):
                rd.append(bias)
        if scale is not None:
            kw["scale"] = scale
            if not isinstance(scale, (int, float)):
                rd.append(scale)
        if accum_out is not None:
            kw["accum_out"] = accum_out
            wr.append(accum_out)
        return self.op("act", lambda: self.nc.scalar.activation(out=out, in_=in_, func=func, **kw), rd, wr)

    def copy(self, out, in_, e="dve"):
        if e == "act":
            return self.act(out, in_, AF.Copy)
        return self.op(e, lambda: self.eng[e].tensor_copy(out=out, in_=in_), [in_], [out])

    def scan(self, out, d0, d1, initial):
        rd = [d0, d1] + ([] if isinstance(initial, (int, float)) else [initial])
        return self.op("dve", lambda: self.nc.vector.tensor_tensor_scan(out=out, data0=d0, data1=d1, initial=initial,
                                                                         op0=ALU.mult, op1=ALU.add), rd, [out])

    def memset(self, ap, val, e="pool"):
        return self.op(e, lambda: self.eng[e].memset(ap, val), [], [ap])

    def recip(self, out, in_, fast=False):
        return self.op("dve", lambda: self.nc.vector.reciprocal(out=out, in_=in_), [in_], [out])


class Arena:
    def __init__(self, nc, name, nbytes):
        self.words = nbytes // 4
        self.t = nc.alloc_sbuf_tensor(name, [128, self.words], F32)
        self.off = 0
        self.peak = 0

    def alloc(self, shape, dtype, parts=128):
        n = 1
        for s in shape:
            n *= s
        nb = (n * ESZ[dtype] + 31) // 32 * 32
        w0 = self.off // 4
        self.off += nb
        self.peak = max(self.peak, self.off)
        assert self.off // 4 <= self.words, "arena overflow %s %d > %d" % (self.t.name, self.off, self.words * 4)
        v = self.t[0:parts, w0:w0 + nb // 4]
        if dtype != F32:
            v = v.bitcast(dtype)
        v = v[:, 0:n]
        if len(shape) == 2:
            v = v.rearrange("p (a b) -> p a b", a=shape[0])
        elif len(shape) == 3:
            v = v.rearrange("p (a b c) -> p a b c", a=shape[0], b=shape[1])
        elif len(shape) == 4:
            v = v.rearrange("p (a b c d) -> p a b c d", a=shape[0], b=shape[1], c=shape[2])
        return v

    def mark(self):
        return self.off

    def reset(self, m):
        self.off = m


class _Stop(Exception):
    pass


class Cfg:
    def __init__(self, nch=32, depth=4, nsb=4, nseq=16):
        self.nch, self.depth, self.nsb, self.nseq = nch, depth, nsb, nseq
        self.stop = 0


def host_consts(cfg):
    nch, nsb = cfg.nch, cfg.nsb
    f = np.float32
    c = {}
    c["ident"] = np.eye(128, dtype=f)
    j = np.arange(64)[:, None]
    i = np.arange(64)[None, :]
    same = (j // 4) == (i // 4)
    c["negmask_p"] = np.where(j <= i, 0.0, NEG).astype(f)
    c["negmask_s"] = np.where((j <= i) & same, 0.0, NEG).astype(f)
    c["strict_p"] = np.where(j < i, -1.0, 0.0).astype(f)
    c["strict_s"] = np.where((j < i) & same, -1.0, 0.0).astype(f)
    c["incl_p"] = np.where(j <= i, 1.0, 0.0).astype(f)
    c["incl_s"] = np.where((j <= i) & same, 1.0, 0.0).astype(f)
    lg = np.log(1.0 - 2.0 ** (-5.0 - np.arange(4, dtype=np.float64)))
    rp = np.zeros((64, 4, 64), np.float64)
    rs = np.zeros((64, 4, 64), np.float64)
    for h in range(4):
        rp[:, h, :] = np.where(j <= i, np.exp(lg[h] * np.maximum(i - j, 0)), 0.0) * 0.125
        rs[:, h, :] = np.where((j <= i) & same, np.exp(lg[h] * np.maximum(i - j, 0)), 0.0) * 0.125
    c["retm_p"] = rp.astype(f)
    c["retm_s"] = rs.astype(f)
    c["seqmask"] = (np.arange(64)[:, None] // 4 == np.arange(16)[None, :]).astype(f)
    sel = np.zeros((8, 8, 128), f)
    for k in range(8):
        sel[k, k, :] = 1.0
    c["sel"] = sel
    perm = np.zeros((128, 128), f)
    for m in range(128):
        k = m + 32 if (m % 64) < 32 else m - 32
        perm[k, m] = 1.0
    c["perm"] = perm
    hrow = np.arange(128) // 64
    fs64 = np.zeros((128, 2, 64), np.float64)
    ts64 = np.zeros((128, 2, 64), np.float64)
    ts16 = np.zeros((128, 2, 16), np.float64)
    fss = np.zeros((128, 2, 64), np.float64)
    tss = np.zeros((128, 2, 64), np.float64)
    dec = np.zeros((128, 2, 3), np.float64)
    pos = np.arange(64)
    for t in range(2):
        g = lg[2 * t + hrow][:, None]
        fs64[:, t, :] = np.exp(g * (pos[None, :] + 1.0))
        ts64[:, t, :] = np.exp(g * (63.0 - pos[None, :])) * 0.125
        ts16[:, t, :] = np.exp(g * (15.0 - pos[None, :16])) * 0.125
        fss[:, t, :] = np.exp(g * ((pos[None, :] % 4) + 1.0))
        tss[:, t, :] = np.exp(g * (3.0 - (pos[None, :] % 4))) * 0.125
        dec[:, t, 0] = np.exp(g[:, 0] * 64.0)
        dec[:, t, 1] = np.exp(g[:, 0] * 16.0)
        dec[:, t, 2] = np.exp(g[:, 0] * 4.0)
    c["fs64"], c["ts64"], c["ts16"], c["fss"], c["tss"], c["retdec"] = [a.astype(f) for a in (fs64, ts64, ts16, fss, tss, dec)]
    lay = layout(cfg)
    tbm = lay["tbmax"]
    reset = np.ones((nsb, 128, tbm), f)
    cos = np.zeros((nsb, 128, tbm), f)
    sin = np.zeros((nsb, 128, tbm), f)
    half = 32
    freqs = (10000.0 ** (-np.arange(half, dtype=np.float32) / half)).astype(np.float32)
    fr = freqs[np.arange(128) % 32]
    sgn = np.where((np.arange(128) % 64) < 32, -1.0, 1.0).astype(f)
    for sb in range(nsb):
        L = lay["sb"][sb]
        posv = np.zeros(tbm, np.float32)
        for (c0, cc, kind, g0) in L["chunks"]:
            if kind == "p":
                reset[sb, :, c0] = 0.0
                posv[c0:c0 + cc] = np.arange(g0, g0 + cc)
            else:
                for s in range(16):
                    reset[sb, :, c0 + 4 * s] = 0.0
                posv[c0:c0 + cc] = 16384 + (np.arange(64) % 4)
        ang = (posv[None, :].astype(np.float32) * fr[:, None]).astype(np.float32)
        cos[sb] = np.cos(ang)
        sin[sb] = np.sin(ang) * sgn[:, None]
    c["reset"], c["cos"], c["sin"] = reset, cos, sin
    return c


def layout(cfg):
    nch, nsb = cfg.nch, cfg.nsb
    tp = 16 + 64 * nch
    per = nch // nsb
    sbs = []
    for sb in range(nsb):
        chunks = []
        col = 0
        p0 = 0 if sb == 0 else 16 + 64 * per * sb
        if sb == 0:
            chunks.append((0, 16, "p", 0))
            col = 16
        for i in range(per * sb, per * (sb + 1)):
            chunks.append((col, 64, "p", 16 + 64 * i))
            col += 64
        npc = col
        if sb == nsb - 1:
            chunks.append((col, 64, "s", tp))
            col += 64
        sbs.append({"chunks": chunks, "np": npc, "tbl": col, "p0": p0, "last": sb == nsb - 1})
    return {"sb": sbs, "tbmax": max(s["tbl"] for s in sbs), "tp": tp, "tt": tp + 64}


def build(cfg, dbg=False):
    nc = bass.Bass("TRN2", target_bir_lowering=False)
    nc.allow_non_contiguous_dma(reason="small strided parameter / state loads").__enter__()
    S = Sch(nc)
    DEPTH, NCH, NSB, NSEQ = cfg.depth, cfg.nch, cfg.nsb, cfg.nseq
    lay = layout(cfg)
    TP, TT, TBM = lay["tp"], lay["tt"], lay["tbmax"]
    SEQ = 64 * NCH

    def din(name, shape):
        return nc.dram_tensor(name, list(shape), F32, kind="ExternalInput").ap()

    def dout(name, shape):
        return nc.dram_tensor(name, list(shape), F32, kind="ExternalOutput").ap()

    xp_d = din("xp", [SEQ, D])
    xs_d = din("xs", [64, D])
    meta_d = din("meta_tokens", [16, D])
    sdelta_d = din("sdelta", [DEPTH, NSEQ, 4, 128, 128])
    sdconv_d = din("sdconv", [DEPTH, NSEQ * 3, 1536])
    slru_d = din("slru", [DEPTH, NSEQ, 512])
    slconv_d = din("slconv", [DEPTH, NSEQ * 3, 512])
    sgla_d = din("sgla", [DEPTH, NSEQ, 4, 64, 128])
    sret_d = din("sret", [DEPTH, NSEQ, 4, 64, 128])
    normw_d = din("norm_w", [DEPTH, D])
    win_d = din("w_in", [DEPTH, D, INW])
    conva_d = din("conv_a", [DEPTH, 4, 1536])
    alog_d = din("a_log", [DEPTH, 4])
    dtb_d = din("dt_bias", [DEPTH, 4])
    norma_d = din("norm_a", [DEPTH, 128])
    convb_d = din("conv_b", [DEPTH, 4, 512])
    convbb_d = din("conv_b_bias", [DEPTH, 512])
    lruwa_d = din("lru_wa", [DEPTH, 4, 128, 128])
    lruba_d = din("lru_ba", [DEPTH, 512])
    lruwx_d = din("lru_wx", [DEPTH, 4, 128, 128])
    lrubx_d = din("lru_bx", [DEPTH, 512])
    lrulam_d = din("lru_lambda", [DEPTH, 512])
    glaw2_d = din("gla_w2", [DEPTH, 16, 256])
    glab2_d = din("gla_b2", [DEPTH, 256])
    normc_d = din("norm_c", [DEPTH, 128])
    wout_d = din("w_out", [DEPTH, D, D])
    fnorm_d = din("final_norm", [D])
    hc = host_consts(cfg)
    cd = {k: din("c_" + k, v.shape) for k, v in hc.items()}

    yp_d = dout("y_p", [SEQ, D])
    ys_d = dout("y_s", [64, D])
    pdelta_d = dout("p_delta", [DEPTH, 4, 128, 128])
    pdconv_d = dout("p_dconv", [DEPTH, 3, 1536])
    plru_d = dout("p_lru", [DEPTH, 4, 128])
    plconv_d = dout("p_lconv", [DEPTH, 3, 512])
    pgla_d = dout("p_gla", [DEPTH, 4, 64, 128])
    pret_d = dout("p_ret", [DEPTH, 4, 64, 128])
    sdelta_o = dout("s_delta", [DEPTH, NSEQ, 4, 128, 128])
    sdconv_o = dout("s_dconv", [DEPTH, NSEQ * 3, 1536])
    slru_o = dout("s_lru", [DEPTH, NSEQ, 512])
    slconv_o = dout("s_lconv", [DEPTH, NSEQ * 3, 512])
    sgla_o = dout("s_gla", [DEPTH, NSEQ, 4, 64, 128])
    sret_o = dout("s_ret", [DEPTH, NSEQ, 4, 64, 128])
    xres = nc.dram_tensor("xres", [TT, D], F32, kind="Internal").ap()

    def sb(name, shape, dt=F32):
        return nc.alloc_sbuf_tensor(name, list(shape), dt)

    ident_f = sb("ident_f", [128, 128])
    ident_b = sb("ident_b", [128, 128], BF16)
    ones_b = sb("ones_b", [128, 128], BF16)
    perm_b = sb("perm_b", [128, 128], BF16)
    perm_f = sb("perm_f", [128, 128])
    cm = {}
    for k in ("negmask_p", "negmask_s", "strict_p", "strict_s", "incl_p", "incl_s"):
        cm[k] = sb("m_" + k, [128, 64])
    for k in ("retm_p", "retm_s"):
        cm[k] = sb("m_" + k, [128, 4, 64])
    cm["seqmask"] = sb("m_seqmask", [128, 16])
    seqmask_b = sb("seqmask_b", [128, 16], BF16)
    sel = sb("sel", [8, 8, 128])
    for k in ("fs64", "ts64", "fss", "tss"):
        cm[k] = sb("m_" + k, [128, 2, 64])
    cm["ts16"] = sb("m_ts16", [128, 2, 16])
    cm["retdec"] = sb("m_retdec", [128, 2, 3])
    reset_t = sb("reset_t", [128, TBM])
    cos_t = sb("cos_t", [128, TBM])
    sin_t = sb("sin_t", [128, TBM])
    NPC = 100 * DEPTH + 8 * DEPTH + 8
    pcols = sb("pcols", [128, NPC])
    prm8 = sb("prm8", [8, DEPTH, 2])
    alg8 = sb("alg8", [8, DEPTH])
    xnT = sb("xnT", [128, KT, TBM], BF16)
    mixT = sb("mixT", [128, KT, TBM], BF16)
    NST = 3
    wstg = [sb("wstg%d" % i, [128, KT, 128]) for i in range(NST)]
    NWB = 3
    wbf = [sb("wbf%d" % i, [128, KT, 128], BF16) for i in range(NWB)]
    wo2 = [sb("wo%d" % i, [128, KT, 256], BF16) for i in range(2)]
    sstg = [sb("sstg%d" % i, [128, 6, 128]) for i in range(2)]
    Sd = sb("Sd", [128, 4, 128])
    Sdb = sb("Sdb", [128, 4, 128], BF16)
    Sg = sb("Sg", [128, 2, 128])
    Sgb = sb("Sgb", [128, 2, 128], BF16)
    Sr = sb("Sr", [128, 2, 128])
    Srb = sb("Srb", [128, 2, 128], BF16)
    hl = sb("hl", [128, 4])
    carryA = sb("carryA", [128, 12, 3])
    carryB = sb("carryB", [128, 4, 3])
    lwa = sb("lwa", [128, 4, 128], BF16)
    lwx = sb("lwx", [128, 4, 128], BF16)
    w2b = sb("w2b", [16, 256], BF16)
    ga = Arena(nc, "garena", cfg.ga_bytes if hasattr(cfg, "ga_bytes") else 81 * 1024)

    pbanks = [nc.alloc_psum_tensor("pb%d" % i, [128, 512], F32) for i in range(8)]
    pstate = {"i": 0}

    def ps():
        b = pbanks[pstate["i"] % 8]
        pstate["i"] += 1
        return b

    evs = {"i": 0}

    def evac(out, in_):
        evs["i"] += 1
        S.copy(out, in_, e=("act" if evs["i"] % 2 else "dve"))

    def CP(k):
        if cfg.stop == k:
            raise _Stop()

    try:
        S.dma(ident_f[:], cd["ident"])
        S.copy(ident_b[:], ident_f[:], e="dve")
        S.memset(ones_b[:], 1.0)
        S.dma(perm_f[:], cd["perm"])
        S.copy(perm_b[:], perm_f[:], e="dve")
        for k in cm:
            if cm[k].shape[0] == 128 and cd[k].shape[0] == 64:
                S.dma(cm[k][0:64], cd[k])
                S.dma(cm[k][64:128], cd[k])
            else:
                S.dma(cm[k][:], cd[k])
        S.copy(seqmask_b[:], cm["seqmask"][:], e="dve")
        S.dma(sel[:], cd["sel"])

        CP(1)
        pc = {"n": 0}

        def load_cols(dram2d, rows):
            base = pc["n"]
            r0 = 0
            while r0 < rows:
                r = min(128, rows - r0)
                stg = sstg[0][0:r, 0, :]
                S.dma(stg, dram2d[r0:r0 + r, :])
                pb = ps()
                S.transpose(pb[:, 0:r], stg, ident_f[0:r, 0:r])
                evac(pcols[:, base + r0: base + r0 + r], pb[:, 0:r])
                r0 += r
            pc["n"] += rows
            return base

        P_NORMW = load_cols(normw_d.rearrange("l (a p) -> (l a) p", p=128), DEPTH * 16)
        P_CONVA = load_cols(conva_d.rearrange("l k (a p) -> (l k a) p", p=128), DEPTH * 48)
        P_CONVB = load_cols(convb_d.rearrange("l k (a p) -> (l k a) p", p=128), DEPTH * 16)
        P_CONVBB = load_cols(convbb_d.rearrange("l (a p) -> (l a) p", p=128), DEPTH * 4)
        P_LBA = load_cols(lruba_d.rearrange("l (a p) -> (l a) p", p=128), DEPTH * 4)
        P_LBX = load_cols(lrubx_d.rearrange("l (a p) -> (l a) p", p=128), DEPTH * 4)
        P_LAM = load_cols(lrulam_d.rearrange("l (a p) -> (l a) p", p=128), DEPTH * 4)
        P_B2 = load_cols(glab2_d.rearrange("l (a p) -> (l a) p", p=128), DEPTH * 2)
        P_NA = load_cols(norma_d, DEPTH)
        P_NC = load_cols(normc_d, DEPTH)
        P_NSP8 = pc["n"]
        pc["n"] += DEPTH * 4
        P_NB2 = pc["n"]
        pc["n"] += DEPTH * 2
        assert pc["n"] <= NPC
        lamc = pcols[:, P_LAM:P_LAM + DEPTH * 4]
        nsp = pcols[:, P_NSP8:P_NSP8 + DEPTH * 4]
        S.act(nsp, lamc, AF.Exp, scale=-1.0)
        S.act(nsp, nsp, AF.Ln, bias=1.0)
        S.ts(nsp, nsp, -8.0, op0=ALU.mult)
        S.ts(pcols[:, P_NB2:P_NB2 + DEPTH * 2], pcols[:, P_B2:P_B2 + DEPTH * 2], -1.0, op0=ALU.mult)
        S.memset(prm8[:], 0.0)
        S.memset(alg8[:], 0.0)
        S.dma(prm8[0:4, :, 0], dtb_d.rearrange("l h -> h l"))
        S.dma(alg8[0:4, :], alog_d.rearrange("l h -> h l"))
        S.act(alg8[:], alg8[:], AF.Exp)
        S.ts(prm8[:, :, 1], alg8[:], -1.0, op0=ALU.mult)

        CP(2)
        S.dma(xres[0:16, :], meta_d)
        R = 0
        while R < SEQ:
            r = min(512, SEQ - R)
            S.dma(xres[16 + R:16 + R + r, :], xp_d[R:R + r, :])
            R += r
        S.dma(xres[TP:TP + 64, :], xs_d)

        CP(3)
        WSEQ = []
        nwb_ = 0
        ncb_ = 0
        for l_ in range(DEPTH):
            for sb_ in range(NSB):
                blocks = [(C_QKV + j * 128, 128) for j in range(12)] + [(C_AL, 8)] + [(C_GATE + m * 128, 128) for m in range(4)]
                for n_ in range(4):
                    blocks += [(C_XB + n_ * 128, 128), (C_GATE + (4 + n_) * 128, 128)]
                blocks += [(C_RC, 16)]
                for t in range(2):
                    blocks += [(C_QC + t * 128, 128), (C_KC + t * 128, 128)]
                blocks += [(C_VC + h * 128, 128) for h in range(4)] + [(C_GATE + (8 + h) * 128, 128) for h in range(4)]
                blocks += [(C_QD + t * 128, 128) for t in range(2)] + [(C_KD + t * 128, 128) for t in range(2)]
                blocks += [(C_VD + h * 128, 128) for h in range(4)] + [(C_GATE + (12 + h) * 128, 128) for h in range(4)]
                for (c0_, nco_) in blocks:
                    WSEQ.append(("in", l_, c0_, nco_, nwb_ % NWB, 0))
                    nwb_ += 1
                for cb in range(8):
                    for q in range(2):
                        WSEQ.append(("out", l_, cb * 256 + q * 128, 128, ncb_ % 2, q))
                    ncb_ += 1
        wst = {"dma": 0, "cast": 0, "used": 0}

        def w_dest(k):
            kind, l_, c0_, nco_, slot, q = WSEQ[k]
            if kind == "in":
                return wbf[slot][:, :, 0:nco_]
            return wo2[slot][:, :, q * 128:(q + 1) * 128]

        def w_dma(k):
            kind, l_, c0_, nco_, slot, q = WSEQ[k]
            src = (win_d if kind == "in" else wout_d)[l_]
            S.dma(wstg[k % NST][:, :, 0:nco_], src[:, c0_:c0_ + nco_].rearrange("(kt p) c -> p kt c", p=128))

        def w_cast(k):
            nco_ = WSEQ[k][3]
            S.copy(w_dest(k), wstg[k % NST][:, :, 0:nco_], e=("act" if (WSEQ[k][0] == "out" and k % 2) else "dve"))

        def w_get(kind, l_, c0_, nco_):
            k = wst["used"]
            assert WSEQ[k][0:4] == (kind, l_, c0_, nco_), (k, WSEQ[k], kind, l_, c0_, nco_)
            while True:
                prog = False
                j = wst["dma"]
                if j < min(k + 1 + LOOKD, len(WSEQ)) and wst["cast"] > j - NST:
                    w_dma(j)
                    wst["dma"] += 1
                    prog = True
                j = wst["cast"]
                if j < min(k + 1 + LOOKC, len(WSEQ)) and j < wst["dma"]:
                    w_cast(j)
                    wst["cast"] += 1
                    prog = True
                if not prog:
                    break
            assert wst["cast"] > k
            wst["used"] += 1
            return WSEQ[k]

        def load_w(w2d, c0, ncols, scale_base=None, rows=KT):
            e_ = w_get("in", l, c0, ncols)
            return wbf[e_[4]]

        def ttiles(L):
            res = []
            t0 = 0
            while t0 < L["np"]:
                n = min(512, L["np"] - t0)
                res.append((t0, n, "p"))
                t0 += n
            if L["last"]:
                res.append((L["np"], 64, "s"))
            return res

        pend = {"f": None, "f2": None}

        def _drain_one():
            f, f2 = pend["f"], pend["f2"]
            pend["f"] = None
            pend["f2"] = None
            if f is not None:
                pend["f2"] = f()
            if f2 is not None:
                f2()

        def flush():
            _drain_one()
            _drain_one()

        def project(wb, M, L, consumer, post=None, post2=None):
            banks = []
            for (t0, n, kind) in ttiles(L):
                pb = ps()
                for kt in range(KT):
                    S.mm(pb[0:M, 0:n], wb[:, kt, 0:M], xnT[:, kt, t0:t0 + n], start=(kt == 0), stop=(kt == KT - 1))
                banks.append((pb, t0, n, kind))
            _drain_one()

            def epi():
                for (pb, t0, n, kind) in banks:
                    consumer(pb, t0, n, kind)
                if post is not None:
                    post()
                return post2

            pend["f"] = epi

        def bview(ap2, a, b):
            return ap2.rearrange("p (a b) -> p a b", a=a)

        def chunk_last(dst, src, L):
            npc = L["np"]
            c0 = 0
            if L["chunks"][0][1] == 16:
                S.copy(dst[:, 0:16], src[:, 15:16].to_broadcast([dst.shape[0], 16]), e="dve")
                c0 = 16
            n64 = (npc - c0) // 64
            if n64 > 0:
                sv = bview(src[:, c0:npc], n64, 64)
                S.copy(bview(dst[:, c0:npc], n64, 64), sv[:, :, 63:64].to_broadcast([dst.shape[0], n64, 64]), e="dve")
            if L["last"]:
                sv = bview(src[:, npc:npc + 64], 16, 4)
                S.copy(bview(dst[:, npc:npc + 64], 16, 4), sv[:, :, 3:4].to_broadcast([dst.shape[0], 16, 4]), e="dve")

        def chunk_last_cols(L):
            cols = []
            for (c0, cc, kind, g0) in L["chunks"]:
                if kind == "p":
                    cols.append(c0 + cc - 1)
                else:
                    cols += [c0 + 4 * s + 3 for s in range(16)]
            return cols

        def compact_last(dst, src, L):
            k = 0
            npc = L["np"]
            c0 = 0
            if L["chunks"][0][1] == 16:
                S.copy(dst[:, 0:1], src[:, 15:16], e="dve")
                k = 1
                c0 = 16
            n64 = (npc - c0) // 64
            if n64 > 0:
                S.copy(dst[:, k:k + n64], bview(src[:, c0:npc], n64, 64)[:, :, 63], e="dve")
                k += n64
            if L["last"]:
                S.copy(dst[:, k:k + 16], bview(src[:, npc:npc + 64], 16, 4)[:, :, 3], e="dve")
                k += 16
            return k

        def conv_block(prew, presw, acc, wbase, L, bias_col=None):
            npc = L["np"]
            w = [pcols[:, wbase + k:wbase + k + 1] for k in range(4)]
            if bias_col is None:
                S.act(acc[:, 0:npc], prew[:, 3:3 + npc], AF.Identity, scale=w[3])
            else:
                S.act(acc[:, 0:npc], prew[:, 3:3 + npc], AF.Identity, scale=w[3], bias=bias_col)
            for k in (2, 1, 0):
                S.stt(acc[:, 0:npc], prew[:, k:k + npc], w[k], acc[:, 0:npc], ALU.mult, ALU.add)
            if L["last"]:
                a3 = bview(acc[:, npc:npc + 64], 16, 4)
                if bias_col is None:
                    S.act(a3, presw[:, :, 3:7], AF.Identity, scale=w[3])
                else:
                    S.act(a3, presw[:, :, 3:7], AF.Identity, scale=w[3], bias=bias_col)
                for k in (2, 1, 0):
                    S.stt(a3, presw[:, :, k:k + 4], w[k], a3, ALU.mult, ALU.add)

        def out_rows(dst2d, src, ncols):
            pb = ps()
            S.transpose(pb[0:ncols, 0:128], src, ident_f[:, :])
            tmp = ga.alloc([128], F32)
            evac(tmp[0:ncols, :], pb[0:ncols, 0:128])
            S.dma(dst2d, tmp[0:ncols, :], is_out=True)

        for l in range(DEPTH):
            wl = win_d[l]
            S.dma(sstg[0][:, 0:4, :], lruwa_d[l].rearrange("n c d -> c n d"))
            S.copy(lwa[:], sstg[0][:, 0:4, :], e="pool")
            S.dma(sstg[1][:, 0:4, :], lruwx_d[l].rearrange("n c d -> c n d"))
            S.copy(lwx[:], sstg[1][:, 0:4, :], e="pool")
            S.dma(sstg[0][0:16, 4, :], glaw2_d[l][:, 0:128])
            S.dma(sstg[0][0:16, 5, :], glaw2_d[l][:, 128:256])
            S.copy(w2b[:].rearrange("p (a b) -> p a b", a=2), sstg[0][0:16, 4:6, :], e="pool")
            S.memset(Sd[:], 0.0)
            S.memset(Sdb[:], 0.0)
            S.memset(Sg[:], 0.0)
            S.memset(Sgb[:], 0.0)
            S.memset(Sr[:], 0.0)
            S.memset(Srb[:], 0.0)
            S.memset(hl[:], 0.0)
            S.memset(carryA[:], 0.0)
            S.memset(carryB[:], 0.0)

            for sbi in range(NSB):
                L = lay["sb"][sbi]
                npc, TBL, last = L["np"], L["tbl"], L["last"]
                if NSB > 1 or l == 0:
                    S.dma(reset_t[:, 0:TBM], cd["reset"][sbi])
                    S.dma(cos_t[:, 0:TBM], cd["cos"][sbi])
                    S.dma(sin_t[:, 0:TBM], cd["sin"][sbi])
                gm0 = ga.mark()

                rows = []
                r = 0
                while r < npc:
                    n = min(128, npc - r)
                    rows.append((L["p0"] + r, n, r))
                    r += n
                if last:
                    rows.append((TP, 64, npc))

                m0 = ga.mark()
                xt2 = [ga.alloc([D], F32) for _ in range(2)]
                xs2 = [ga.alloc([D], BF16) for _ in range(2)]
                junk = ga.alloc([D], BF16)
                st = ga.alloc([8], F32)
                for ti, (r0, n, c0) in enumerate(rows):
                    xt = xt2[ti % 2]
                    xs = xs2[ti % 2]
                    S.dma(xt[0:n, :], xres[r0:r0 + n, :])
                    S.act(junk[0:n, :], xt[0:n, :], AF.Square, accum_out=st[0:n, 0:1])
                    S.act(st[0:n, 1:2], st[0:n, 0:1], AF.Sqrt, scale=1.0 / D, bias=EPS)
                    S.recip(st[0:n, 2:3], st[0:n, 1:2])
                    S.act(xs[0:n, :], xt[0:n, :], AF.Identity, scale=st[0:n, 2:3])
                    for half in range(2):
                        pb = ps()
                        pbb = pb[:, :].bitcast(BF16)
                        for j in range(8):
                            kt = half * 8 + j
                            S.transpose(pbb[:, j * 128:j * 128 + n], xs[0:n, kt * 128:(kt + 1) * 128], ident_b[0:n, 0:n],
                                        inc=(j == 7))
                        S.tt(xnT[:, half * 8:half * 8 + 8, c0:c0 + n], pbb.rearrange("p (a b) -> p a b", a=8)[:, :, 0:n],
                             pcols[:, P_NORMW + l * 16 + half * 8:P_NORMW + l * 16 + half * 8 + 8].unsqueeze(2)
                             .to_broadcast([128, 8, n]), ALU.mult)
                ga.reset(m0)

                CP(4)
                nwcol = P_NORMW + l * 16

                gcount = {"i": 0}

                def gate_mix(mt, o_ap, normcol, do_norm):
                    wb = load_w(wl, C_GATE + mt * 128, 128, nwcol)
                    mk = ga.mark()
                    SGs = [ga.alloc([TBL], F32) for _ in range(2)]
                    RSs = [ga.alloc([TBL], F32) for _ in range(2)]
                    SQg = ga.alloc([TBL], BF16)
                    TMg = ga.alloc([TBL], F32)
                    par = gcount["i"] % 2
                    gcount["i"] += 1
                    SG, RS = SGs[par], RSs[par]

                    def cons(pb, t0, n, kind):
                        S.act(SG[:, t0:t0 + n], pb[:, 0:n], AF.Silu)
                        o = o_ap[:, t0:t0 + n]
                        if do_norm:
                            S.tt(SQg[:, t0:t0 + n], o, o, ALU.mult, e="pool")
                            pq = ps()
                            S.mm(pq[:, 0:n], ones_b[:, :], SQg[:, t0:t0 + n])
                            S.act(RS[:, t0:t0 + n], pq[:, 0:n], AF.Sqrt, scale=1.0 / 128, bias=EPS)
                        else:
                            S.tt(mixT[:, mt, t0:t0 + n], o, SG[:, t0:t0 + n], ALU.mult)

                    def post2():
                        if not do_norm:
                            return
                        S.recip(RS[:, 0:TBL], RS[:, 0:TBL], fast=True)
                        S.tt(TMg[:, 0:TBL], o_ap[:, 0:TBL], RS[:, 0:TBL], ALU.mult)
                        if normcol is not None:
                            S.stt(mixT[:, mt, 0:TBL], TMg[:, 0:TBL], normcol, SG[:, 0:TBL], ALU.mult, ALU.mult)
                        else:
                            S.tt(mixT[:, mt, 0:TBL], TMg[:, 0:TBL], SG[:, 0:TBL], ALU.mult)

                    project(wb, 128, L, cons, None, post2)
                    ga.reset(mk)

                mA = ga.mark()
                QN = ga.alloc([4, TBL], BF16)
                KN = ga.alloc([4, TBL], BF16)
                BKN = ga.alloc([4, TBL], BF16)
                KQG = ga.alloc([4, 2, TBL], BF16)
                BKT = ga.alloc([4, TBL], BF16)
                VT = ga.alloc([4, TBL], BF16)
                OT = ga.alloc([4, TBL], BF16)
                mA2 = ga.mark()
                pre2 = [ga.alloc([3 + max(npc, 1)], F32) for _ in range(2)]
                pres2 = [ga.alloc([16, 7], F32) for _ in range(2)]
                acc2 = [ga.alloc([TBL], F32) for _ in range(2)]
                sqb = ga.alloc([TBL], BF16)
                rinA = [ga.alloc([TBL], F32) for _ in range(2)]
                cst = ga.alloc([12, 48], F32)
                if last:
                    for half in range(2):
                        stg = sstg[half][0:48, 0:6, :]
                        S.dma(stg, sdconv_d[l][:, half * 768:(half + 1) * 768].rearrange("r (a p) -> r a p", p=128))
                        pb = ps()
                        for a in range(6):
                            S.transpose(pb[:, a * 48:(a + 1) * 48], sstg[half][0:48, a, :], ident_f[0:48, 0:48], inc=(a == 5))
                        evac(cst[:, half * 6:half * 6 + 6, :], pb[:, 0:288].rearrange("p (a b) -> p a b", a=6))
                for j in range(12):
                    which, h = j // 4, j % 4
                    wb = load_w(wl, C_QKV + j * 128, 128, nwcol)
                    prew, presw, acc = pre2[j % 2], pres2[j % 2], acc2[j % 2]
                    S.copy(prew[:, 0:3], carryA[:, j, :], e="pool")
                    if last:
                        S.copy(presw[:, :, 0:3], cst[:, j, :].rearrange("p (s r) -> p s r", r=3), e="pool")

                    def cons(pb, t0, n, kind, prew=prew, presw=presw):
                        if kind == "p":
                            evac(prew[:, 3 + t0:3 + t0 + n], pb[:, 0:n])
                        else:
                            evac(presw[:, :, 3:7], pb[:, 0:64].rearrange("p (s t) -> p s t", t=4))

                    def post(j=j, which=which, h=h, prew=prew, presw=presw, acc=acc):
                        wcolk = [pcols[:, P_CONVA + (l * 4 + k) * 12 + j: P_CONVA + (l * 4 + k) * 12 + j + 1] for k in range(4)]
                        if npc > 0:
                            S.ts(acc[:, 0:npc], prew[:, 3:3 + npc], wcolk[3], op0=ALU.mult)
                            for k in (2, 1, 0):
                                S.stt(acc[:, 0:npc], prew[:, k:k + npc], wcolk[k], acc[:, 0:npc], ALU.mult, ALU.add)
                            S.copy(carryA[:, j, :], prew[:, npc:npc + 3], e="pool")
                        if last:
                            a3 = bview(acc[:, npc:npc + 64], 16, 4)
                            S.ts(a3, presw[:, :, 3:7], wcolk[3], op0=ALU.mult)
                            for k in (2, 1, 0):
                                S.stt(a3, presw[:, :, k:k + 4], wcolk[k], a3, ALU.mult, ALU.add)
                            S.copy(cst[:, j, :].rearrange("p (s r) -> p s r", r=3), presw[:, :, 4:7], e="pool")
                        if which == 2:
                            S.act(VT[:, h, :], acc[:, 0:TBL], AF.Silu)
                        else:
                            S.act(acc[:, 0:TBL], acc[:, 0:TBL], AF.Silu)
                            S.tt(sqb[:, 0:TBL], acc[:, 0:TBL], acc[:, 0:TBL], ALU.mult, e="pool")
                            dst = QN if which == 0 else KN
                            sc_ = 128.0 if which == 0 else 1.0
                            for (t0, n, kind) in ttiles(L):
                                pq = ps()
                                S.mm(pq[:, 0:n], ones_b[:, :], sqb[:, t0:t0 + n])
                                S.act(rinA[j % 2][:, t0:t0 + n], pq[:, 0:n], AF.Sqrt, scale=sc_, bias=sc_ * EPS)

                    def post2(j=j, which=which, h=h, acc=acc):
                        if which == 2:
                            return
                        dst = QN if which == 0 else KN
                        rn = rinA[j % 2]
                        S.recip(rn[:, 0:TBL], rn[:, 0:TBL], fast=True)
                        S.tt(dst[:, h, 0:TBL], acc[:, 0:TBL], rn[:, 0:TBL], ALU.mult)

                    project(wb, 128, L, cons, post, post2)
                flush()
                if last:
                    for half in range(2):
                        tmp = ga.alloc([768], F32)
                        pb = ps()
                        for a in range(4):
                            S.transpose(pb[0:48, a * 128:(a + 1) * 128], cst[:, half * 6 + a, :], ident_f[:, :], inc=(a == 3))
                        evac(tmp[0:48, 0:512], pb[0:48, 0:512])
                        pb2 = ps()
                        for a in range(2):
                            S.transpose(pb2[0:48, a * 128:(a + 1) * 128], cst[:, half * 6 + 4 + a, :], ident_f[:, :], inc=(a == 1))
                        evac(tmp[0:48, 512:768], pb2[0:48, 0:256])
                        S.dma(sdconv_o[l][:, half * 768:(half + 1) * 768], tmp[0:48, :], is_out=True)
                    pb = ps()
                    S.transpose(pb[0:36, 0:128], carryA[:, :, :].rearrange("p a r -> p (a r)"), ident_f[:, :])
                    tmp = ga.alloc([128], F32)
                    evac(tmp[0:36, :], pb[0:36, 0:128])
                    for a in range(12):
                        S.dma(pdconv_d[l][:, a * 128:(a + 1) * 128], tmp[a * 3:a * 3 + 3, :], is_out=True)
                ga.reset(mA2)

                CP(5)
                GC = ga.alloc([TBL], F32)
                nlast = len(chunk_last_cols(L))
                GLc = ga.alloc([nlast], F32)
                decS = ga.alloc([4, nlast], F32)
                colG = ga.alloc([len(L["chunks"]), 8], F32)
                mAs = ga.mark()
                AB = ga.alloc([TBL], F32)
                Bt = ga.alloc([TBL], F32)
                G = ga.alloc([TBL], F32)
                GL = ga.alloc([TBL], F32)
                EKT = ga.alloc([TBL], F32)
                EG = ga.alloc([512], F32)
                wb = load_w(wl, C_AL, 8, nwcol)

                def cons(pb, t0, n, kind):
                    evac(AB[0:8, t0:t0 + n], pb[0:8, 0:n])

                project(wb, 8, L, cons)
                flush()
                A8, B8, G8, GC8, GL8, EK8 = AB[0:8, :], Bt[0:8, :], G[0:8, :], GC[0:8, :], GL[0:8, :], EKT[0:8, :]
                S.act(B8, A8, AF.Sigmoid)
                S.act(G8, A8, AF.Exp, bias=prm8[:, l, 0:1])
                S.act(G8, G8, AF.Ln, bias=1.0)
                S.ts(G8, G8, prm8[:, l, 1:2], op0=ALU.mult)
                S.scan(GC8, reset_t[0:8, 0:TBL], G8, 0.0)
                chunk_last(GL8, GC8, L)
                S.tt(EK8, GL8, GC8, ALU.subtract)
                S.act(EK8, EK8, AF.Exp)
                compact_last(GLc[0:8, :], GC8, L)
                for h in range(4):
                    pb = ps()
                    S.mm(pb[:, 0:nlast], sel[:, h, :], GLc[0:8, :])
                    S.act(decS[:, h, :], pb[:, 0:nlast], AF.Exp)
                for (t0, n, kind) in ttiles(L):
                    for h in range(4):
                        pg = ps()
                        S.mm(pg[:, 0:n], sel[:, h, :], GC8[:, t0:t0 + n])
                        S.act(EG[:, 0:n], pg[:, 0:n], AF.Exp)
                        S.tt(KQG[:, h, 0, t0:t0 + n], KN[:, h, t0:t0 + n], EG[:, 0:n], ALU.mult)
                        S.tt(KQG[:, h, 1, t0:t0 + n], QN[:, h, t0:t0 + n], EG[:, 0:n], ALU.mult)
                        pbb = ps()
                        S.mm(pbb[:, 0:n], sel[:, 4 + h, :], B8[:, t0:t0 + n])
                        S.tt(BKN[:, h, t0:t0 + n], KN[:, h, t0:t0 + n], pbb[:, 0:n], ALU.mult)
                        pe_ = ps()
                        S.mm(pe_[:, 0:n], sel[:, h, :], EK8[:, t0:t0 + n])
                        S.tt(BKT[:, h, t0:t0 + n], BKN[:, h, t0:t0 + n], pe_[:, 0:n], ALU.mult)
                for ci, (c0, cc, kind, g0) in enumerate(L["chunks"]):
                    pb = ps()
                    S.transpose(pb[0:cc, 0:8], GC8[:, c0:c0 + cc], ident_f[0:8, 0:8])
                    evac(colG[0:cc, ci, :], pb[0:cc, 0:8])

                CP(6)
                ga.reset(mAs)
                mA3 = ga.mark()
                nchk = len(L["chunks"])
                RT = ga.alloc([4, 64], BF16)
                OIN = ga.alloc([4, 64], F32)
                Rtok = ga.alloc([4, 128], BF16)
                Wb = ga.alloc([4, 128], BF16)

                def mkset():
                    d = {}
                    d["ARG"] = ga.alloc([4, 64], F32)
                    d["EI"] = ga.alloc([4, 64], F32)
                    d["ES"] = ga.alloc([4, 64], F32)
                    d["X"] = [ga.alloc([4, 64], F32) for _ in range(2)]
                    d["XT"] = [ga.alloc([4, 64], F32) for _ in range(2)]
                    d["PP"] = [ga.alloc([4, 64], F32) for _ in range(2)]
                    d["Xb"] = [ga.alloc([4, 64], BF16) for _ in range(2)]
                    d["XTb"] = [ga.alloc([4, 64], BF16) for _ in range(2)]
                    d["PPb"] = ga.alloc([4, 64], BF16)
                    d["AT"] = ga.alloc([4, 64], BF16)
                    d["TTb"] = ga.alloc([4, 64], BF16)
                    d["BKtok"] = ga.alloc([4, 128], BF16)
                    return d

                psets = [None, None]
                psets[(nchk + 1) % 2] = mkset()
                msamp = ga.mark()
                psets[nchk % 2] = mkset()

                def prep_gen(ci):
                    c0, c, kind, g0 = L["chunks"][ci]
                    isS = (kind == "s")
                    negm = cm["negmask_s"] if isS else cm["negmask_p"]
                    strm = cm["strict_s"] if isS else cm["strict_p"]
                    levels = 2 if isS else (6 if c == 64 else 4)
                    d = psets[ci % 2]
                    ARG, EI, ES, X, XT, PP, AT, TTb, BKtok = (d["ARG"], d["EI"], d["ES"], d["X"], d["XT"], d["PP"],
                                                              d["AT"], d["TTb"], d["BKtok"])
                    pg = ps()
                    for h in range(4):
                        S.mm(pg[0:c, h * 64:h * 64 + c], sel[:, h, 0:c], GC8[:, c0:c0 + c])
                    pk = ps()
                    pq = ps()
                    for h in range(4):
                        S.mm(pk[0:c, h * 64:h * 64 + c], BKN[:, h, c0:c0 + c], KN[:, h, c0:c0 + c])
                    for h in range(4):
                        S.mm(pq[0:c, h * 64:h * 64 + c], BKN[:, h, c0:c0 + c], QN[:, h, c0:c0 + c])
                    pbt = ps()
                    pbtb = pbt[:, :].bitcast(BF16)
                    for h in range(4):
                        S.transpose(pbtb[0:c, h * 128:(h + 1) * 128], BKT[:, h, c0:c0 + c], ident_b[:, :], inc=(h == 3))
                    for h in range(4):
                        S.stt(ARG[0:c, h, 0:c], pg[0:c, h * 64:h * 64 + c], colG[0:c, ci, h:h + 1], negm[0:c, 0:c],
                              ALU.subtract, ALU.add)
                    S.act(EI[0:c, :, 0:c], ARG[0:c, :, 0:c], AF.Exp)
                    S.tt(ES[0:c, :, 0:c], EI[0:c, :, 0:c], strm[0:c, 0:c].unsqueeze(1).to_broadcast([c, 4, c]), ALU.mult,
                         e="pool")
                    pk3 = pk[0:c, 0:256].rearrange("p (h i) -> p h i", h=4)[:, :, 0:c]
                    pq3 = pq[0:c, 0:256].rearrange("p (h i) -> p h i", h=4)[:, :, 0:c]
                    S.tt(X[0][0:c, :, 0:c], pk3, ES[0:c, :, 0:c], ALU.mult)
                    S.tt(AT[0:c, :, 0:c], pq3, EI[0:c, :, 0:c], ALU.mult)
                    evac(BKtok[0:c, :, :], pbtb[0:c, 0:512].rearrange("p (h d) -> p h d", h=4))
                    yield
                    pt = ps()
                    for h in range(4):
                        S.transpose(pt[0:c, h * 64:h * 64 + c], X[0][0:c, h, 0:c], ident_f[0:c, 0:c], inc=(h == 3))
                    pt3 = pt[0:c, 0:256].rearrange("p (h i) -> p h i", h=4)[:, :, 0:c]
                    evac(XT[0][0:c, :, 0:c], pt3)
                    S.tt(PP[0][0:c, :, 0:c], X[0][0:c, :, 0:c], ident_f[0:c, 0:c].unsqueeze(1).to_broadcast([c, 4, c]),
                         ALU.add, e="pool")
                    Xb, XTb, PPb = d["Xb"], d["XTb"], d["PPb"]
                    cur = 0
                    nlev = levels - 1
                    NF32 = 1
                    v3 = lambda p_: p_[0:c, 0:256].rearrange("p (h i) -> p h i", h=4)[:, :, 0:c]
                    for lv in range(nlev):
                        nxt = 1 - cur
                        lastlv = (lv == nlev - 1)
                        lowp = (lv >= NF32)
                        nlow = (lv + 1 >= NF32) and not lastlv
                        Xc, XTc = (Xb[cur], XTb[cur]) if lowp else (X[cur], XT[cur])
                        yield
                        pxt = ps()
                        for h in range(4):
                            S.mm(pxt[0:c, h * 64:h * 64 + c], Xc[0:c, h, 0:c], XTc[0:c, h, 0:c])
                        if not lastlv:
                            px = ps()
                            for h in range(4):
                                S.mm(px[0:c, h * 64:h * 64 + c], XTc[0:c, h, 0:c], Xc[0:c, h, 0:c])
                        XTn = XTb[nxt] if lowp else XT[nxt]
                        evac(XTn[0:c, :, 0:c], v3(pxt))
                        if nlow and not lowp:
                            evac(XTb[nxt][0:c, :, 0:c], v3(pxt))
                        if not lastlv:
                            if lowp:
                                evac(Xb[nxt][0:c, :, 0:c], v3(px))
                            else:
                                if nlow:
                                    evac(Xb[nxt][0:c, :, 0:c], v3(px))
                                else:
                                    evac(X[nxt][0:c, :, 0:c], v3(px))
                        yield
                        pp = ps()
                        for h in range(4):
                            if lowp:
                                S.mm(pp[0:c, h * 64:h * 64 + c], XTb[nxt][0:c, h, 0:c], PPb[0:c, h, 0:c])
                            else:
                                S.mm(pp[0:c, h * 64:h * 64 + c], XT[nxt][0:c, h, 0:c], PP[cur][0:c, h, 0:c])
                        if lastlv:
                            S.tt(TTb[0:c, :, 0:c], v3(pp), PP[cur][0:c, :, 0:c], ALU.add)
                        else:
                            S.tt(PP[nxt][0:c, :, 0:c], v3(pp), PP[cur][0:c, :, 0:c], ALU.add)
                            if nlow:
                                S.copy(PPb[0:c, :, 0:c], PP[nxt][0:c, :, 0:c], e="act")
                        cur = nxt

                def chain_gen(ci):
                    c0, c, kind, g0 = L["chunks"][ci]
                    isS = (kind == "s")
                    d = psets[ci % 2]
                    AT, TTb, BKtok = d["AT"], d["TTb"], d["BKtok"]
                    if not isS:
                        ppq = ps()
                        for h in range(4):
                            S.mm(ppq[:, h * 128:h * 128 + 2 * c].rearrange("p (a b) -> p a b", a=2), Sdb[:, h, :],
                                 KQG[:, h, :, c0:c0 + c])
                        for h in range(4):
                            v = ppq[:, h * 128:h * 128 + 2 * c].rearrange("p (a b) -> p a b", a=2)
                            S.tt(RT[:, h, 0:c], VT[:, h, c0:c0 + c], v[:, 0, :], ALU.subtract)
                        for h in range(4):
                            v = ppq[:, h * 128:h * 128 + 2 * c].rearrange("p (a b) -> p a b", a=2)
                            S.copy(OIN[:, h, 0:c], v[:, 1, :], e="act")
                        yield
                        prt = ps()
                        prtb = prt[:, :].bitcast(BF16)
                        for h in range(4):
                            S.transpose(prtb[0:c, h * 128:(h + 1) * 128], RT[:, h, 0:c], ident_b[:, :], inc=(h == 3))
                        evac(Rtok[0:c, :, :], prtb[0:c, 0:512].rearrange("p (h d) -> p h d", h=4))
                        yield
                        pw = ps()
                        for h in range(4):
                            S.mm(pw[0:c, h * 128:(h + 1) * 128], TTb[0:c, h, 0:c], Rtok[0:c, h, :])
                        evac(Wb[0:c, :, :], pw[0:c, :].rearrange("p (h d) -> p h d", h=4))
                        yield
                        pS = ps()
                        for h in range(4):
                            S.mm(pS[:, h * 128:(h + 1) * 128], BKtok[0:c, h, :], Wb[0:c, h, :])
                        po = ps()
                        for h in range(4):
                            S.mm(po[:, h * 64:h * 64 + c], Wb[0:c, h, :], AT[0:c, h, 0:c])
                        S.tt(Sd[:, :, :], Sd[:, :, :], decS[:, :, ci:ci + 1].to_broadcast([128, 4, 128]), ALU.mult)
                        S.tt(Sd[:, :, :], Sd[:, :, :], pS[:, :].rearrange("p (h d) -> p h d", h=4), ALU.add)
                        S.copy(Sdb[:, :, :], Sd[:, :, :], e="act")
                        S.tt(OT[:, :, c0:c0 + c], po[:, 0:256].rearrange("p (h i) -> p h i", h=4)[:, :, 0:c],
                             OIN[:, :, 0:c], ALU.add)
                    else:
                        mk_ = ga.mark()
                        ga.reset(msamp)
                        kbase = len(L["chunks"]) - 1
                        Ss = ga.alloc([16, 128], F32)
                        Ssb = ga.alloc([16, 128], BF16)
                        Sn = ga.alloc([16, 128], F32)
                        BKm = ga.alloc([16, 128], BF16)
                        for h in range(4):
                            S.dma(Ss[:, :, :], sdelta_d[l][:, h].rearrange("s k v -> k s v"))
                            S.copy(Ssb[:, :, :], Ss[:, :, :], e="pool")
                            ppq = ps()
                            for s in range(16):
                                for a_ in range(2):
                                    S.mm(ppq[:, a_ * 64 + 4 * s:a_ * 64 + 4 * s + 4], Ssb[:, s, :],
                                         KQG[:, h, a_, c0 + 4 * s:c0 + 4 * s + 4])
                            v = ppq[:, 0:128].rearrange("p (a b) -> p a b", a=2)
                            S.tt(RT[:, h, :], VT[:, h, c0:c0 + 64], v[:, 0, :], ALU.subtract)
                            evac(OIN[:, h, :], v[:, 1, :])
                            prt = ps()
                            prtb = prt[:, :].bitcast(BF16)
                            S.transpose(prtb[0:64, 0:128], RT[:, h, :], ident_b[:, :])
                            evac(Rtok[0:64, h, :], prtb[0:64, 0:128])
                            pw = ps()
                            S.mm(pw[0:64, 0:128], TTb[0:64, h, :], Rtok[0:64, h, :])
                            evac(Wb[0:64, h, :], pw[0:64, 0:128])
                            po = ps()
                            S.mm(po[:, 0:64], Wb[0:64, h, :], AT[0:64, h, :])
                            S.tt(OT[:, h, c0:c0 + 64], po[:, 0:64], OIN[:, h, :], ALU.add)
                            S.tt(BKm[0:64, :, :], BKtok[0:64, h, :].unsqueeze(1).to_broadcast([64, 16, 128]),
                                 seqmask_b[0:64, :].unsqueeze(2).to_broadcast([64, 16, 128]), ALU.mult, e="pool")
                            S.tt(Sn[:, :, :], Ss[:, :, :],
                                 decS[:, h, kbase:kbase + 16].unsqueeze(2).to_broadcast([128, 16, 128]), ALU.mult, e="pool")
                            for q4 in range(4):
                                pS = ps()
                                for s4 in range(4):
                                    s = q4 * 4 + s4
                                    S.mm(pS[:, s4 * 128:(s4 + 1) * 128], BKm[0:64, s, :], Wb[0:64, h, :])
                                S.tt(Sn[:, q4 * 4:q4 * 4 + 4, :], Sn[:, q4 * 4:q4 * 4 + 4, :],
                                     pS[:, :].rearrange("p (s d) -> p s d", s=4), ALU.add)
                            S.dma(sdelta_o[l][:, h].rearrange("s k v -> k s v"), Sn[:, :, :], is_out=True)
                        ga.reset(mk_)

                def step(g):
                    if g is None:
                        return None
                    try:
                        next(g)
                        return g
                    except StopIteration:
                        return None

                g = prep_gen(0)
                while g is not None:
                    g = step(g)
                for ci in range(nchk):
                    gp = prep_gen(ci + 1) if ci + 1 < nchk else None
                    gc = chain_gen(ci)
                    while gp is not None or gc is not None:
                        for _ in range(3):
                            gp = step(gp)
                        gc = step(gc)
                ga.reset(mA3)
                if last:
                    for h in range(4):
                        S.dma(pdelta_d[l][h], Sd[:, h, :], is_out=True)
                CP(9)
                for h in range(4):
                    gate_mix(h, OT[:, h, :], pcols[:, P_NA + l:P_NA + l + 1], True)
                flush()
                ga.reset(mA)

                CP(10)
                mB = ga.mark()
                XB = ga.alloc([TBL], F32)
                XBb = ga.alloc([TBL], BF16)
                Rg = ga.alloc([TBL], F32)
                Ig = ga.alloc([TBL], F32)
                Hh = ga.alloc([TBL], F32)
                prew = ga.alloc([3 + max(npc, 1)], F32)
                presw = ga.alloc([16, 7], F32)
                cstb = ga.alloc([4, 48], F32)
                h0 = ga.alloc([4, 16], F32)
                hs_out = ga.alloc([4, 16], F32)
                tmp16 = ga.alloc([16], F32)
                if last:
                    stg = sstg[0][0:48, 0:4, :]
                    S.dma(stg, slconv_d[l].rearrange("r (a p) -> r a p", p=128))
                    pb = ps()
                    for a in range(4):
                        S.transpose(pb[:, a * 48:(a + 1) * 48], sstg[0][0:48, a, :], ident_f[0:48, 0:48], inc=(a == 3))
                    evac(cstb[:, :, :], pb[:, 0:192].rearrange("p (a b) -> p a b", a=4))
                    stg = sstg[1][0:16, 0:4, :]
                    S.dma(stg, slru_d[l].rearrange("s (a p) -> s a p", p=128))
                    pb = ps()
                    for a in range(4):
                        S.transpose(pb[:, a * 16:(a + 1) * 16], sstg[1][0:16, a, :], ident_f[0:16, 0:16], inc=(a == 3))
                    evac(h0[:, :, :], pb[:, 0:64].rearrange("p (a b) -> p a b", a=4))
                for n_ in range(4):
                    wb = load_w(wl, C_XB + n_ * 128, 128, nwcol)
                    S.copy(prew[:, 0:3], carryB[:, n_, :], e="pool")
                    if last:
                        S.copy(presw[:, :, 0:3], cstb[:, n_, :].rearrange("p (s r) -> p s r", r=3), e="pool")

                    def cons(pb, t0, n, kind):
                        if kind == "p":
                            evac(prew[:, 3 + t0:3 + t0 + n], pb[:, 0:n])
                        else:
                            evac(presw[:, :, 3:7], pb[:, 0:64].rearrange("p (s t) -> p s t", t=4))

                    def post(n_=n_):
                        wcolk = [pcols[:, P_CONVB + (l * 4 + k) * 4 + n_: P_CONVB + (l * 4 + k) * 4 + n_ + 1] for k in range(4)]
                        bcol = pcols[:, P_CONVBB + l * 4 + n_: P_CONVBB + l * 4 + n_ + 1]
                        if npc > 0:
                            S.act(XB[:, 0:npc], prew[:, 3:3 + npc], AF.Identity, scale=wcolk[3], bias=bcol)
                            for k in (2, 1, 0):
                                S.stt(XB[:, 0:npc], prew[:, k:k + npc], wcolk[k], XB[:, 0:npc], ALU.mult, ALU.add)
                            S.copy(carryB[:, n_, :], prew[:, npc:npc + 3], e="pool")
                        if last:
                            a3 = bview(XB[:, npc:npc + 64], 16, 4)
                            S.act(a3, presw[:, :, 3:7], AF.Identity, scale=wcolk[3], bias=bcol)
                            for k in (2, 1, 0):
                                S.stt(a3, presw[:, :, k:k + 4], wcolk[k], a3, ALU.mult, ALU.add)
                            S.copy(cstb[:, n_, :].rearrange("p (s r) -> p s r", r=3), presw[:, :, 4:7], e="pool")
                        S.copy(XBb[:, 0:TBL], XB[:, 0:TBL], e="act")
                        for (t0, n, kind) in ttiles(L):
                            pr = ps()
                            S.mm(pr[:, 0:n], lwa[:, n_, :], XBb[:, t0:t0 + n])
                            S.act(Rg[:, t0:t0 + n], pr[:, 0:n], AF.Sigmoid,
                                  bias=pcols[:, P_LBA + l * 4 + n_:P_LBA + l * 4 + n_ + 1])
                            pi = ps()
                            S.mm(pi[:, 0:n], lwx[:, n_, :], XBb[:, t0:t0 + n])
                            S.act(Ig[:, t0:t0 + n], pi[:, 0:n], AF.Sigmoid,
                                  bias=pcols[:, P_LBX + l * 4 + n_:P_LBX + l * 4 + n_ + 1])
                        S.act(Rg[:, 0:TBL], Rg[:, 0:TBL], AF.Exp, scale=pcols[:, P_NSP8 + l * 4 + n_:P_NSP8 + l * 4 + n_ + 1])
                        S.tt(Hh[:, 0:TBL], Rg[:, 0:TBL], Rg[:, 0:TBL], ALU.mult)
                        S.ts(Hh[:, 0:TBL], Hh[:, 0:TBL], -1.0, 1.0, op0=ALU.mult, op1=ALU.add, e="pool")
                        S.act(Hh[:, 0:TBL], Hh[:, 0:TBL], AF.Sqrt)
                        S.tt(Ig[:, 0:TBL], Ig[:, 0:TBL], Hh[:, 0:TBL], ALU.mult)
                        S.tt(Ig[:, 0:TBL], Ig[:, 0:TBL], XB[:, 0:TBL], ALU.mult, e="pool")
                        if last:
                            A3 = bview(Rg[:, npc:npc + 64], 16, 4)
                            B3 = bview(Ig[:, npc:npc + 64], 16, 4)
                            S.tt(tmp16[:, :], A3[:, :, 0], h0[:, n_, :], ALU.mult)
                            S.tt(B3[:, :, 0], B3[:, :, 0], tmp16[:, :], ALU.add)
                            S.memset(A3[:, :, 0], 0.0)
                        if npc > 0:
                            S.scan(Hh[:, 0:npc], Rg[:, 0:npc], Ig[:, 0:npc], hl[:, n_:n_ + 1])
                            S.copy(hl[:, n_:n_ + 1], Hh[:, npc - 1:npc], e="pool")
                        if last:
                            S.scan(Hh[:, npc:npc + 64], Rg[:, npc:npc + 64], Ig[:, npc:npc + 64], 0.0)
                            S.copy(hs_out[:, n_, :], bview(Hh[:, npc:npc + 64], 16, 4)[:, :, 3], e="pool")

                    project(wb, 128, L, cons, post)
                    gate_mix(4 + n_, Hh, None, False)
                flush()
                if last:
                    pb = ps()
                    S.transpose(pb[0:4, 0:128], hl[:, :], ident_f[:, :])
                    t4 = ga.alloc([128], F32)
                    evac(t4[0:4, :], pb[0:4, 0:128])
                    S.dma(plru_d[l], t4[0:4, :], is_out=True)
                    pb = ps()
                    for a in range(4):
                        S.transpose(pb[0:16, a * 128:(a + 1) * 128], hs_out[:, a, :], ident_f[:, :], inc=(a == 3))
                    t5 = ga.alloc([512], F32)
                    evac(t5[0:16, :], pb[0:16, :])
                    S.dma(slru_o[l], t5[0:16, :], is_out=True)
                    pb = ps()
                    for a in range(4):
                        S.transpose(pb[0:48, a * 128:(a + 1) * 128], cstb[:, a, :], ident_f[:, :], inc=(a == 3))
                    t6 = ga.alloc([512], F32)
                    evac(t6[0:48, :], pb[0:48, :])
                    S.dma(slconv_o[l], t6[0:48, :], is_out=True)
                    pb = ps()
                    S.transpose(pb[0:12, 0:128], carryB[:, :, :].rearrange("p a r -> p (a r)"), ident_f[:, :])
                    t7 = ga.alloc([128], F32)
                    evac(t7[0:12, :], pb[0:12, 0:128])
                    for a in range(4):
                        S.dma(plconv_d[l][:, a * 128:(a + 1) * 128], t7[a * 3:a * 3 + 3, :], is_out=True)
                ga.reset(mB)

                CP(11)
                for grp in ("C", "D"):
                    mC = ga.mark()
                    QA = ga.alloc([2, TBL], BF16)
                    QS_ = ga.alloc([2, TBL], BF16)
                    KA = ga.alloc([2, TBL], BF16)
                    KS_ = ga.alloc([2, TBL], BF16)
                    VT2 = ga.alloc([4, TBL], BF16)
                    OT2 = ga.alloc([4, TBL], BF16)
                    nlast = len(chunk_last_cols(L))
                    decC = ga.alloc([2, nlast], F32)
                    Sx, Sxb = (Sg, Sgb) if grp == "C" else (Sr, Srb)
                    sst_d, sst_o, pst_d = (sgla_d, sgla_o, pgla_d) if grp == "C" else (sret_d, sret_o, pret_d)
                    cq, ck, cv = (C_QC, C_KC, C_VC) if grp == "C" else (C_QD, C_KD, C_VD)
                    mC2 = ga.mark()
                    if grp == "C":
                        RCT = ga.alloc([TBL], BF16)
                        LT = ga.alloc([TBL], F32)
                        CS = ga.alloc([2, TBL], F32)
                        CSL = ga.alloc([2, TBL], F32)
                        EB = ga.alloc([2, TBL], F32)
                        EBN = ga.alloc([2, TBL], F32)
                        EKS = ga.alloc([2, TBL], F32)
                        CLc = ga.alloc([2, nlast], F32)
                        wb = load_w(wl, C_RC, 16, nwcol)

                        def cons(pb, t0, n, kind):
                            evac(RCT[0:16, t0:t0 + n], pb[0:16, 0:n])

                        project(wb, 16, L, cons)
                        flush()
                        for t in range(2):
                            for (t0, n, kind) in ttiles(L):
                                pz = ps()
                                S.mm(pz[:, 0:n], w2b[0:16, t * 128:(t + 1) * 128], RCT[0:16, t0:t0 + n])
                                S.act(LT[:, t0:t0 + n], pz[:, 0:n], AF.Exp, scale=-1.0,
                                      bias=pcols[:, P_NB2 + l * 2 + t:P_NB2 + l * 2 + t + 1])
                            S.act(LT[:, 0:TBL], LT[:, 0:TBL], AF.Ln, bias=1.0)
                            S.scan(CS[:, t, :], reset_t[:, 0:TBL], LT[:, 0:TBL], 0.0)
                            chunk_last(CSL[:, t, :], CS[:, t, :], L)
                            compact_last(CLc[:, t, :], CS[:, t, :], L)
                        S.act(EB[:, :, :], CS[:, :, :], AF.Exp, scale=-1.0 / 16)
                        S.act(EBN[:, :, :], CS[:, :, :], AF.Exp, scale=1.0 / 16)
                        S.tt(EKS[:, :, :], CS[:, :, :], CSL[:, :, :], ALU.subtract)
                        S.act(EKS[:, :, :], EKS[:, :, :], AF.Exp, scale=1.0 / 16)
                        S.act(decC[:, :, :], CLc[:, :, :], AF.Exp, scale=-1.0 / 16)
                        for t in range(2):
                            wb = load_w(wl, cq + t * 128, 128, nwcol)

                            def cons(pb, t0, n, kind, t=t):
                                S.stt(QA[:, t, t0:t0 + n], pb[:, 0:n], 0.125, EB[:, t, t0:t0 + n], ALU.mult, ALU.mult)

                            project(wb, 128, L, cons)
                            wb = load_w(wl, ck + t * 128, 128, nwcol)

                            def cons(pb, t0, n, kind, t=t):
                                S.tt(KA[:, t, t0:t0 + n], pb[:, 0:n], EBN[:, t, t0:t0 + n], ALU.mult)
                                S.tt(KS_[:, t, t0:t0 + n], pb[:, 0:n], EKS[:, t, t0:t0 + n], ALU.mult)

                            project(wb, 128, L, cons)
                        QSt = QA
                    else:
                        QRf = ga.alloc([TBL], F32)
                        T1 = ga.alloc([512], F32)
                        T2 = ga.alloc([512], F32)
                        Qb = ga.alloc([512], BF16)
                        for which in range(2):
                            for t in range(2):
                                wb = load_w(wl, (cq if which == 0 else ck) + t * 128, 128, nwcol)

                                def cons(pb, t0, n, kind):
                                    S.copy(Qb[:, 0:n], pb[:, 0:n], e="act")
                                    pm = ps()
                                    S.mm(pm[:, 0:n], perm_b[:, :], Qb[:, 0:n])
                                    S.tt(T1[:, 0:n], pb[:, 0:n], cos_t[:, t0:t0 + n], ALU.mult)
                                    S.tt(T2[:, 0:n], pm[:, 0:n], sin_t[:, t0:t0 + n], ALU.mult)
                                    S.tt(QRf[:, t0:t0 + n], T1[:, 0:n], T2[:, 0:n], ALU.add, e="pool")

                                def post(which=which, t=t):
                                    dA = QA if which == 0 else KA
                                    dS = QS_ if which == 0 else KS_
                                    evac(dA[:, t, :], QRf[:, 0:TBL])
                                    c0 = 0
                                    if L["chunks"][0][1] == 16:
                                        tb = cm["fs64"][:, t, 0:16] if which == 0 else cm["ts16"][:, t, :]
                                        S.tt(dS[:, t, 0:16], QRf[:, 0:16], tb, ALU.mult)
                                        c0 = 16
                                    n64 = (npc - c0) // 64
                                    if n64 > 0:
                                        tb = cm["fs64"][:, t, :] if which == 0 else cm["ts64"][:, t, :]
                                        S.tt(bview(dS[:, t, c0:npc], n64, 64), bview(QRf[:, c0:npc], n64, 64),
                                             tb.unsqueeze(1).to_broadcast([128, n64, 64]), ALU.mult)
                                    if last:
                                        tb = cm["fss"][:, t, :] if which == 0 else cm["tss"][:, t, :]
                                        S.tt(dS[:, t, npc:npc + 64], QRf[:, npc:npc + 64], tb, ALU.mult)

                                project(wb, 128, L, cons, post)
                        QSt = QS_
                    for h in range(4):
                        wb = load_w(wl, cv + h * 128, 128, nwcol)

                        def cons(pb, t0, n, kind, h=h):
                            evac(VT2[:, h, t0:t0 + n], pb[:, 0:n])

                        project(wb, 128, L, cons)
                    flush()
                    ga.reset(mC2)
                    QAm = ga.alloc([2, 2, TBL], BF16)
                    S.memset(QAm[:, :, :, :], 0.0)
                    for t in range(2):
                        evac(QAm[0:64, t, 0, :], QA[0:64, t, :])
                        evac(QAm[64:128, t, 1, :], QA[64:128, t, :])
                    if grp == "C":
                        QSm = QAm
                    else:
                        QSm = ga.alloc([2, 2, TBL], BF16)
                        S.memset(QSm[:, :, :, :], 0.0)
                        for t in range(2):
                            evac(QSm[0:64, t, 0, :], QS_[0:64, t, :])
                            evac(QSm[64:128, t, 1, :], QS_[64:128, t, :])
                    ATs = [ga.alloc([4, 64], BF16) for _ in range(2)]
                    Vtoks = [ga.alloc([4, 128], BF16) for _ in range(2)]
                    KStoks = [ga.alloc([2, 128], BF16) for _ in range(2)]
                    mC3 = ga.mark()

                    def stage1(ci):
                        c0, c, kind, g0 = L["chunks"][ci]
                        isS = (kind == "s")
                        AT, Vtok, KStok = ATs[ci % 2], Vtoks[ci % 2], KStoks[ci % 2]
                        pa = ps()
                        for h in range(4):
                            t, e_ = h // 2, h % 2
                            S.mm(pa[0:c, h * 64:h * 64 + c], KA[:, t, c0:c0 + c], QAm[:, t, e_, c0:c0 + c])
                        pa3 = pa[0:c, 0:256].rearrange("p (h i) -> p h i", h=4)[:, :, 0:c]
                        if grp == "C":
                            m_ = cm["incl_s"] if isS else cm["incl_p"]
                            S.tt(AT[0:c, :, 0:c], pa3, m_[0:c, 0:c].unsqueeze(1).to_broadcast([c, 4, c]), ALU.mult)
                        else:
                            m_ = cm["retm_s"] if isS else cm["retm_p"]
                            S.tt(AT[0:c, :, 0:c], pa3, m_[0:c, :, 0:c], ALU.mult)
                        pv = ps()
                        pvb = pv[:, :].bitcast(BF16)
                        for h in range(4):
                            S.transpose(pvb[0:c, h * 128:(h + 1) * 128], VT2[:, h, c0:c0 + c], ident_b[:, :], inc=(h == 3))
                        evac(Vtok[0:c, :, :], pvb[0:c, 0:512].rearrange("p (h d) -> p h d", h=4))
                        pk = ps()
                        pkb = pk[:, :].bitcast(BF16)
                        for t in range(2):
                            S.transpose(pkb[0:c, t * 128:(t + 1) * 128], KS_[:, t, c0:c0 + c], ident_b[:, :], inc=(t == 1))
                        evac(KStok[0:c, :, :], pkb[0:c, 0:256].rearrange("p (t d) -> p t d", t=2))

                    def stage2(ci):
                        c0, c, kind, g0 = L["chunks"][ci]
                        isS = (kind == "s")
                        AT, Vtok, KStok = ATs[ci % 2], Vtoks[ci % 2], KStoks[ci % 2]
                        if not isS:
                            po = ps()
                            for h in range(4):
                                t, e_ = h // 2, h % 2
                                S.mm(po[:, h * 64:h * 64 + c], Sxb[:, t, :], QSm[:, t, e_, c0:c0 + c], start=True, stop=False)
                                S.mm(po[:, h * 64:h * 64 + c], Vtok[0:c, h, :], AT[0:c, h, 0:c], start=False, stop=True)
                            evac(OT2[:, :, c0:c0 + c], po[:, 0:256].rearrange("p (h i) -> p h i", h=4)[:, :, 0:c])
                            for e_ in range(2):
                                hp = 64 * e_
                                pS = ps()
                                for t in range(2):
                                    S.mm(pS[hp:hp + 64, t * 128:(t + 1) * 128], KStok[0:c, t, hp:hp + 64],
                                         Vtok[0:c, 2 * t + e_, :])
                                for t in range(2):
                                    if grp == "C":
                                        dcol = decC[hp:hp + 64, t, ci:ci + 1]
                                    else:
                                        dcol = cm["retdec"][hp:hp + 64, t, (0 if c == 64 else 1):(1 if c == 64 else 2)]
                                    S.stt(Sx[hp:hp + 64, t, :], Sx[hp:hp + 64, t, :], dcol,
                                          pS[hp:hp + 64, t * 128:(t + 1) * 128], ALU.mult, ALU.add)
                            S.copy(Sxb[:, :, :], Sx[:, :, :], e="act")
                        else:
                            ga.reset(mC3)
                            kbase = len(L["chunks"]) - 1
                            Ss = ga.alloc([16, 2, 128], F32)
                            Ssb = ga.alloc([16, 2, 128], BF16)
                            Sn = ga.alloc([16, 2, 128], F32)
                            KSm = ga.alloc([16, 2, 128], BF16)
                            for hp_ in range(2):
                                S.dma(Ss[hp_ * 64:(hp_ + 1) * 64, :, :, :],
                                      sst_d[l].rearrange("s (t e) k v -> e k s t v", e=2)[hp_])
                            S.copy(Ssb[:, :, :, :], Ss[:, :, :, :], e="pool")
                            S.tt(KSm[0:64, :, :, :], KStok[0:64, :, :].unsqueeze(1).to_broadcast([64, 16, 2, 128]),
                                 seqmask_b[0:64, :].unsqueeze(2).unsqueeze(3).to_broadcast([64, 16, 2, 128]), ALU.mult,
                                 e="pool")
                            if grp == "C":
                                S.tt(Sn[:, :, :, :], Ss[:, :, :, :],
                                     decC[:, :, kbase:kbase + 16].rearrange("p t s -> p s t").unsqueeze(3)
                                     .to_broadcast([128, 16, 2, 128]), ALU.mult, e="pool")
                            else:
                                S.tt(Sn[:, :, :, :], Ss[:, :, :, :],
                                     cm["retdec"][:, :, 2:3].unsqueeze(1).to_broadcast([128, 16, 2, 128]), ALU.mult,
                                     e="pool")
                            po = ps()
                            for h in range(4):
                                t, e_ = h // 2, h % 2
                                S.mm(po[:, h * 64:h * 64 + 64], Vtok[0:64, h, :], AT[0:64, h, :], start=True, stop=False)
                                for s_i in range(16):
                                    S.mm(po[:, h * 64 + 4 * s_i:h * 64 + 4 * s_i + 4], Ssb[:, s_i, t, :],
                                         QSm[:, t, e_, c0 + 4 * s_i:c0 + 4 * s_i + 4], start=False, stop=(s_i == 15))
                            evac(OT2[:, :, c0:c0 + 64], po[:, 0:256].rearrange("p (h i) -> p h i", h=4))
                            for e_ in range(2):
                                hp = 64 * e_
                                for s2 in range(8):
                                    pS = ps()
                                    for s_ in range(2):
                                        s_i = s2 * 2 + s_
                                        for t in range(2):
                                            S.mm(pS[hp:hp + 64, (s_ * 2 + t) * 128:(s_ * 2 + t + 1) * 128],
                                                 KSm[0:64, s_i, t, hp:hp + 64], Vtok[0:64, 2 * t + e_, :])
                                    S.tt(Sn[hp:hp + 64, s2 * 2:s2 * 2 + 2, :, :], Sn[hp:hp + 64, s2 * 2:s2 * 2 + 2, :, :],
                                         pS[hp:hp + 64, :].rearrange("p (s t d) -> p s t d", s=2, t=2), ALU.add)
                            for hp_ in range(2):
                                S.dma(sst_o[l].rearrange("s (t e) k v -> e k s t v", e=2)[hp_],
                                      Sn[hp_ * 64:(hp_ + 1) * 64, :, :, :], is_out=True)

                    nchk = len(L["chunks"])
                    stage1(0)
                    for ci in range(nchk):
                        if ci + 1 < nchk:
                            stage1(ci + 1)
                        stage2(ci)
                    ga.reset(mC3)
                    if last:
                        for h in range(4):
                            t, hp = h // 2, (h % 2) * 64
                            S.dma(pst_d[l][h], Sx[hp:hp + 64, t, :], is_out=True)
                    for h in range(4):
                        mt = (8 if grp == "C" else 12) + h
                        gate_mix(mt, OT2[:, h, :], pcols[:, P_NC + l:P_NC + l + 1] if grp == "C" else None, True)
                    flush()
                    ga.reset(mC)

                CP(12)
                flush()
                mO = ga.mark()
                xo2 = [ga.alloc([256], F32) for _ in range(4)]
                for cb in range(8):
                    e0 = w_get("out", l, cb * 256, 128)
                    w_get("out", l, cb * 256 + 128, 128)
                    wo = wo2[e0[4]]
                    for ti, (r0, n, c0) in enumerate(rows):
                        xo = xo2[(cb * len(rows) + ti) % 4]
                        S.dma(xo[0:n, :], xres[r0:r0 + n, cb * 256:(cb + 1) * 256])
                        pb = ps()
                        for kt in range(KT):
                            S.mm(pb[0:n, 0:256], mixT[:, kt, c0:c0 + n], wo[:, kt, :], start=(kt == 0), stop=(kt == KT - 1))
                        S.tt(xo[0:n, :], xo[0:n, :], pb[0:n, 0:256], ALU.add)
                        S.dma(xres[r0:r0 + n, cb * 256:(cb + 1) * 256], xo[0:n, :])
                ga.reset(mO)
                ga.reset(gm0)

        CP(13)
        fnb = ga.alloc([D], F32)
        S.dma(fnb[:, :], fnorm_d.partition_broadcast(128))
        xt2 = [ga.alloc([D], F32) for _ in range(2)]
        junk = ga.alloc([D], BF16)
        st = ga.alloc([8], F32)
        rows = []
        r = 0
        while r < TT:
            n = min(128, (TP if r < TP else TT) - r)
            rows.append((r, n))
            r += n
        for ti, (r0, n) in enumerate(rows):
            xt = xt2[ti % 2]
            S.dma(xt[0:n, :], xres[r0:r0 + n, :])
            S.act(junk[0:n, :], xt[0:n, :], AF.Square, accum_out=st[0:n, 0:1])
            S.act(st[0:n, 1:2], st[0:n, 0:1], AF.Sqrt, scale=1.0 / D, bias=EPS)
            S.recip(st[0:n, 2:3], st[0:n, 1:2])
            S.act(xt[0:n, :], xt[0:n, :], AF.Identity, scale=st[0:n, 2:3])
            S.tt(xt[0:n, :], xt[0:n, :], fnb[0:n, :], ALU.mult)
            if r0 >= TP:
                S.dma(ys_d[r0 - TP:r0 - TP + n, :], xt[0:n, :], is_out=True)
            else:
                a = max(r0, 16)
                if a < r0 + n:
                    S.dma(yp_d[a - 16:r0 + n - 16, :], xt[a - r0:n, :], is_out=True)

    except _Stop:
        pass
    S.finish()
    return nc, hc


_CACHE = {}


def kernel(**inputs):
    cfg = Cfg(nch=32, depth=4, nsb=4, nseq=16)
    if "nc" not in _CACHE:
        _CACHE["nc"] = build(cfg)
    nc, hc = _CACHE["nc"]
    f = np.float32
    g = lambda k: np.ascontiguousarray(np.asarray(inputs[k], dtype=f))
    shared = {k: g(k) for k in ("meta_tokens", "norm_w", "w_in", "conv_a", "a_log", "dt_bias", "norm_a", "conv_b",
                                "conv_b_bias", "lru_wa", "lru_ba", "lru_wx", "lru_bx", "lru_lambda", "gla_w2",
                                "gla_b2", "norm_c", "w_out", "final_norm")}
    for k, v in hc.items():
        shared["c_" + k] = v
    xp, xs = g("x_prompt"), g("x_sample")
    sd, sdc, sl, slc, sg_, sr_ = (g("state_delta"), g("state_delta_conv"), g("state_lru"), g("state_lru_conv"),
                                  g("state_gla"), g("state_ret"))
    in_maps = []
    for i in range(8):
        b = i % 4
        sl_ = slice(16 * i, 16 * i + 16)
        m = dict(shared)
        m["xp"] = xp[b]
        m["xs"] = np.ascontiguousarray(xs[sl_].reshape(64, D))
        m["sdelta"] = np.ascontiguousarray(sd[:, sl_])
        m["sdconv"] = np.ascontiguousarray(sdc[:, sl_].reshape(4, 48, 1536))
        m["slru"] = np.ascontiguousarray(sl[:, sl_])
        m["slconv"] = np.ascontiguousarray(slc[:, sl_].reshape(4, 48, 512))
        m["sgla"] = np.ascontiguousarray(sg_[:, sl_])
        m["sret"] = np.ascontiguousarray(sr_[:, sl_])
        in_maps.append(m)
    res = run_bass_kernel_spmd(nc, in_maps, core_ids=list(range(8))).results
    cat = lambda k, ax: np.concatenate([res[i][k] for i in range(8)], axis=ax)
    stk = lambda k: np.stack([res[i][k] for i in range(4)], axis=1)
    y_p = np.stack([res[i]["y_p"] for i in range(4)], axis=0)
    y_s = cat("y_s", 0).reshape(128, 4, D)
    outs = (y_p, y_s, stk("p_delta"), stk("p_dconv"), stk("p_lru").reshape(4, 4, 512), stk("p_lconv"),
            stk("p_gla"), stk("p_ret"),
            cat("s_delta", 1), cat("s_dconv", 1).reshape(4, 128, 3, 1536), cat("s_lru", 1),
            cat("s_lconv", 1).reshape(4, 128, 3, 512), cat("s_gla", 1), cat("s_ret", 1))
    return tuple(np.ascontiguousarray(o.astype(f)) for o in outs)
```
